# Optimizing a Trainium2 kernel written in Bass

```python
import math
import jax, jax.numpy as jnp
from jax import lax
import numpy as np

D_MODEL = 1024
BATCH = 1
SEQ = 16384
DEPTH = 4

GRID_W = 64
CTX_LEN = 256
EPS = 1e-6
N_MOD = 6
D_MIX = D_MODEL
D_GROUP = D_MIX // 4
HY_WIDTH = D_GROUP
HY_ORDER = 2
HY_SHORT = 3
HY_FREQS = 16
HY_EMB = 1 + 2 * HY_FREQS
HY_FFN = 64
HY_SIN_FREQ = 1.0
HY_DECAY_MIN = 3.07
HY_DECAY_MAX = 15.35
S5_WIDTH = D_GROUP
S5_H = 16
S5_G = S5_WIDTH // S5_H
S5_P = 64
S5_DT_MIN = 0.001
S5_DT_MAX = 0.1
POOL_WIDTH = D_GROUP
POOL_WINDOWS = (2, 4, 8, 16)
POOL_GC = POOL_WIDTH // len(POOL_WINDOWS)
ATT_HEAD_DIM = 64
ATT_Q_HEADS = D_GROUP // ATT_HEAD_DIM
ATT_KV_HEADS = 2
ATT_Q_PER_KV = ATT_Q_HEADS // ATT_KV_HEADS
ATT_SCALE = 1.0 / math.sqrt(ATT_HEAD_DIM)
ROPE_AXIS_DIM = ATT_HEAD_DIM // 2
ROPE_THETA = 10000.0
Q_BLOCK = 128
D_FF = 4 * D_MODEL
IN_COLS = 3 * HY_WIDTH + S5_WIDTH + POOL_WIDTH + ATT_Q_HEADS * ATT_HEAD_DIM + 2 * ATT_KV_HEADS * ATT_HEAD_DIM
SPLIT_IDX = (3 * HY_WIDTH,
             3 * HY_WIDTH + S5_WIDTH,
             3 * HY_WIDTH + S5_WIDTH + POOL_WIDTH,
             3 * HY_WIDTH + S5_WIDTH + POOL_WIDTH + ATT_Q_HEADS * ATT_HEAD_DIM,
             3 * HY_WIDTH + S5_WIDTH + POOL_WIDTH + (ATT_Q_HEADS + ATT_KV_HEADS) * ATT_HEAD_DIM)

kernel_name = "hybrid_parallel_heads_diffusion_block"


def rmsnorm(x, g):
    xf = x.astype(jnp.float32)
    y = xf * lax.rsqrt(jnp.mean(xf * xf, axis=-1, keepdims=True) + EPS)
    return (y * g.astype(jnp.float32)).astype(x.dtype)


def modulate(h, shift, scale):
    return h * (1 + scale) + shift


def short_conv(u, w, b):
    L = u.shape[1]
    pad = HY_SHORT // 2
    up = jnp.pad(u, ((0, 0), (pad, pad), (0, 0)))
    y = b
    for k in range(HY_SHORT):
        y = y + w[k] * up[:, k:k + L]
    return y


def hyena_kernel_spectra(L, w1, b1, w2, b2, w3, decay):
    f32 = jnp.float32
    t = jnp.arange(L, dtype=f32) / L
    freqs = jnp.arange(1, HY_FREQS + 1, dtype=f32)
    ang = 2.0 * math.pi * t[:, None] * freqs[None, :]
    feats = jnp.concatenate([t[:, None], jnp.cos(ang), jnp.sin(ang)], axis=-1)
    h = jnp.sin(HY_SIN_FREQ * (feats @ w1.astype(f32) + b1.astype(f32)))
    h = jnp.sin(HY_SIN_FREQ * (h @ w2.astype(f32) + b2.astype(f32)))
    h = (h @ w3.astype(f32)).reshape(L, HY_ORDER, 2, HY_WIDTH)
    h = h * jnp.exp(-t[:, None, None, None] * jnp.abs(decay.astype(f32))[None])
    h_fwd, h_bwd = h[:, :, 0], h[:, :, 1]
    kern = jnp.concatenate([h_fwd, jnp.zeros((1, HY_ORDER, HY_WIDTH), f32), h_bwd[:L - 1][::-1]], axis=0)
    return jnp.fft.rfft(kern, axis=0)


def fft_long_conv(z, kern_f):
    L = z.shape[1]
    Z = jnp.fft.rfft(z.astype(jnp.float32), n=2 * L, axis=1)
    return jnp.fft.irfft(Z * kern_f[None], n=2 * L, axis=1)[:, :L]


def hyena_mix(u3, conv_w, conv_b, w1, b1, w2, b2, w3, decay, fbias):
    L = u3.shape[1]
    kern_f = hyena_kernel_spectra(L, w1, b1, w2, b2, w3, decay)
    v, x1, x2 = jnp.split(short_conv(u3, conv_w, conv_b), 3, axis=-1)
    z = v.astype(jnp.float32)
    for o, gate in enumerate((x1, x2)):
        z = gate.astype(jnp.float32) * (fft_long_conv(z, kern_f[:, o]) + fbias[o].astype(jnp.float32) * z)
    return z.astype(u3.dtype)


def s5_zoh(a_re, a_im, log_dt, b_re, b_im):
    f32 = jnp.float32
    a_re, a_im = a_re.astype(f32), a_im.astype(f32)
    dt = jnp.exp(log_dt.astype(f32))[:, None]
    mag = jnp.exp(a_re * dt)
    lam_re, lam_im = mag * jnp.cos(a_im * dt), mag * jnp.sin(a_im * dt)
    den = a_re * a_re + a_im * a_im
    nr, ni = lam_re - 1.0, lam_im
    cr = (nr * a_re + ni * a_im) / den
    ci = (ni * a_re - nr * a_im) / den
    b_re, b_im = b_re.astype(f32), b_im.astype(f32)
    bb_re = cr[..., None] * b_re - ci[..., None] * b_im
    bb_im = cr[..., None] * b_im + ci[..., None] * b_re
    return lam_re, lam_im, bb_re, bb_im


def _complex_affine_combine(e1, e2):
    a1r, a1i, b1r, b1i = e1
    a2r, a2i, b2r, b2i = e2
    return (a2r * a1r - a2i * a1i,
            a2r * a1i + a2i * a1r,
            a2r * b1r - a2i * b1i + b2r,
            a2r * b1i + a2i * b1r + b2i)


def diag_scan(lam_re, lam_im, bu_re, bu_im, h0_re, h0_im, reverse):
    if h0_re is not None:
        edge = -1 if reverse else 0
        bu_re = bu_re.at[:, edge].add(lam_re * h0_re - lam_im * h0_im)
        bu_im = bu_im.at[:, edge].add(lam_re * h0_im + lam_im * h0_re)
    a_re = jnp.broadcast_to(lam_re, bu_re.shape)
    a_im = jnp.broadcast_to(lam_im, bu_im.shape)
    _, _, h_re, h_im = lax.associative_scan(_complex_affine_combine, (a_re, a_im, bu_re, bu_im), reverse=reverse, axis=1)
    return h_re, h_im


def s5_readout(h_re, h_im, c_re, c_im):
    B, L = h_re.shape[:2]
    y = jnp.einsum('blgp,ghp->blgh', h_re, c_re.astype(jnp.float32)) - jnp.einsum('blgp,ghp->blgh', h_im, c_im.astype(jnp.float32))
    return y.reshape(B, L, S5_WIDTH)


def s5_glu(y, w, b):
    g = jax.nn.gelu(y)
    return g * jax.nn.sigmoid(g @ w.astype(jnp.float32) + b.astype(jnp.float32))


def s5_mix(u_l, u_c, a_re, a_im, log_dt, b_re, b_im, c_re, c_im, d, glu_w, glu_b, need_ctx_out):
    B, L, _ = u_l.shape
    Lc = u_c.shape[1]
    ul = u_l.astype(jnp.float32)
    uc = u_c.astype(jnp.float32)
    ulg = ul.reshape(B, L, S5_G, S5_H)
    ucg = uc.reshape(B, Lc, S5_G, S5_H)
    dd = d.astype(jnp.float32)
    y_l = dd * ul
    y_c = dd * uc
    for direction in range(2):
        rev = direction == 1
        lam_re, lam_im, bb_re, bb_im = s5_zoh(a_re[direction], a_im[direction], log_dt[direction], b_re[direction], b_im[direction])
        hc_re, hc_im = diag_scan(lam_re, lam_im,
                                 jnp.einsum('blgh,gph->blgp', ucg, bb_re), jnp.einsum('blgh,gph->blgp', ucg, bb_im),
                                 None, None, rev)
        edge = 0 if rev else -1
        hl_re, hl_im = diag_scan(lam_re, lam_im,
                                 jnp.einsum('blgh,gph->blgp', ulg, bb_re), jnp.einsum('blgh,gph->blgp', ulg, bb_im),
                                 hc_re[:, edge], hc_im[:, edge], rev)
        y_l = y_l + s5_readout(hl_re, hl_im, c_re[direction], c_im[direction])
        if need_ctx_out:
            y_c = y_c + s5_readout(hc_re, hc_im, c_re[direction], c_im[direction])
    out_l = s5_glu(y_l, glu_w, glu_b).astype(u_l.dtype)
    out_c = s5_glu(y_c, glu_w, glu_b).astype(u_c.dtype) if need_ctx_out else None
    return out_l, out_c


def pool_mix(u, w, scale):
    B, L, W = u.shape
    uf = u.astype(jnp.float32)
    cs = jnp.concatenate([jnp.zeros((B, 1, W), jnp.float32), jnp.cumsum(uf, axis=1)], axis=1)
    t = jnp.arange(L)
    outs = []
    for g, win in enumerate(POOL_WINDOWS):
        csg = cs[..., g * POOL_GC:(g + 1) * POOL_GC]
        lo = jnp.clip(t - win // 2, 0, L)
        hi = jnp.clip(t + win // 2, 0, L)
        mean = (jnp.take(csg, hi, axis=1) - jnp.take(csg, lo, axis=1)) / (hi - lo).astype(jnp.float32)[None, :, None]
        outs.append(mean - uf[..., g * POOL_GC:(g + 1) * POOL_GC])
    pooled = jnp.stack(outs, axis=2)
    mixed = jnp.einsum('blgc,gcd->blgd', pooled, w.astype(jnp.float32)).reshape(B, L, W)
    return (mixed * scale.astype(jnp.float32)).astype(u.dtype)


def axial_rope_tables(rows, cols):
    inv = ROPE_THETA ** (-jnp.arange(0, ROPE_AXIS_DIM, 2, dtype=jnp.float32) / ROPE_AXIS_DIM)
    ang = jnp.concatenate([rows.astype(jnp.float32)[:, None] * inv[None, :],
                           cols.astype(jnp.float32)[:, None] * inv[None, :]], axis=-1)
    return jnp.cos(ang), jnp.sin(ang)


def apply_rope(x, cos, sin):
    xp = x.astype(jnp.float32).reshape(*x.shape[:-1], ATT_HEAD_DIM // 2, 2)
    x0, x1 = xp[..., 0], xp[..., 1]
    c = cos[None, :, None, :]
    s = sin[None, :, None, :]
    return jnp.stack([x0 * c - x1 * s, x0 * s + x1 * c], axis=-1).reshape(x.shape).astype(x.dtype)


def gqa_softmax(q, k, v):
    s = jnp.einsum('bqkgd,bskd->bkgqs', q, k).astype(jnp.float32) * ATT_SCALE
    p = jax.nn.softmax(s, axis=-1).astype(v.dtype)
    return jnp.einsum('bkgqs,bskd->bqkgd', p, v)


def attention_mix(q_l, k_l, v_l, q_c, k_c, v_c, q_gain, k_gain, rope_cos, rope_sin, need_ctx_out):
    B, L, _ = q_l.shape
    Lc = k_c.shape[1]
    heads = lambda t, h: t.reshape(t.shape[0], t.shape[1], h, ATT_HEAD_DIM)
    ql = apply_rope(rmsnorm(heads(q_l, ATT_Q_HEADS), q_gain), rope_cos, rope_sin)
    kl = apply_rope(rmsnorm(heads(k_l, ATT_KV_HEADS), k_gain), rope_cos, rope_sin)
    kc = rmsnorm(heads(k_c, ATT_KV_HEADS), k_gain)
    vl, vc = heads(v_l, ATT_KV_HEADS), heads(v_c, ATT_KV_HEADS)
    k_all = jnp.concatenate([kl, kc], axis=1)
    v_all = jnp.concatenate([vl, vc], axis=1)
    qb = ql.reshape(B, L // Q_BLOCK, Q_BLOCK, ATT_KV_HEADS, ATT_Q_PER_KV, ATT_HEAD_DIM).transpose(1, 0, 2, 3, 4, 5)
    ob = lax.map(lambda q: gqa_softmax(q, k_all, v_all), qb)
    y_l = ob.transpose(1, 0, 2, 3, 4, 5).reshape(B, L, ATT_Q_HEADS * ATT_HEAD_DIM)
    y_c = None
    if need_ctx_out:
        qc = rmsnorm(heads(q_c, ATT_Q_HEADS), q_gain).reshape(B, Lc, ATT_KV_HEADS, ATT_Q_PER_KV, ATT_HEAD_DIM)
        y_c = gqa_softmax(qc, kc, vc).reshape(B, Lc, ATT_Q_HEADS * ATT_HEAD_DIM)
    return y_l, y_c


def sq_relu_mlp(h, w1, w2):
    return jnp.square(jax.nn.relu(h @ w1)) @ w2


def token_mix_one_stream(a, s5_y, p, att_y, lp):
    y_hy = hyena_mix(a, lp["hy_conv_w"], lp["hy_conv_b"], lp["hy_ffn_w1"], lp["hy_ffn_b1"],
                     lp["hy_ffn_w2"], lp["hy_ffn_b2"], lp["hy_ffn_w3"], lp["hy_decay"], lp["hy_bias"])
    y_pool = pool_mix(p, lp["pool_w"], lp["pool_scale"])
    merged = jnp.concatenate([y_hy.astype(a.dtype), s5_y.astype(a.dtype), y_pool.astype(a.dtype), att_y.astype(a.dtype)], axis=-1)
    return merged @ lp["w_out"]


def layer(xl, xc, c, c_ctx, rope_cos, rope_sin, lp, need_ctx_out):
    mod_l = jnp.split((jax.nn.silu(c) @ lp["mod_w"] + lp["mod_b"])[:, None, :], N_MOD, axis=-1)
    mod_c = jnp.split(jax.nn.silu(c_ctx) @ lp["mod_w"] + lp["mod_b"], N_MOD, axis=-1)
    hl = modulate(rmsnorm(xl, lp["norm_pre_mix"]), mod_l[0], mod_l[1])
    hc = modulate(rmsnorm(xc, lp["norm_pre_mix"]), mod_c[0], mod_c[1])
    a_l, s_l, p_l, q_l, k_l, v_l = jnp.split(hl @ lp["w_in"], SPLIT_IDX, axis=-1)
    a_c, s_c, p_c, q_c, k_c, v_c = jnp.split(hc @ lp["w_in"], SPLIT_IDX, axis=-1)
    y_s5_l, y_s5_c = s5_mix(s_l, s_c, lp["s5_a_re"], lp["s5_a_im"], lp["s5_log_dt"], lp["s5_b_re"], lp["s5_b_im"],
                            lp["s5_c_re"], lp["s5_c_im"], lp["s5_d"], lp["s5_glu_w"], lp["s5_glu_b"], need_ctx_out)
    y_att_l, y_att_c = attention_mix(q_l, k_l, v_l, q_c, k_c, v_c, lp["att_q_norm"], lp["att_k_norm"],
                                     rope_cos, rope_sin, need_ctx_out)
    ol = token_mix_one_stream(a_l, y_s5_l, p_l, y_att_l, lp)
    xl = xl + mod_l[2] * rmsnorm(ol, lp["norm_post_mix"])
    fl = sq_relu_mlp(modulate(rmsnorm(xl, lp["norm_pre_mlp"]), mod_l[3], mod_l[4]), lp["mlp_w1"], lp["mlp_w2"])
    xl = xl + mod_l[5] * rmsnorm(fl, lp["norm_post_mlp"])
    if not need_ctx_out:
        return xl, xc
    oc = token_mix_one_stream(a_c, y_s5_c, p_c, y_att_c, lp)
    xc = xc + mod_c[2] * rmsnorm(oc, lp["norm_post_mix"])
    fc = sq_relu_mlp(modulate(rmsnorm(xc, lp["norm_pre_mlp"]), mod_c[3], mod_c[4]), lp["mlp_w1"], lp["mlp_w2"])
    xc = xc + mod_c[5] * rmsnorm(fc, lp["norm_post_mlp"])
    return xl, xc


def setup_inputs(seed: int = 0) -> dict:
    key = jax.random.key(seed)
    ks = iter(jax.random.split(key, 64))
    f32 = jnp.float32

    def nrm(shape, scale):
        return scale * jax.random.normal(next(ks), shape, f32)

    def gain(shape):
        return 1.0 + nrm(shape, 0.05)

    x = nrm((BATCH, SEQ, D_MODEL), 1.0)
    c = nrm((BATCH, D_MODEL), 1.0)
    ctx = nrm((BATCH, CTX_LEN, D_MODEL), 1.0)
    c_ctx = nrm((D_MODEL,), 1.0)
    mod_w = nrm((DEPTH, D_MODEL, N_MOD * D_MODEL), 0.5 * D_MODEL ** -0.5)
    mod_b = nrm((DEPTH, N_MOD * D_MODEL), 0.02)
    norm_pre_mix = gain((DEPTH, D_MODEL))
    norm_post_mix = gain((DEPTH, D_MODEL))
    norm_pre_mlp = gain((DEPTH, D_MODEL))
    norm_post_mlp = gain((DEPTH, D_MODEL))
    w_in = nrm((DEPTH, D_MODEL, IN_COLS), D_MODEL ** -0.5)
    w_out = nrm((DEPTH, D_MIX, D_MODEL), D_MIX ** -0.5)
    hy_conv_w = nrm((DEPTH, HY_SHORT, 3 * HY_WIDTH), HY_SHORT ** -0.5)
    hy_conv_b = nrm((DEPTH, 3 * HY_WIDTH), 0.02)
    hy_ffn_w1 = nrm((DEPTH, HY_EMB, HY_FFN), HY_EMB ** -0.5)
    hy_ffn_b1 = nrm((DEPTH, HY_FFN), 0.1)
    hy_ffn_w2 = nrm((DEPTH, HY_FFN, HY_FFN), HY_FFN ** -0.5)
    hy_ffn_b2 = nrm((DEPTH, HY_FFN), 0.1)
    hy_ffn_w3 = nrm((DEPTH, HY_FFN, HY_ORDER * 2 * HY_WIDTH), 0.05 * HY_FFN ** -0.5)
    hy_decay = jnp.linspace(HY_DECAY_MIN, HY_DECAY_MAX, HY_WIDTH, dtype=f32)[None, None, None, :] + nrm((DEPTH, HY_ORDER, 2, HY_WIDTH), 0.1)
    hy_bias = nrm((DEPTH, HY_ORDER, HY_WIDTH), 0.1)
    s5_a_re = -0.5 + nrm((DEPTH, 2, S5_G, S5_P), 0.01)
    s5_a_im = math.pi * jnp.arange(S5_P, dtype=f32)[None, None, None, :] + nrm((DEPTH, 2, S5_G, S5_P), 0.01)
    s5_log_dt = math.log(S5_DT_MIN) + jax.random.uniform(next(ks), (DEPTH, 2, S5_G), f32) * (math.log(S5_DT_MAX) - math.log(S5_DT_MIN))
    s5_b_re = nrm((DEPTH, 2, S5_G, S5_P, S5_H), (2 * S5_H) ** -0.5)
    s5_b_im = nrm((DEPTH, 2, S5_G, S5_P, S5_H), (2 * S5_H) ** -0.5)
    s5_c_re = nrm((DEPTH, 2, S5_G, S5_H, S5_P), (2 * S5_P) ** -0.5)
    s5_c_im = nrm((DEPTH, 2, S5_G, S5_H, S5_P), (2 * S5_P) ** -0.5)
    s5_d = nrm((DEPTH, S5_WIDTH), 0.5)
    s5_glu_w = nrm((DEPTH, S5_WIDTH, S5_WIDTH), S5_WIDTH ** -0.5)
    s5_glu_b = nrm((DEPTH, S5_WIDTH), 0.02)
    pool_w = nrm((DEPTH, len(POOL_WINDOWS), POOL_GC, POOL_GC), POOL_GC ** -0.5)
    pool_scale = 1.0 + nrm((DEPTH, POOL_WIDTH), 0.1)
    att_q_norm = gain((DEPTH, ATT_HEAD_DIM))
    att_k_norm = gain((DEPTH, ATT_HEAD_DIM))
    mlp_w1 = nrm((DEPTH, D_MODEL, D_FF), D_MODEL ** -0.5)
    mlp_w2 = nrm((DEPTH, D_FF, D_MODEL), D_FF ** -0.5)
    return {"x": x, "c": c, "ctx": ctx, "c_ctx": c_ctx, "mod_w": mod_w, "mod_b": mod_b,
            "norm_pre_mix": norm_pre_mix, "norm_post_mix": norm_post_mix,
            "norm_pre_mlp": norm_pre_mlp, "norm_post_mlp": norm_post_mlp,
            "w_in": w_in, "w_out": w_out, "hy_conv_w": hy_conv_w, "hy_conv_b": hy_conv_b,
            "hy_ffn_w1": hy_ffn_w1, "hy_ffn_b1": hy_ffn_b1, "hy_ffn_w2": hy_ffn_w2, "hy_ffn_b2": hy_ffn_b2,
            "hy_ffn_w3": hy_ffn_w3, "hy_decay": hy_decay, "hy_bias": hy_bias,
            "s5_a_re": s5_a_re, "s5_a_im": s5_a_im, "s5_log_dt": s5_log_dt,
            "s5_b_re": s5_b_re, "s5_b_im": s5_b_im, "s5_c_re": s5_c_re, "s5_c_im": s5_c_im,
            "s5_d": s5_d, "s5_glu_w": s5_glu_w, "s5_glu_b": s5_glu_b,
            "pool_w": pool_w, "pool_scale": pool_scale, "att_q_norm": att_q_norm, "att_k_norm": att_k_norm,
            "mlp_w1": mlp_w1, "mlp_w2": mlp_w2}


def reference(x, c, ctx, c_ctx, mod_w, mod_b, norm_pre_mix, norm_post_mix, norm_pre_mlp, norm_post_mlp,
              w_in, w_out, hy_conv_w, hy_conv_b, hy_ffn_w1, hy_ffn_b1, hy_ffn_w2, hy_ffn_b2, hy_ffn_w3,
              hy_decay, hy_bias, s5_a_re, s5_a_im, s5_log_dt, s5_b_re, s5_b_im, s5_c_re, s5_c_im, s5_d,
              s5_glu_w, s5_glu_b, pool_w, pool_scale, att_q_norm, att_k_norm, mlp_w1, mlp_w2):
    L = x.shape[1]
    ROWS = L // GRID_W
    rows = jnp.repeat(jnp.arange(ROWS, dtype=jnp.int32), GRID_W)
    cols = jnp.tile(jnp.arange(GRID_W, dtype=jnp.int32), ROWS)
    rope_cos, rope_sin = axial_rope_tables(rows, cols)
    xl, xc = x, ctx
    for i in range(DEPTH):
        lp = {"mod_w": mod_w[i], "mod_b": mod_b[i],
              "norm_pre_mix": norm_pre_mix[i], "norm_post_mix": norm_post_mix[i],
              "norm_pre_mlp": norm_pre_mlp[i], "norm_post_mlp": norm_post_mlp[i],
              "w_in": w_in[i], "w_out": w_out[i], "hy_conv_w": hy_conv_w[i], "hy_conv_b": hy_conv_b[i],
              "hy_ffn_w1": hy_ffn_w1[i], "hy_ffn_b1": hy_ffn_b1[i], "hy_ffn_w2": hy_ffn_w2[i],
              "hy_ffn_b2": hy_ffn_b2[i], "hy_ffn_w3": hy_ffn_w3[i], "hy_decay": hy_decay[i], "hy_bias": hy_bias[i],
              "s5_a_re": s5_a_re[i], "s5_a_im": s5_a_im[i], "s5_log_dt": s5_log_dt[i],
              "s5_b_re": s5_b_re[i], "s5_b_im": s5_b_im[i], "s5_c_re": s5_c_re[i], "s5_c_im": s5_c_im[i],
              "s5_d": s5_d[i], "s5_glu_w": s5_glu_w[i], "s5_glu_b": s5_glu_b[i],
              "pool_w": pool_w[i], "pool_scale": pool_scale[i],
              "att_q_norm": att_q_norm[i], "att_k_norm": att_k_norm[i],
              "mlp_w1": mlp_w1[i], "mlp_w2": mlp_w2[i]}
        xl, xc = layer(xl, xc, c, c_ctx, rope_cos, rope_sin, lp, i < DEPTH - 1)
    return xl
```

```python
import math
import numpy as np
from contextlib import ExitStack
import concourse.bass as bass
import concourse.mybir as mybir
from concourse.bass_utils import run_bass_kernel_spmd

F32 = mybir.dt.float32
BF16 = mybir.dt.bfloat16
AF = mybir.ActivationFunctionType
ALU = mybir.AluOpType
AX = mybir.AxisListType

NCORES = 8
D = 1024
L = 16384
LC = 256
DEPTH = 4
TPC = L // NCORES
NT_MAIN = TPC // 128
NT = NT_MAIN + LC // 128
INC = 1792
EPS = 1e-6
NDSEM = 8


class Dep:
    __slots__ = ("w", "r")

    def __init__(self):
        self.w = None
        self.r = {}


class Buf(Dep):
    __slots__ = ("t",)

    def __init__(self, t):
        Dep.__init__(self)
        self.t = t

    def __getitem__(self, k):
        return self.t[k]


class Prog:
    def __init__(self):
        self.nc = bass.Bass("TRN2", target_bir_lowering=False)
        self.es = ExitStack()
        nc = self.nc
        self.E = {"pe": nc.tensor, "dve": nc.vector, "act": nc.scalar, "pool": nc.gpsimd, "sp": nc.sync}
        self.sem = {k: self.es.enter_context(nc.semaphore("s_" + k)) for k in self.E}
        self.cnt = {k: 0 for k in self.E}
        self.waited = {}
        self.dq = {}
        for q in ("sp", "pool", "act"):
            self.dq[q] = [[self.es.enter_context(nc.semaphore("d_%s%d" % (q, i))), 0] for i in range(NDSEM)]
        self.dqi = {q: 0 for q in self.dq}
        self.nalloc = 0

    def inp(self, name, shape, dt=F32):
        return self.nc.dram_tensor(name, list(shape), dt, kind="ExternalInput").ap()

    def outp(self, name, shape, dt=F32):
        return self.nc.dram_tensor(name, list(shape), dt, kind="ExternalOutput").ap()

    def sb(self, shape, dt=F32, name=None):
        self.nalloc += 1
        return self.es.enter_context(self.nc.sbuf_tensor(name or ("t%d" % self.nalloc), list(shape), dt))

    def bsb(self, shape, dt=F32):
        return Buf(self.sb(shape, dt))

    def bps(self, shape, dt=F32):
        return Buf(self.ps(shape, dt))

    def ps(self, shape, dt=F32, name=None):
        self.nalloc += 1
        return self.es.enter_context(self.nc.psum_tensor(name or ("p%d" % self.nalloc), list(shape), dt))

    def _semh(self, key):
        if key[0] == "e":
            return self.sem[key[1]]
        return self.dq[key[1]][key[2]][0]

    def wait(self, eng, tok):
        if tok is None:
            return
        key, val = tok
        if eng == "pe" and key == ("e", "pe"):
            return
        k = (eng, key)
        if self.waited.get(k, 0) >= val:
            return
        self.E[eng].wait_ge(self._semh(key), val)
        self.waited[k] = val

    def _pre(self, eng, reads, writes):
        for d in reads:
            self.wait(eng, d.w)
        for d in writes:
            self.wait(eng, d.w)
            for t in d.r.values():
                self.wait(eng, t)

    def _post(self, tok, reads, writes):
        for d in reads:
            d.r[tok[0]] = tok
        for d in writes:
            d.w = tok
            d.r = {}

    def op(self, eng, reads, writes, fn):
        self._pre(eng, reads, writes)
        ins = fn(self.E[eng])
        self.cnt[eng] += 1
        ins.then_inc(self.sem[eng], 1)
        self._post((("e", eng), self.cnt[eng]), reads, writes)

    def dma(self, q, reads, writes, out, in_, **kw):
        self._pre(q, reads, writes)
        idx = self.dqi[q] % NDSEM
        self.dqi[q] += 1
        slot = self.dq[q][idx]
        key = ("d", q, idx)
        if slot[1] > 0:
            self.wait(q, (key, slot[1]))
        slot[1] += 16
        self.E[q].dma_start(out=out, in_=in_, **kw).then_inc(slot[0], 16)
        self._post((key, slot[1]), reads, writes)

    def finish(self):
        for q in self.dq:
            for i, slot in enumerate(self.dq[q]):
                if slot[1] > 0:
                    self.wait("sp", (("d", q, i), slot[1]))
        for e in self.E:
            if e != "sp" and self.cnt[e] > 0:
                self.wait("sp", (("e", e), self.cnt[e]))
        self.es.close()
        return self.nc

    def mm(self, out, lhsT, rhs, start, stop, reads, writes):
        self.op("pe", reads, writes, lambda e: e.matmul(out, lhsT, rhs, start=start, stop=stop))

    def tr(self, out, in_, ident, reads, writes):
        self.op("pe", reads, writes, lambda e: e.transpose(out, in_, ident))

    def act(self, out, in_, func, reads, writes, bias=None, scale=None, accum_out=None, eng="act"):
        kw = {}
        if bias is not None:
            kw["bias"] = bias
        if scale is not None:
            kw["scale"] = scale
        if accum_out is not None:
            kw["accum_out"] = accum_out
        self.op("act", reads, writes, lambda e: e.activation(out, in_, func, **kw))

    def tt(self, eng, out, in0, in1, op, reads, writes):
        self.op(eng, reads, writes, lambda e: e.tensor_tensor(out, in0, in1, op))

    def ts(self, eng, out, in0, s1, s2, op0, op1, reads, writes):
        if op1 is None:
            self.op(eng, reads, writes, lambda e: e.tensor_scalar(out, in0, s1, None, op0))
        else:
            self.op(eng, reads, writes, lambda e: e.tensor_scalar(out, in0, s1, s2, op0, op1))

    def stt(self, out, in0, scalar, in1, op0, op1, reads, writes):
        self.op("dve", reads, writes, lambda e: e.scalar_tensor_tensor(out, in0, scalar, in1, op0, op1))

    def cp(self, eng, out, in_, reads, writes):
        if eng == "act":
            self.op("act", reads, writes, lambda e: e.copy(out, in_))
        else:
            self.op(eng, reads, writes, lambda e: e.tensor_copy(out, in_))

    def recip(self, out, in_, reads, writes):
        self.op("dve", reads, writes, lambda e: e.reciprocal(out, in_))

    def memset(self, eng, ap, val, writes):
        self.op(eng, [], writes, lambda e: e.memset(ap, val))


def run(nc, in_maps):
    res = run_bass_kernel_spmd(nc, in_maps, core_ids=list(range(NCORES)))
    return res.results


_cache = {}


def build_pm():
    p = Prog()
    cc = p.inp("cc", [128, 8, 2])
    mw = p.inp("mw", [1024, 3072])
    mb = p.inp("mb", [1, 3072])
    out = p.outp("out", [2, 3072])
    cs = p.sb([128, 8, 2]); d_cs = Dep()
    sc = p.sb([128, 8, 2]); d_sc = Dep()
    wsb = p.sb([128, 8, 3072]); d_w = [Dep() for _ in range(8)]
    bsb = p.sb([2, 3072]); d_b = Dep()
    res = p.sb([2, 3072]); d_res = Dep()
    pss = [p.ps([128, 512]) for _ in range(2)]; d_ps = [Dep(), Dep()]
    p.dma("sp", [], [d_cs], cs[:], cc)
    p.dma("sp", [], [d_b], bsb[:], mb.partition_broadcast(2))
    for kc in range(8):
        p.dma("sp" if kc % 2 == 0 else "pool", [], [d_w[kc]], wsb[:, kc, :], mw[kc * 128:(kc + 1) * 128, :])
    p.act(sc[:], cs[:], AF.Silu, [d_cs], [d_sc])
    for n in range(6):
        b = n % 2
        for kc in range(8):
            p.mm(pss[b][0:2, :], sc[:, kc, :], wsb[:, kc, n * 512:(n + 1) * 512], kc == 0, kc == 7,
                 [d_sc, d_w[kc]], [d_ps[b]])
        p.tt("dve", res[:, n * 512:(n + 1) * 512], pss[b][0:2, :], bsb[:, n * 512:(n + 1) * 512], ALU.add,
             [d_ps[b], d_b], [d_res])
    p.dma("sp", [d_res], [], out, res[:])
    return p.finish()


def run_pm(inputs):
    if "pm" not in _cache:
        _cache["pm"] = build_pm()
    c2 = np.stack([inputs["c"][0], inputs["c_ctx"]], axis=-1)
    cc = np.ascontiguousarray(c2.reshape(8, 128, 2).transpose(1, 0, 2))
    maps = []
    for i in range(NCORES):
        l, h = i // 2, i % 2
        maps.append({"cc": cc,
                     "mw": np.ascontiguousarray(inputs["mod_w"][l][:, h * 3072:(h + 1) * 3072]),
                     "mb": np.ascontiguousarray(inputs["mod_b"][l][None, h * 3072:(h + 1) * 3072])})
    r = run(_cache["pm"], maps)
    mod = np.zeros((DEPTH, 2, 6144), np.float32)
    for i in range(NCORES):
        l, h = i // 2, i % 2
        mod[l, :, h * 3072:(h + 1) * 3072] = r[i]["out"]
    return mod


def build_pa(NT=NT, NT_MAIN=NT_MAIN, NB=2):
    p = Prog()
    x = p.inp("x", [NT * 128, D])
    modr = p.inp("modr", [4, D])
    gpre = p.inp("gpre", [1, D])
    win = p.inp("win", [D, INC])
    qkg = p.inp("qkg", [1, 384])
    rc = p.inp("rc", [NT * 128, 192])
    rs = p.inp("rs", [NT * 128, 192])
    ident = p.inp("ident", [128, 128])
    out = p.outp("proj", [NT * 128, INC])

    idb = p.bsb([128, 128], BF16)
    p.dma("pool", [], [idb], idb[:], ident)
    wsb = p.bsb([128, 8, INC], BF16)
    for kc in range(8):
        p.dma("pool", [], [wsb], wsb[:, kc, :], win[kc * 128:(kc + 1) * 128, :])
    gb = p.bsb([128, D])
    p.dma("sp", [], [gb], gb[:], gpre.partition_broadcast(128))
    mods = []
    for i in range(4):
        m = p.bsb([128, D])
        p.dma("sp", [], [m], m[:], modr[i:i + 1, :].partition_broadcast(128))
        mods.append(m)
    G = []
    for s in range(2):
        g = p.bsb([128, D])
        p.stt(g[:], mods[2 * s + 1][:], 1.0, gb[:], ALU.add, ALU.mult, [mods[2 * s + 1], gb], [g])
        G.append(g)
    SH = [mods[0], mods[2]]
    qkgb = p.bsb([128, 384])
    p.dma("sp", [], [qkgb], qkgb[:], qkg.partition_broadcast(128))

    xs = [p.bsb([128, D]) for _ in range(NB)]
    junk = p.bsb([128, D], BF16)
    ss = [p.bsb([128, 1]) for _ in range(NB)]
    rt = [p.bsb([128, 1]) for _ in range(NB)]
    rstd = [p.bsb([128, 1]) for _ in range(NB)]
    t1 = p.bsb([128, D])
    h = [p.bsb([128, D], BF16) for _ in range(NB)]
    psT = p.bps([128, 8, 128], BF16)
    hT = [p.bsb([128, 8, 128], BF16) for _ in range(NB)]
    pso = [p.bps([128, 512]) for _ in range(4)]
    osb = [p.bsb([128, INC]) for _ in range(NB)]
    sqb = p.bsb([128, 384])
    ssq = p.bsb([128, 6]); rt6 = p.bsb([128, 6]); rs6 = p.bsb([128, 6])
    qn = p.bsb([128, 384])
    tA = p.bsb([128, 192]); tB = p.bsb([128, 192])
    rcs = [p.bsb([128, 192]) for _ in range(NB)]
    rss = [p.bsb([128, 192]) for _ in range(NB)]
    chunks = [(0, 512), (512, 512), (1024, 256), (1280, 512)]

    for i in range(NT):
        b = i % NB
        s = 0 if i < NT_MAIN else 1
        rows = slice(i * 128, (i + 1) * 128)
        p.dma("sp", [], [xs[b]], xs[b][:], x[rows, :])
        p.dma("sp", [], [rcs[b]], rcs[b][:], rc[rows, :])
        p.dma("sp", [], [rss[b]], rss[b][:], rs[rows, :])
        p.act(junk[:], xs[b][:], AF.Square, [xs[b]], [junk, ss[b]], accum_out=ss[b][:])
        p.act(rt[b][:], ss[b][:], AF.Sqrt, [ss[b]], [rt[b]], bias=EPS, scale=1.0 / D)
        p.recip(rstd[b][:], rt[b][:], [rt[b]], [rstd[b]])
        p.stt(t1[:], xs[b][:], rstd[b][:, 0:1], G[s][:], ALU.mult, ALU.mult, [xs[b], rstd[b], G[s]], [t1])
        p.tt("dve", h[b][:], t1[:], SH[s][:], ALU.add, [t1, SH[s]], [h[b]])
        for kc in range(8):
            p.tr(psT[:, kc, :], h[b][:, kc * 128:(kc + 1) * 128], idb[:], [h[b], idb], [psT])
        p.cp("act", hT[b][:], psT[:], [psT], [hT[b]])
        for ci, (c0, w) in enumerate(chunks):
            for kc in range(8):
                p.mm(pso[ci][:, 0:w], hT[b][:, kc, :], wsb[:, kc, c0:c0 + w], kc == 0, kc == 7,
                     [hT[b], wsb], [pso[ci]])
        o = osb[b]
        p.cp("act", o[:, 0:512], pso[0][:, :], [pso[0]], [o])
        p.cp("dve", o[:, 512:1024], pso[1][:, :], [pso[1]], [o])
        p.cp("act", o[:, 1024:1280], pso[2][:, 0:256], [pso[2]], [o])
        p.cp("act", o[:, 1664:1792], pso[3][:, 384:512], [pso[3]], [o])
        p.act(sqb[:], pso[3][:, 0:384], AF.Square, [pso[3]], [sqb])
        p.op("dve", [sqb], [ssq], lambda e: e.tensor_reduce(
            ssq[:], sqb[:].rearrange("p (h d) -> p h d", d=64), AX.X, ALU.add))
        p.act(rt6[:], ssq[:], AF.Sqrt, [ssq], [rt6], bias=EPS, scale=1.0 / 64)
        p.recip(rs6[:], rt6[:], [rt6], [rs6])
        for hh in range(6):
            cs = slice(hh * 64, (hh + 1) * 64)
            p.stt(qn[:, cs], pso[3][:, cs], rs6[:, hh:hh + 1], qkgb[:, cs], ALU.mult, ALU.mult,
                  [pso[3], rs6, qkgb], [qn])
        qv = qn[:].rearrange("p (j two) -> p j two", two=2)
        ov = o[:, 1280:1664].rearrange("p (j two) -> p j two", two=2)
        x0, x1 = qv[:, :, 0], qv[:, :, 1]
        p.tt("dve", tA[:], x0, rcs[b][:], ALU.mult, [qn, rcs[b]], [tA])
        p.tt("dve", tB[:], x1, rss[b][:], ALU.mult, [qn, rss[b]], [tB])
        p.tt("dve", ov[:, :, 0], tA[:], tB[:], ALU.subtract, [tA, tB], [o])
        p.tt("dve", tA[:], x0, rss[b][:], ALU.mult, [qn, rss[b]], [tA])
        p.tt("dve", tB[:], x1, rcs[b][:], ALU.mult, [qn, rcs[b]], [tB])
        p.tt("dve", ov[:, :, 1], tA[:], tB[:], ALU.add, [tA, tB], [o])
        p.dma("sp", [o], [], out[rows, :], o[:])
    return p.finish()


def rope_tables():
    inv = (10000.0 ** (-np.arange(0, 32, 2, dtype=np.float32) / 32)).astype(np.float32)
    t = np.arange(L)
    rows = (t // 64).astype(np.float32)
    cols = (t % 64).astype(np.float32)
    ang = np.concatenate([rows[:, None] * inv[None, :], cols[:, None] * inv[None, :]], axis=-1).astype(np.float32)
    cos = np.cos(ang).astype(np.float32)
    sin = np.sin(ang).astype(np.float32)
    cos = np.concatenate([cos, np.ones((LC, 32), np.float32)], 0)
    sin = np.concatenate([sin, np.zeros((LC, 32), np.float32)], 0)
    return np.tile(cos, (1, 6)), np.tile(sin, (1, 6))


def tok_shard(main, ctx, i):
    return np.ascontiguousarray(np.concatenate([main[i * TPC:(i + 1) * TPC], ctx], axis=0))


def run_pa(xl, xc, mod_l, lw):
    if "pa" not in _cache:
        _cache["pa"] = build_pa()
        _cache["rope"] = rope_tables()
    cos, sin = _cache["rope"]
    modr = np.ascontiguousarray(np.stack([mod_l[0, 0:D], mod_l[0, D:2 * D], mod_l[1, 0:D], mod_l[1, D:2 * D]]))
    qkg = np.concatenate([np.tile(lw["att_q_norm"], 4), np.tile(lw["att_k_norm"], 2)])[None, :]
    maps = []
    for i in range(NCORES):
        maps.append({"x": tok_shard(xl, xc, i), "modr": modr, "gpre": lw["norm_pre_mix"][None, :],
                     "win": lw["w_in"], "qkg": np.ascontiguousarray(qkg),
                     "rc": tok_shard(cos[:L], cos[L:], i), "rs": tok_shard(sin[:L], sin[L:], i),
                     "ident": np.eye(128, dtype=np.float32)})
    r = run(_cache["pa"], maps)
    pl = np.concatenate([r[i]["proj"][:TPC] for i in range(NCORES)], 0)
    pc = r[0]["proj"][TPC:]
    return pl, pc


def bcast_load(p, q, src_row, n=128, width=D):
    b = p.bsb([n, width])
    p.dma(q, [], [b], b[:], src_row.partition_broadcast(n))
    return b


def rms_residual(p, xin, o_ps, GG, xout, ss, rt, rstd, junk, t1, width=D):
    nb = len(o_ps)
    parts = [p.bsb([128, 1]) for _ in range(0)]
    for j, ps in enumerate(o_ps):
        p.act(junk[:, j * 512:(j + 1) * 512], ps[:, :], AF.Square, [ps], [junk, ss[j]], accum_out=ss[j][:])
    if nb == 2:
        p.tt("dve", ss[0][:], ss[0][:], ss[1][:], ALU.add, [ss[0], ss[1]], [ss[0]])
    p.act(rt[:], ss[0][:], AF.Sqrt, [ss[0]], [rt], bias=EPS, scale=1.0 / width)
    p.recip(rstd[:], rt[:], [rt], [rstd])
    for j, ps in enumerate(o_ps):
        cs = slice(j * 512, (j + 1) * 512)
        p.stt(t1[:, cs], ps[:, :], rstd[:, 0:1], GG[:, cs], ALU.mult, ALU.mult, [ps, rstd, GG], [t1])
    p.tt("dve", xout[:], t1[:], xin[:], ALU.add, [t1, xin], [xout])


def build_pc1(NT=NT, NT_MAIN=NT_MAIN):
    p = Prog()
    NTOK = NT * 128
    NM = NT_MAIN * 128
    NCX = NTOK - NM
    WP = NM + 16 + NCX + 16
    x = p.inp("x", [NTOK, D])
    yhy = p.inp("yhy", [256, NTOK])
    ys5 = p.inp("ys5", [256, NTOK])
    pp = p.inp("pp", [256, WP])
    att = p.inp("att", [256, NTOK])
    wout = p.inp("wout", [D, D])
    gluw = p.inp("gluw", [256, 256])
    glub = p.inp("glub", [128, 2])
    poolw = p.inp("poolw", [2, 128, 128])
    pools = p.inp("pools", [128, 2])
    icnt = p.inp("icnt", [128, 2, WP])
    gate = p.inp("gate", [2, D])
    gpost = p.inp("gpost", [1, D])
    xo = p.outp("xo", [NTOK, D])

    wsb = p.bsb([128, 8, D], BF16)
    for kc in range(8):
        p.dma("pool", [], [wsb], wsb[:, kc, :], wout[kc * 128:(kc + 1) * 128, :])
    gw = p.bsb([128, 2, 256], BF16)
    for kc in range(2):
        p.dma("pool", [], [gw], gw[:, kc, :], gluw[kc * 128:(kc + 1) * 128, :])
    pw = p.bsb([128, 2, 128], BF16)
    for kc in range(2):
        p.dma("pool", [], [pw], pw[:, kc, :], poolw[kc])
    gbias = p.bsb([128, 2]); p.dma("sp", [], [gbias], gbias[:], glub)
    psc = p.bsb([128, 2]); p.dma("sp", [], [psc], psc[:], pools)
    gpb = bcast_load(p, "sp", gpost)
    GG = []
    for s in range(2):
        gt = bcast_load(p, "sp", gate[s:s + 1, :])
        gg = p.bsb([128, D])
        p.tt("dve", gg[:], gt[:], gpb[:], ALU.mult, [gt, gpb], [gg])
        GG.append(gg)

    mT = p.bsb([128, 8, NTOK], BF16)
    SA = p.bsb([128, 2, WP]); SB = p.bsb([128, 2, WP]); SC = p.bsb([128, 2, WP])
    SD = p.bsb([128, 2, WP]); SE = p.bsb([128, 2, WP])
    HB = p.bsb([128, 2, WP], BF16)
    for (src, k0, st) in ((yhy, 0, SD), (att, 6, SE)):
        for kc in range(2):
            p.dma("sp", [], [st], st[:, kc, 0:NTOK], src[kc * 128:(kc + 1) * 128, :])
        p.cp("dve", mT[:, k0:k0 + 2, :], st[:, :, 0:NTOK], [st], [mT])
    yv = SA; y2 = SB; gf = SC; gbf = HB
    for kc in range(2):
        p.dma("sp", [], [yv], yv[:, kc, 0:NTOK], ys5[kc * 128:(kc + 1) * 128, :])
    N_ = slice(0, NTOK)
    p.tt("dve", y2[:, :, N_], yv[:, :, N_], yv[:, :, N_], ALU.mult, [yv], [y2])
    p.ts("dve", y2[:, :, N_], y2[:, :, N_], 0.044715, 1.0, ALU.mult, ALU.add, [y2], [y2])
    p.tt("dve", y2[:, :, N_], y2[:, :, N_], yv[:, :, N_], ALU.mult, [y2, yv], [y2])
    p.act(y2[:, :, N_], y2[:, :, N_], AF.Sigmoid, [y2], [y2], scale=2.0 * math.sqrt(2.0 / math.pi))
    p.tt("dve", gf[:, :, N_], y2[:, :, N_], yv[:, :, N_], ALU.mult, [y2, yv], [gf])
    p.cp("dve", gbf[:, :, N_], gf[:, :, N_], [gf], [gbf])
    psg = [p.bps([128, 512]) for _ in range(2)]
    sg = p.bsb([128, 512])
    ci = 0
    for t0 in range(0, NTOK, 512):
        w = min(512, NTOK - t0)
        for mc in range(2):
            ps = psg[ci % 2]; ci += 1
            for kc in range(2):
                p.mm(ps[:, 0:w], gw[:, kc, mc * 128:(mc + 1) * 128], gbf[:, kc, t0:t0 + w], kc == 0, kc == 1,
                     [gw, gbf], [ps])
            p.act(sg[:, 0:w], ps[:, 0:w], AF.Sigmoid, [ps, gbias], [sg], bias=gbias[:, mc:mc + 1])
            p.tt("dve", mT[:, 2 + mc, t0:t0 + w], sg[:, 0:w], gf[:, mc, t0:t0 + w], ALU.mult, [sg, gf], [mT])
    pv = SA; ic = SB; pm = SC; pmb = HB
    for kc in range(2):
        p.dma("sp", [], [pv], pv[:, kc, :], pp[kc * 128:(kc + 1) * 128, :])
    p.dma("sp", [], [ic], ic[:], icnt)
    p.tt("dve", SD[:, :, 1:WP], pv[:, :, 0:WP - 1], pv[:, :, 1:WP], ALU.add, [pv], [SD])
    p.tt("dve", pm[0:64, 0, 1:WP], SD[0:64, 0, 1:WP], ic[0:64, 0, 1:WP], ALU.mult, [SD, ic], [pm])
    p.tt("dve", SE[:, :, 2:WP - 1], SD[:, :, 1:WP - 2], SD[:, :, 3:WP], ALU.add, [SD], [SE])
    p.tt("dve", pm[64:128, 0, 2:WP - 1], SE[64:128, 0, 2:WP - 1], ic[64:128, 0, 2:WP - 1], ALU.mult, [SE, ic], [pm])
    p.tt("dve", SD[:, :, 4:WP - 3], SE[:, :, 2:WP - 5], SE[:, :, 6:WP - 1], ALU.add, [SE], [SD])
    p.tt("dve", pm[0:64, 1, 4:WP - 3], SD[0:64, 1, 4:WP - 3], ic[0:64, 1, 4:WP - 3], ALU.mult, [SD, ic], [pm])
    p.tt("dve", SE[:, :, 8:WP - 7], SD[:, :, 4:WP - 11], SD[:, :, 12:WP - 3], ALU.add, [SD], [SE])
    p.tt("dve", pm[64:128, 1, 8:WP - 7], SE[64:128, 1, 8:WP - 7], ic[64:128, 1, 8:WP - 7], ALU.mult, [SE, ic], [pm])
    V_ = slice(8, WP - 8)
    p.tt("dve", pmb[:, :, V_], pm[:, :, V_], pv[:, :, V_], ALU.subtract, [pm, pv], [pmb])
    segs = [(8, 0, NM), (NM + 16 + 8, NM, NCX)]
    for (so, do, cnt) in segs:
        for t0 in range(0, cnt, 512):
            w = min(512, cnt - t0)
            for kc in range(2):
                ps = psg[ci % 2]; ci += 1
                p.mm(ps[:, 0:w], pw[:, kc, :], pmb[:, kc, so + t0:so + t0 + w], True, True, [pw, pmb], [ps])
                p.ts("dve", mT[:, 4 + kc, do + t0:do + t0 + w], ps[:, 0:w], psc[:, kc:kc + 1], None, ALU.mult, None,
                     [ps, psc], [mT])
    pso = [p.bps([128, 512]) for _ in range(2)]
    xs = [p.bsb([128, D]) for _ in range(2)]
    xn = [p.bsb([128, D]) for _ in range(2)]
    junk = p.bsb([128, D], BF16)
    ss = [p.bsb([128, 1]) for _ in range(2)]
    rt = p.bsb([128, 1]); rstd = p.bsb([128, 1]); t1 = p.bsb([128, D])
    for i in range(NT):
        b = i % 2
        s = 0 if i < NT_MAIN else 1
        rows = slice(i * 128, (i + 1) * 128)
        p.dma("sp", [], [xs[b]], xs[b][:], x[rows, :])
        for j in range(2):
            for kc in range(8):
                p.mm(pso[j][:, :], mT[:, kc, i * 128:(i + 1) * 128], wsb[:, kc, j * 512:(j + 1) * 512], kc == 0, kc == 7,
                     [mT, wsb], [pso[j]])
        rms_residual(p, xs[b], pso, GG[s], xn[b], ss, rt, rstd, junk, t1)
        p.dma("sp", [xn[b]], [], xo[rows, :], xn[b][:])
    return p.finish()


def pool_inv_counts(Lseq):
    t = np.arange(Lseq)
    out = []
    for w in (2, 4, 8, 16):
        lo = np.clip(t - w // 2, 0, Lseq); hi = np.clip(t + w // 2, 0, Lseq)
        out.append((1.0 / (hi - lo)).astype(np.float32))
    return np.stack(out)


def pad_seg(a, lo, hi):
    C, Ls = a.shape
    out = np.zeros((C, hi - lo), a.dtype)
    s0, s1 = max(lo, 0), min(hi, Ls)
    out[:, s0 - lo:s1 - lo] = a[:, s0:s1]
    return out


def run_pc1(xl, xc, yhy_l, yhy_c, ys5_l, ys5_c, p_l, p_c, att_l, att_c, mod_l, lw):
    if "pc1" not in _cache:
        _cache["pc1"] = build_pc1()
        _cache["icm"] = pool_inv_counts(L); _cache["icc"] = pool_inv_counts(LC)
    icm, icc = _cache["icm"], _cache["icc"]
    poolw = np.zeros((2, 128, 128), np.float32)
    for g in range(4):
        poolw[g // 2, (g % 2) * 64:(g % 2) * 64 + 64, (g % 2) * 64:(g % 2) * 64 + 64] = lw["pool_w"][g]
    gate = np.ascontiguousarray(np.stack([mod_l[0, 2 * D:3 * D], mod_l[1, 2 * D:3 * D]]))
    maps = []
    for i in range(NCORES):
        lo, hi = i * TPC, (i + 1) * TPC
        cat = lambda m, c: np.ascontiguousarray(np.concatenate([m[lo:hi], c], 0).T)
        pp = np.concatenate([pad_seg(p_l.T, lo - 8, hi + 8), pad_seg(p_c.T, -8, LC + 8)], 1)
        ic = np.concatenate([pad_seg(icm, lo - 8, hi + 8), pad_seg(icc, -8, LC + 8)], 1)
        icnt = np.repeat(ic.reshape(2, 2, 1, -1), 64, axis=2).reshape(2, 128, -1).transpose(1, 0, 2)
        maps.append({"x": tok_shard(xl, xc, i), "yhy": cat(yhy_l, yhy_c), "ys5": cat(ys5_l, ys5_c),
                     "pp": np.ascontiguousarray(pp), "att": cat(att_l, att_c), "wout": lw["w_out"],
                     "gluw": lw["s5_glu_w"], "glub": np.ascontiguousarray(lw["s5_glu_b"].reshape(2, 128).T),
                     "poolw": poolw, "pools": np.ascontiguousarray(lw["pool_scale"].reshape(2, 128).T),
                     "icnt": np.ascontiguousarray(icnt), "gate": gate, "gpost": lw["norm_post_mix"][None, :]})
    r = run(_cache["pc1"], maps)
    return (np.concatenate([r[i]["xo"][:TPC] for i in range(NCORES)], 0), r[0]["xo"][TPC:])


def build_pc2(NT=NT, NT_MAIN=NT_MAIN, GT=2):
    p = Prog()
    NTOK = NT * 128
    x = p.inp("x", [NTOK, D])
    w1 = p.inp("w1", [D, 4 * D])
    w2 = p.inp("w2", [4 * D, D])
    modr = p.inp("modr", [6, D])
    gpre = p.inp("gpre", [1, D])
    gpost = p.inp("gpost", [1, D])
    ident = p.inp("ident", [128, 128])
    xo = p.outp("xo", [NTOK, D])
    idb = p.bsb([128, 128], BF16)
    p.dma("pool", [], [idb], idb[:], ident)
    w1sb = p.bsb([128, 8, 4 * D], BF16)
    for kc in range(8):
        for hh in range(2):
            p.dma("pool", [], [w1sb], w1sb[:, kc, hh * 2048:(hh + 1) * 2048],
                  w1[kc * 128:(kc + 1) * 128, hh * 2048:(hh + 1) * 2048])
    w2sb = p.bsb([128, 32, D], BF16)
    for fc in range(32):
        p.dma("pool", [], [w2sb], w2sb[:, fc, :], w2[fc * 128:(fc + 1) * 128, :])
    tmpc = p.bsb([128, D])
    Gc = p.bsb([128, D]); SHc = p.bsb([128, D]); GGc = p.bsb([128, D])
    W = GT * 128
    xs = p.bsb([128, GT, D])
    hT = p.bsb([128, 8, W], BF16)
    uT = p.bsb([128, 32, W], BF16)
    junk = p.bsb([128, D], BF16)
    ss = [p.bsb([128, 1]) for _ in range(2)]
    rt = p.bsb([128, 1]); rstd = p.bsb([128, 1]); t1 = p.bsb([128, D])
    hb = p.bsb([128, D], BF16)
    psT = p.bps([128, 8, 128], BF16)
    psu = [p.bps([128, 512]) for _ in range(2)]
    pso = [p.bps([128, 512]) for _ in range(2)]
    rl = p.bsb([128, W])
    xn = [p.bsb([128, D]) for _ in range(2)]
    cur_stream = -1
    tiles = list(range(NT))
    groups = []
    i = 0
    while i < NT:
        lim = NT_MAIN if i < NT_MAIN else NT
        g = tiles[i:min(i + GT, lim)]
        groups.append(g)
        i += len(g)
    for g in groups:
        s = 0 if g[0] < NT_MAIN else 1
        if s != cur_stream:
            cur_stream = s
            p.dma("sp", [], [tmpc], tmpc[:], gpre.partition_broadcast(128))
            p.dma("sp", [], [Gc], Gc[:], modr[3 * s + 1:3 * s + 2, :].partition_broadcast(128))
            p.stt(Gc[:], Gc[:], 1.0, tmpc[:], ALU.add, ALU.mult, [Gc, tmpc], [Gc])
            p.dma("sp", [], [SHc], SHc[:], modr[3 * s:3 * s + 1, :].partition_broadcast(128))
            p.dma("sp", [], [tmpc], tmpc[:], gpost.partition_broadcast(128))
            p.dma("sp", [], [GGc], GGc[:], modr[3 * s + 2:3 * s + 3, :].partition_broadcast(128))
            p.tt("dve", GGc[:], GGc[:], tmpc[:], ALU.mult, [GGc, tmpc], [GGc])
        w = len(g) * 128
        for gi, ti in enumerate(g):
            rows = slice(ti * 128, (ti + 1) * 128)
            p.dma("sp", [], [xs], xs[:, gi, :], x[rows, :])
            p.act(junk[:], xs[:, gi, :], AF.Square, [xs], [junk, ss[0]], accum_out=ss[0][:])
            p.act(rt[:], ss[0][:], AF.Sqrt, [ss[0]], [rt], bias=EPS, scale=1.0 / D)
            p.recip(rstd[:], rt[:], [rt], [rstd])
            p.stt(t1[:], xs[:, gi, :], rstd[:, 0:1], Gc[:], ALU.mult, ALU.mult, [xs, rstd, Gc], [t1])
            p.tt("dve", hb[:], t1[:], SHc[:], ALU.add, [t1, SHc], [hb])
            for kc in range(8):
                p.tr(psT[:, kc, :], hb[:, kc * 128:(kc + 1) * 128], idb[:], [hb, idb], [psT])
            p.cp("act", hT[:, :, gi * 128:(gi + 1) * 128], psT[:], [psT], [hT])
        for fc in range(32):
            ps = psu[fc % 2]
            for kc in range(8):
                p.mm(ps[:, 0:w], w1sb[:, kc, fc * 128:(fc + 1) * 128], hT[:, kc, 0:w], kc == 0, kc == 7,
                     [w1sb, hT], [ps])
            p.act(rl[:, 0:w], ps[:, 0:w], AF.Relu, [ps], [rl])
            p.tt("dve", uT[:, fc, 0:w], rl[:, 0:w], rl[:, 0:w], ALU.mult, [rl], [uT])
        for gi, ti in enumerate(g):
            rows = slice(ti * 128, (ti + 1) * 128)
            b = ti % 2
            for j in range(2):
                for fc in range(32):
                    p.mm(pso[j][:, :], uT[:, fc, gi * 128:(gi + 1) * 128], w2sb[:, fc, j * 512:(j + 1) * 512],
                         fc == 0, fc == 31, [uT, w2sb], [pso[j]])
            rms_residual(p, _View(xs, lambda t, gi=gi: t[:, gi, :]), pso, GGc, xn[b], ss, rt, rstd, junk, t1)
            p.dma("sp", [xn[b]], [], xo[rows, :], xn[b][:])
    return p.finish()


class _View:
    def __init__(self, buf, fn):
        self._b = buf
        self._fn = fn

    def __getitem__(self, k):
        return self._fn(self._b.t)

    @property
    def w(self):
        return self._b.w

    @w.setter
    def w(self, v):
        self._b.w = v

    @property
    def r(self):
        return self._b.r

    @r.setter
    def r(self, v):
        self._b.r = v


def run_pc2(xl, xc, mod_l, lw):
    if "pc2" not in _cache:
        _cache["pc2"] = build_pc2()
    modr = np.ascontiguousarray(np.stack([mod_l[s, k * D:(k + 1) * D] for s in range(2) for k in (3, 4, 5)]))
    maps = []
    for i in range(NCORES):
        maps.append({"x": tok_shard(xl, xc, i), "w1": lw["mlp_w1"], "w2": lw["mlp_w2"], "modr": modr,
                     "gpre": lw["norm_pre_mlp"][None, :], "gpost": lw["norm_post_mlp"][None, :],
                     "ident": np.eye(128, dtype=np.float32)})
    r = run(_cache["pc2"], maps)
    return (np.concatenate([r[i]["xo"][:TPC] for i in range(NCORES)], 0), r[0]["xo"][TPC:])


def build_pt(NQM=TPC, NQC=LC, NKM=L, NKC=LC):
    p = Prog()
    NQ = NQM + NQC
    NK = NKM + NKC
    NKT = NK // 128
    qT = p.inp("qT", [64, 4, NQ])
    kT = p.inp("kT", [64, 2, NK])
    vt = p.inp("vt", [128, NKT, 2, 64])
    oT = p.outp("oT", [256, NQ])
    qsb = p.bsb([64, 4, NQ], BF16)
    for h in range(4):
        for c0 in range(0, NQ, 2048):
            w = min(2048, NQ - c0)
            p.dma("pool", [], [qsb], qsb[:, h, c0:c0 + w], qT[:, h, c0:c0 + w])
    ksb = p.bsb([64, 2, NK], BF16)
    for kv in range(2):
        for c0 in range(0, NK, 2048):
            w = min(2048, NK - c0)
            p.dma("pool", [], [ksb], ksb[:, kv, c0:c0 + w], kT[:, kv, c0:c0 + w])
    v1 = p.bsb([128, NKT, 2, 65], BF16)
    p.memset("dve", v1[:], 1.0, [v1])
    vst = [p.bsb([128, 13, 2, 64]) for _ in range(2)]
    for gi, k0 in enumerate(range(0, NKT, 13)):
        n = min(13, NKT - k0)
        st = vst[gi % 2]
        p.dma("sp", [], [st], st[:, 0:n], vt[:, k0:k0 + n])
        p.cp("dve", v1[:, k0:k0 + n, :, 0:64], st[:, 0:n], [st], [v1])
    ones = p.bsb([128, 64]); p.memset("dve", ones[:], 1.0, [ones])
    pss = [p.bps([128, 512]) for _ in range(3)]
    pso = [p.bps([128, 512]) for _ in range(2)]
    psb = p.bps([128, 512])
    pT = [p.bsb([128, 512], BF16) for _ in range(3)]
    rs = p.bsb([128, 512]); oc = p.bsb([128, 512])
    on = [p.bsb([64, 512]) for _ in range(2)]
    jobs = []
    for h in range(4):
        for q0 in range(0, NQM, 512):
            jobs.append((h, q0, min(512, NQM - q0), list(range(NKT))))
        jobs.append((h, NQM, NQC, list(range(NKM // 128, NKT))))
    it = 0
    for ji, (h, q0, w, kts) in enumerate(jobs):
        kv = h // 2
        po = pso[ji % 2]
        for n, kt in enumerate(kts):
            ps = pss[it % 3]; pt = pT[it % 3]; it += 1
            p.mm(ps[:, 0:w], ksb[:, kv, kt * 128:(kt + 1) * 128], qsb[:, h, q0:q0 + w], True, True, [ksb, qsb], [ps])
            p.act(pt[:, 0:w], ps[:, 0:w], AF.Exp, [ps], [pt], scale=0.125)
            p.mm(po[0:65, 0:w], v1[:, kt, kv, :], pt[:, 0:w], n == 0, n == len(kts) - 1, [v1, pt], [po])
        p.recip(rs[64:65, 0:w], po[64:65, 0:w], [po], [rs])
        p.cp("act", oc[0:64, 0:w], po[0:64, 0:w], [po], [oc])
        p.mm(psb[0:64, 0:w], ones[64:65, 0:64], rs[64:65, 0:w], True, True, [ones, rs], [psb])
        o = on[ji % 2]
        p.tt("dve", o[:, 0:w], oc[0:64, 0:w], psb[0:64, 0:w], ALU.mult, [oc, psb], [o])
        p.dma("sp", [o], [], oT[h * 64:(h + 1) * 64, q0:q0 + w], o[:, 0:w])
    return p.finish()


def run_pt(pl, pc):
    if "pt" not in _cache:
        _cache["pt"] = build_pt()
    kall = np.concatenate([pl[:, 1536:1664], pc[:, 1536:1664]], 0)
    vall = np.concatenate([pl[:, 1664:1792], pc[:, 1664:1792]], 0)
    NK = kall.shape[0]
    kT = np.ascontiguousarray(kall.reshape(NK, 2, 64).transpose(2, 1, 0))
    vt = np.ascontiguousarray(vall.reshape(NK // 128, 128, 2, 64).transpose(1, 0, 2, 3))
    maps = []
    for i in range(NCORES):
        q = np.concatenate([pl[i * TPC:(i + 1) * TPC, 1280:1536], pc[:, 1280:1536]], 0)
        maps.append({"qT": np.ascontiguousarray(q.reshape(-1, 4, 64).transpose(2, 1, 0)), "kT": kT, "vt": vt})
    r = run(_cache["pt"], maps)
    att_l = np.concatenate([r[i]["oT"][:, :TPC].T for i in range(NCORES)], 0)
    att_c = r[0]["oT"][:, TPC:].T
    return att_l, att_c


MAGIC = 12582912.0
TWO_PI = 2.0 * math.pi
PI_LO = 3.1415925


def sin_rr(p, out, in_, phase, tmp, reads, writes):
    p.ts("dve", tmp, in_, 1.0 / TWO_PI, phase / TWO_PI, ALU.mult, ALU.add, reads, writes)
    p.ts("dve", tmp, tmp, MAGIC, None, ALU.add, None, writes, writes)
    p.ts("dve", tmp, tmp, -MAGIC, None, ALU.add, None, writes, writes)
    p.stt(tmp, tmp, -TWO_PI, in_, ALU.mult, ALU.add, reads + writes, writes)
    p.ts("dve", tmp, tmp, phase, PI_LO, ALU.add, ALU.min, writes, writes)
    p.ts("dve", tmp, tmp, -PI_LO, None, ALU.max, None, writes, writes)
    return tmp


def build_ps(NSC=LC, NSM=L, T=512):
    p = Prog()
    NS = NSC + NSM
    u_f = p.inp("uf", [32, NS])
    u_r = p.inp("ur", [32, NS])
    are = p.inp("are", [2, 128, 1]); aim = p.inp("aim", [2, 128, 1]); ldt = p.inp("ldt", [2, 128, 1])
    bre = p.inp("bre", [2, 128, 16]); bim = p.inp("bim", [2, 128, 16])
    cre = p.inp("cre", [2, 128, 32]); cim = p.inp("cim", [2, 128, 32])
    dd = p.inp("dd", [32, 32])
    jt = p.inp("jt", [128, T + 1])
    ident = p.inp("ident", [128, 128])
    yo = p.outp("y", [32, NS])

    ub = [p.bsb([32, NS], BF16), p.bsb([32, NS], BF16)]
    for d, src in enumerate((u_f, u_r)):
        for c0 in range(0, NS, 2048):
            w = min(2048, NS - c0)
            p.dma("pool", [], [ub[d]], ub[d][:, c0:c0 + w], src[:, c0:c0 + w])
    yf = p.bsb([32, NS])
    idf = p.bsb([128, 128]); p.dma("sp", [], [idf], idf[:], ident)
    jts = p.bsb([128, T + 1]); p.dma("sp", [], [jts], jts[:], jt)
    ddb = p.bsb([32, 32], BF16); p.dma("pool", [], [ddb], ddb[:], dd)
    pst = p.bps([128, 512])
    psA = [p.bps([128, 512]) for _ in range(2)]
    psB = [p.bps([128, 512]) for _ in range(2)]
    psY = [p.bps([128, 512]) for _ in range(2)]

    def small(n=1):
        return p.bsb([128, n])

    chunks = [(0, NSC)] + [(NSC + k * T, T) for k in range(NSM // T)]
    m1 = p.bsb([128, T]); m2 = p.bsb([128, T]); m3 = p.bsb([128, T]); m4 = p.bsb([128, T])
    btr = p.bsb([128, T]); bti = p.bsb([128, T])
    gr = [p.bsb([128, T]) for _ in range(2)]; gi = [p.bsb([128, T]) for _ in range(2)]
    hr = [p.bsb([128, T], BF16) for _ in range(2)]; hi = [p.bsb([128, T], BF16) for _ in range(2)]
    cosT = p.bsb([128, T + 1]); sinT = p.bsb([128, T + 1]); xt = p.bsb([128, T + 1]); tmpT = p.bsb([128, T + 1])
    rB = p.bsb([128, T])
    n1 = p.bsb([128, T]); n2 = p.bsb([128, T]); n3 = p.bsb([128, T]); n4 = p.bsb([128, T])
    for d in range(2):
        a_re = small(); a_im = small(); l_dt = small()
        p.dma("sp", [], [a_re], a_re[:], are[d]); p.dma("sp", [], [a_im], a_im[:], aim[d])
        p.dma("sp", [], [l_dt], l_dt[:], ldt[d])
        b_re = small(16); b_im = small(16)
        p.dma("sp", [], [b_re], b_re[:], bre[d]); p.dma("sp", [], [b_im], b_im[:], bim[d])
        c_re = p.bsb([128, 32], BF16); c_imf = small(32); c_imn = p.bsb([128, 32], BF16)
        p.dma("pool", [], [c_re], c_re[:], cre[d]); p.dma("sp", [], [c_imf], c_imf[:], cim[d])
        p.ts("dve", c_imn[:], c_imf[:], -1.0, None, ALU.mult, None, [c_imf], [c_imn])
        dt = small(); mag = small(); ang = small(); t0_ = small(); t1_ = small()
        p.act(dt[:], l_dt[:], AF.Exp, [l_dt], [dt])
        p.tt("dve", t0_[:], a_re[:], dt[:], ALU.mult, [a_re, dt], [t0_])
        p.act(mag[:], t0_[:], AF.Exp, [t0_], [mag])
        p.tt("dve", ang[:], a_im[:], dt[:], ALU.mult, [a_im, dt], [ang])
        sn = small(); cs = small(); w0 = small(); w1 = small()
        sin_rr(p, w0[:], ang[:], 0.0, w0[:], [ang], [w0])
        p.act(sn[:], w0[:], AF.Sin, [w0], [sn])
        sin_rr(p, w1[:], ang[:], math.pi / 2, w1[:], [ang], [w1])
        p.act(cs[:], w1[:], AF.Sin, [w1], [cs])
        lre = small(); lim = small()
        p.tt("dve", lre[:], mag[:], cs[:], ALU.mult, [mag, cs], [lre])
        p.tt("dve", lim[:], mag[:], sn[:], ALU.mult, [mag, sn], [lim])
        den = small(); rden = small(); nr = small()
        p.tt("dve", den[:], a_re[:], a_re[:], ALU.mult, [a_re], [den])
        p.stt(den[:], a_im[:], a_im[:, 0:1], den[:], ALU.mult, ALU.add, [a_im, den], [den])
        p.recip(rden[:], den[:], [den], [rden])
        p.ts("dve", nr[:], lre[:], -1.0, None, ALU.add, None, [lre], [nr])
        cr = small(); ci = small(); nci = small()
        p.tt("dve", t0_[:], lim[:], a_im[:], ALU.mult, [lim, a_im], [t0_])
        p.stt(cr[:], nr[:], a_re[:, 0:1], t0_[:], ALU.mult, ALU.add, [nr, a_re, t0_], [cr])
        p.tt("dve", cr[:], cr[:], rden[:], ALU.mult, [cr, rden], [cr])
        p.tt("dve", t1_[:], nr[:], a_im[:], ALU.mult, [nr, a_im], [t1_])
        p.stt(ci[:], lim[:], a_re[:, 0:1], t1_[:], ALU.mult, ALU.subtract, [lim, a_re, t1_], [ci])
        p.tt("dve", ci[:], ci[:], rden[:], ALU.mult, [ci, rden], [ci])
        p.ts("dve", nci[:], ci[:], -1.0, None, ALU.mult, None, [ci], [nci])
        bbr = small(16); bbi = small(16)
        p.ts("dve", bbr[:], b_re[:], cr[:, 0:1], None, ALU.mult, None, [b_re, cr], [bbr])
        p.stt(bbr[:], b_im[:], nci[:, 0:1], bbr[:], ALU.mult, ALU.add, [b_im, nci, bbr], [bbr])
        p.ts("dve", bbi[:], b_im[:], cr[:, 0:1], None, ALU.mult, None, [b_im, cr], [bbi])
        p.stt(bbi[:], b_re[:], ci[:, 0:1], bbi[:], ALU.mult, ALU.add, [b_re, ci, bbi], [bbi])
        LB = []
        for bb in (bbr, bbi):
            bx = small(32)
            p.memset("dve", bx[:], 0.0, [bx])
            p.cp("dve", bx[0:64, 0:16], bb[0:64, :], [bb], [bx])
            p.cp("dve", bx[64:128, 16:32], bb[64:128, :], [bb], [bx])
            p.tr(pst[0:32, 0:128], bx[:], idf[:], [bx, idf], [pst])
            lb = p.bsb([32, 128], BF16)
            p.cp("dve", lb[:], pst[0:32, 0:128], [pst], [lb])
            LB.append(lb)
        p.ts("dve", xt[:], jts[:], ang[:, 0:1], None, ALU.mult, None, [jts, ang], [xt])
        sin_rr(p, tmpT[:], xt[:], 0.0, tmpT[:], [xt], [tmpT])
        p.act(sinT[:], tmpT[:], AF.Sin, [tmpT], [sinT])
        sin_rr(p, tmpT[:], xt[:], math.pi / 2, tmpT[:], [xt], [tmpT])
        p.act(cosT[:], tmpT[:], AF.Sin, [tmpT], [cosT])
        p.ts("dve", rB[:], jts[:, 0:T], 0.0, mag[:, 0:1], ALU.mult, ALU.add, [jts, mag], [rB])
        init = [(small(), small()) for _ in range(2)]
        p.memset("dve", init[0][0][:], 0.0, [init[0][0]]); p.memset("dve", init[0][1][:], 0.0, [init[0][1]])
        tq = small(); tq2 = small()
        for ck, (c0, Tc) in enumerate(chunks):
            b = ck % 2
            A, B_, Y = psA[b], psB[b], psY[b]
            rhs = ub[d][:, c0:c0 + Tc]
            p.mm(A[:, 0:Tc], LB[0][:], rhs, True, True, [LB[0], ub[d]], [A])
            p.mm(B_[:, 0:Tc], LB[1][:], rhs, True, True, [LB[1], ub[d]], [B_])
            C_, S_ = cosT[:, 0:Tc], sinT[:, 0:Tc]
            p.tt("dve", m1[:, 0:Tc], A[:, 0:Tc], C_, ALU.mult, [A, cosT], [m1])
            p.tt("dve", m2[:, 0:Tc], B_[:, 0:Tc], S_, ALU.mult, [B_, sinT], [m2])
            p.tt("dve", btr[:, 0:Tc], m1[:, 0:Tc], m2[:, 0:Tc], ALU.add, [m1, m2], [btr])
            p.tt("dve", m3[:, 0:Tc], B_[:, 0:Tc], C_, ALU.mult, [B_, cosT], [m3])
            p.tt("dve", m4[:, 0:Tc], A[:, 0:Tc], S_, ALU.mult, [A, sinT], [m4])
            p.tt("dve", bti[:, 0:Tc], m3[:, 0:Tc], m4[:, 0:Tc], ALU.subtract, [m3, m4], [bti])
            ir, ii = init[b]
            p.op("dve", [rB, btr, ir], [gr[b]], lambda e, b=b, Tc=Tc, ir=ir: e.tensor_tensor_scan(
                gr[b][:, 0:Tc], rB[:, 0:Tc], btr[:, 0:Tc], ir[:, 0:1], ALU.mult, ALU.add))
            p.op("dve", [rB, bti, ii], [gi[b]], lambda e, b=b, Tc=Tc, ii=ii: e.tensor_tensor_scan(
                gi[b][:, 0:Tc], rB[:, 0:Tc], bti[:, 0:Tc], ii[:, 0:1], ALU.mult, ALU.add))
            nir, nii = init[1 - b]
            cT, sT = cosT[:, Tc:Tc + 1], sinT[:, Tc:Tc + 1]
            ge_r, ge_i = gr[b][:, Tc - 1:Tc], gi[b][:, Tc - 1:Tc]
            p.ts("dve", tq[:], ge_i, sT, None, ALU.mult, None, [gi[b], sinT], [tq])
            p.stt(nir[:], ge_r, cT, tq[:], ALU.mult, ALU.subtract, [gr[b], cosT, tq], [nir])
            p.ts("dve", tq2[:], ge_i, cT, None, ALU.mult, None, [gi[b], cosT], [tq2])
            p.stt(nii[:], ge_r, sT, tq2[:], ALU.mult, ALU.add, [gr[b], sinT, tq2], [nii])
            p.tt("pool", n1[:, 0:Tc], gr[b][:, 0:Tc], C_, ALU.mult, [gr[b], cosT], [n1])
            p.tt("pool", n2[:, 0:Tc], gi[b][:, 0:Tc], S_, ALU.mult, [gi[b], sinT], [n2])
            p.tt("pool", hr[b][:, 0:Tc], n1[:, 0:Tc], n2[:, 0:Tc], ALU.subtract, [n1, n2], [hr[b]])
            p.tt("pool", n3[:, 0:Tc], gr[b][:, 0:Tc], S_, ALU.mult, [gr[b], sinT], [n3])
            p.tt("pool", n4[:, 0:Tc], gi[b][:, 0:Tc], C_, ALU.mult, [gi[b], cosT], [n4])
            p.tt("pool", hi[b][:, 0:Tc], n3[:, 0:Tc], n4[:, 0:Tc], ALU.add, [n3, n4], [hi[b]])
            p.mm(Y[0:32, 0:Tc], c_re[:], hr[b][:, 0:Tc], True, False, [c_re, hr[b]], [Y])
            p.mm(Y[0:32, 0:Tc], c_imn[:], hi[b][:, 0:Tc], False, d == 1, [c_imn, hi[b]], [Y])
            if d == 0:
                p.mm(Y[0:32, 0:Tc], ddb[:], rhs, False, True, [ddb, ub[d]], [Y])
                p.cp("act", yf[:, c0:c0 + Tc], Y[0:32, 0:Tc], [Y], [yf])
            else:
                if ck == 0:
                    lo = 0
                else:
                    lo = NSC + NSM - (ck) * T
                yv = yf[:, lo:lo + Tc]
                p.tt("dve", yv, yv, Y[0:32, 0:Tc][:, ::-1], ALU.add, [yf, Y], [yf])
    for c0 in range(0, NS, 4096):
        w = min(4096, NS - c0)
        p.dma("sp", [yf], [], yo[:, c0:c0 + w], yf[:, c0:c0 + w])
    return p.finish()


def _cbias(p, val):
    key = ("cb", val)
    if not hasattr(p, "_consts"):
        p._consts = {}
    if key not in p._consts:
        b = p.bsb([128, 1])
        p.memset("dve", b[:], val, [b])
        p._consts[key] = b
    return p._consts[key]


def run_ps(pl, pc, lw):
    if "ps" not in _cache:
        _cache["ps"] = build_ps()
    T = 512
    s_l = pl[:, 768:1024]; s_c = pc[:, 768:1024]
    seq_f = np.concatenate([s_c, s_l], 0)
    seq_r = np.concatenate([s_c[::-1], s_l[::-1]], 0)
    jt = np.ascontiguousarray(np.tile(np.arange(T + 1, dtype=np.float32)[None, :], (128, 1)))
    maps = []
    for i in range(NCORES):
        g0 = 2 * i
        ch = slice(32 * i, 32 * i + 32)

        def gp(a):
            return np.ascontiguousarray(a[:, g0:g0 + 2].reshape(2, 128, *a.shape[3:]))
        cre = np.zeros((2, 128, 32), np.float32); cim = np.zeros((2, 128, 32), np.float32)
        for d in range(2):
            for gl in range(2):
                cre[d, gl * 64:(gl + 1) * 64, gl * 16:(gl + 1) * 16] = lw["s5_c_re"][d, g0 + gl].T
                cim[d, gl * 64:(gl + 1) * 64, gl * 16:(gl + 1) * 16] = lw["s5_c_im"][d, g0 + gl].T
        ldt = np.repeat(lw["s5_log_dt"][:, g0:g0 + 2, None], 64, axis=2).reshape(2, 128, 1)
        dd = np.zeros((32, 32), np.float32); dd[np.arange(32), np.arange(32)] = lw["s5_d"][ch]
        maps.append({"uf": np.ascontiguousarray(seq_f[:, ch].T), "ur": np.ascontiguousarray(seq_r[:, ch].T),
                     "are": gp(lw["s5_a_re"])[..., None], "aim": gp(lw["s5_a_im"])[..., None],
                     "ldt": np.ascontiguousarray(ldt), "bre": gp(lw["s5_b_re"]), "bim": gp(lw["s5_b_im"]),
                     "cre": cre, "cim": cim, "dd": dd, "jt": jt, "ident": np.eye(128, dtype=np.float32)})
    r = run(_cache["ps"], maps)
    y = np.concatenate([r[i]["y"] for i in range(NCORES)], 0).T
    return np.ascontiguousarray(y[LC:]), np.ascontiguousarray(y[:LC])


def build_ph(LM=L, LCX=LC):
    p = Prog()
    nc = p.nc
    NJ = LM // 128
    NJC = LCX // 128
    a3m = p.inp("a3m", [3, 128, NJ, 96]); a3c = p.inp("a3c", [3, 128, NJC, 96])
    cw = p.inp("cw", [1, 3 * 96]); cb = p.inp("cb", [1, 96])
    fw1 = p.inp("fw1", [33, 64]); fb1 = p.inp("fb1", [64, 1]); fw2 = p.inp("fw2", [64, 64]); fb2 = p.inp("fb2", [64, 1])
    fw3 = p.inp("fw3", [64, 128]); dec = p.inp("dec", [128, 1]); fbias = p.inp("fbias", [1, 64])
    ftm = p.inp("ftm", [33, LM]); ftc = p.inp("ftc", [33, LCX])
    antiid = p.inp("antiid", [128, 128])
    ym = p.outp("ym", [128, NJ, 32]); yc = p.outp("yc", [128, NJC, 32])
    kdm_t = nc.dram_tensor("kdm", [64, 2 * LM], BF16, kind="Internal")
    kdc_t = nc.dram_tensor("kdc", [64, 2 * LCX], BF16, kind="Internal")
    d_kdm = Dep(); d_kdc = Dep()

    w1s = p.bsb([33, 64]); p.dma("sp", [], [w1s], w1s[:], fw1)
    w2s = p.bsb([64, 64]); p.dma("sp", [], [w2s], w2s[:], fw2)
    w3s = p.bsb([64, 128]); p.dma("sp", [], [w3s], w3s[:], fw3)
    b1s = p.bsb([64, 1]); p.dma("sp", [], [b1s], b1s[:], fb1)
    b2s = p.bsb([64, 1]); p.dma("sp", [], [b2s], b2s[:], fb2)
    dcs = p.bsb([128, 1]); p.dma("sp", [], [dcs], dcs[:], dec)
    nd = p.bsb([128, 1])
    p.ts("dve", nd[:], dcs[:], -1.0, None, ALU.mult, None, [dcs], [nd])
    p.tt("dve", nd[:], nd[:], dcs[:], ALU.min, [nd, dcs], [nd])
    cws = bcast_load(p, "sp", cw, 128, 3 * 96)
    cbs = bcast_load(p, "sp", cb, 128, 96)
    fbs = bcast_load(p, "sp", fbias, 128, 64)
    Jb = p.bsb([128, 128], BF16)
    p.dma("pool", [], [Jb], Jb[:], antiid)

    fts = p.bsb([33, 512]); tg = p.bsb([128, 512])
    xa = p.bsb([64, 512]); xb = p.bsb([64, 512]); h1 = p.bsb([64, 512]); h2 = p.bsb([64, 512])
    E = p.bsb([128, 512]); hf = p.bsb([128, 512], BF16); hrv = p.bsb([128, 512], BF16)
    ps1 = p.bps([128, 512]); ps2 = p.bps([128, 512]); ps3 = p.bps([128, 512])

    def gen(ft, Lg, kd_t, d_kd):
        kd = kd_t.ap()
        for c0 in range(0, Lg, 512):
            w = min(512, Lg - c0)
            p.dma("sp", [], [fts], fts[:, 0:w], ft[:, c0:c0 + w])
            p.dma("sp", [], [tg], tg[:, 0:w], ft[0:1, c0:c0 + w].partition_broadcast(128))
            p.mm(ps1[0:64, 0:w], w1s[:], fts[:, 0:w], True, True, [w1s, fts], [ps1])
            p.ts("dve", xa[:, 0:w], ps1[0:64, 0:w], b1s[:, 0:1], None, ALU.add, None, [ps1, b1s], [xa])
            sin_rr(p, None, xa[:, 0:w], 0.0, xb[:, 0:w], [xa], [xb])
            p.act(h1[:, 0:w], xb[:, 0:w], AF.Sin, [xb], [h1])
            p.mm(ps2[0:64, 0:w], w2s[:], h1[:, 0:w], True, True, [w2s, h1], [ps2])
            p.ts("dve", xa[:, 0:w], ps2[0:64, 0:w], b2s[:, 0:1], None, ALU.add, None, [ps2, b2s], [xa])
            sin_rr(p, None, xa[:, 0:w], 0.0, xb[:, 0:w], [xa], [xb])
            p.act(h2[:, 0:w], xb[:, 0:w], AF.Sin, [xb], [h2])
            p.mm(ps3[:, 0:w], w3s[:], h2[:, 0:w], True, True, [w3s, h2], [ps3])
            p.act(E[:, 0:w], tg[:, 0:w], AF.Exp, [tg, nd], [E], scale=nd[:, 0:1])
            p.tt("dve", hf[:, 0:w], ps3[:, 0:w], E[:, 0:w], ALU.mult, [ps3, E], [hf])
            p.cp("dve", hrv[:, 0:w], hf[:, 0:w][:, ::-1], [hf], [hrv])
            for o in range(2):
                p.dma("sp", [hf], [d_kd], kd[o * 32:(o + 1) * 32, Lg + c0:Lg + c0 + w], hf[o * 64:o * 64 + 32, 0:w])
                p.dma("sp", [hrv], [d_kd], kd[o * 32:(o + 1) * 32, Lg - c0 - w:Lg - c0],
                      hrv[o * 64 + 32:o * 64 + 64, 0:w])

    gen(ftc, LCX, kdc_t, d_kdc)
    gen(ftm, LM, kdm_t, d_kdm)

    U = p.bsb([128, NJ, 96])
    HS = max(1, NJ // 4)
    stage = p.bsb([128, HS, 96])
    z1 = p.bsb([128, NJ, 32]); zb = p.bsb([128, NJ, 32], BF16); yout = p.bsb([128, NJ, 32])
    zbr = p.bsb([128, NJ, 32], BF16)
    strips = [p.bsb([128, 128 * 128], BF16) for _ in range(2)]
    psY = [p.bps([128, 512]) for _ in range(2)]
    tq = p.bsb([128, NJ])
    state = {"si": 0}

    def stream(a3, NJs, Lg, kd_t, d_kd, yo):
        for k in range(3):
            for j0 in range(0, NJs, HS):
                n = min(HS, NJs - j0)
                p.dma("sp", [], [stage], stage[:, 0:n, :], a3[k, :, j0:j0 + n, :])
                wk = cws[:, k * 96:(k + 1) * 96].unsqueeze(1).to_broadcast([128, n, 96])
                if k == 0:
                    p.tt("dve", U[:, j0:j0 + n, :], stage[:, 0:n, :], wk, ALU.mult, [stage, cws], [U])
                    p.tt("dve", U[:, j0:j0 + n, :], U[:, j0:j0 + n, :],
                         cbs[:, :].unsqueeze(1).to_broadcast([128, n, 96]), ALU.add, [U, cbs], [U])
                else:
                    p.tt("dve", stage[:, 0:n, :], stage[:, 0:n, :], wk, ALU.mult, [stage, cws], [stage])
                    p.tt("dve", U[:, j0:j0 + n, :], U[:, j0:j0 + n, :], stage[:, 0:n, :], ALU.add, [U, stage], [U])
        for o in range(2):
            zsrc = (lambda c: U[:, 0:NJs, c]) if o == 0 else (lambda c: z1[:, 0:NJs, c])
            zdep = U if o == 0 else z1
            gate = (lambda c: U[:, 0:NJs, 32 + c]) if o == 0 else (lambda c: U[:, 0:NJs, 64 + c])
            dst = z1 if o == 0 else yout
            if o == 0:
                p.cp("dve", zb[:, 0:NJs, :], U[:, 0:NJs, 0:32], [U], [zb])
            else:
                p.cp("dve", zb[:, 0:NJs, :], z1[:, 0:NJs, :], [z1], [zb])
            zbf = zb[:].rearrange("p j c -> p (j c)")
            zrf = zbr[:].rearrange("p j c -> p (j c)")
            for q0 in range(0, NJs * 32, 512):
                qw = min(512, NJs * 32 - q0)
                Yf = psY[(q0 // 512) % 2]
                p.mm(Yf[:, 0:qw], Jb[:], zbf[:, q0:q0 + qw], True, True, [Jb, zb], [Yf])
                p.cp("act", zrf[:, q0:q0 + qw], Yf[:, 0:qw], [Yf], [zbr])
            for c in range(32):
                row = o * 32 + c
                Y = psY[c % 2]
                halves = [list(range(0, NJs)), list(range(-(NJs - 1), 0))]
                nmm = sum(len(h_) for h_ in halves)
                cnt = 0
                for hv in halves:
                    if not hv:
                        continue
                    dmin = hv[0]
                    ndd = len(hv)
                    sb_ = strips[state["si"] % 2]; state["si"] += 1
                    src = bass.AP(tensor=kd_t, offset=row * 2 * Lg + Lg + 128 * dmin - 127, ap=[[1, 128], [1, 128 * ndd]])
                    p.dma("sp", [d_kd], [sb_], sb_[:, 0:128 * ndd], src)
                    for d in hv:
                        j0 = max(0, -d); j1 = min(NJs, NJs - d)
                        p.mm(Y[:, j0 + d:j1 + d], sb_[:, 128 * (d - dmin):128 * (d - dmin) + 128], zbr[:, j0:j1, c],
                             cnt == 0, cnt == nmm - 1, [sb_, zbr], [Y])
                        cnt += 1
                p.stt(tq[:, 0:NJs], zsrc(c), fbs[:, row:row + 1], Y[:, 0:NJs], ALU.mult, ALU.add, [zdep, fbs, Y], [tq])
                p.tt("dve", dst[:, 0:NJs, c], tq[:, 0:NJs], gate(c), ALU.mult, [tq, U], [dst])
        p.dma("sp", [yout], [], yo, yout[:, 0:NJs, :])

    stream(a3c, NJC, LCX, kdc_t, d_kdc, yc)
    stream(a3m, NJ, LM, kdm_t, d_kdm, ym)
    return p.finish()


def hyena_feats(Lg):
    t = (np.arange(Lg, dtype=np.float32) / np.float32(Lg)).astype(np.float32)
    fr = np.arange(1, 17, dtype=np.float32)
    ang = (np.float32(2.0 * math.pi) * t[:, None] * fr[None, :]).astype(np.float32)
    return np.ascontiguousarray(np.concatenate([t[:, None], np.cos(ang), np.sin(ang)], -1).T.astype(np.float32))


def run_ph(pl, pc, lw):
    if "ph" not in _cache:
        _cache["ph"] = build_ph()
        _cache["ftm"] = hyena_feats(L); _cache["ftc"] = hyena_feats(LC)
    maps = []
    for i in range(NCORES):
        cols = np.concatenate([np.arange(32) + 32 * i + 256 * part for part in range(3)])

        def a3(a, Lg):
            am = a[:, cols]
            pad = np.concatenate([np.zeros((1, 96), np.float32), am, np.zeros((1, 96), np.float32)], 0)
            return np.ascontiguousarray(np.stack([pad[k:k + Lg].reshape(Lg // 128, 128, 96).transpose(1, 0, 2) for k in range(3)]))
        w3cols = np.concatenate([np.arange(32) + 32 * i + 256 * (o * 2 + dr) for o in range(2) for dr in range(2)])
        maps.append({"a3m": a3(pl, L), "a3c": a3(pc, LC),
                     "cw": np.ascontiguousarray(lw["hy_conv_w"][:, cols].reshape(1, 288)), "cb": lw["hy_conv_b"][None, cols],
                     "fw1": lw["hy_ffn_w1"], "fb1": lw["hy_ffn_b1"][:, None], "fw2": lw["hy_ffn_w2"], "fb2": lw["hy_ffn_b2"][:, None],
                     "fw3": np.ascontiguousarray(lw["hy_ffn_w3"][:, w3cols]),
                     "dec": np.ascontiguousarray(lw["hy_decay"][:, :, 32 * i:32 * i + 32].reshape(128, 1)),
                     "fbias": np.ascontiguousarray(lw["hy_bias"][:, 32 * i:32 * i + 32].reshape(1, 64)),
                     "ftm": _cache["ftm"], "ftc": _cache["ftc"],
                     "antiid": np.ascontiguousarray(np.eye(128, dtype=np.float32)[::-1])})
    r = run(_cache["ph"], maps)
    yl = np.concatenate([r[i]["ym"].transpose(1, 0, 2).reshape(L, 32) for i in range(NCORES)], 1)
    ycx = np.concatenate([r[i]["yc"].transpose(1, 0, 2).reshape(LC, 32) for i in range(NCORES)], 1)
    return yl, ycx


PARAM_KEYS = ["norm_pre_mix", "norm_post_mix", "norm_pre_mlp", "norm_post_mlp", "w_in", "w_out", "hy_conv_w",
              "hy_conv_b", "hy_ffn_w1", "hy_ffn_b1", "hy_ffn_w2", "hy_ffn_b2", "hy_ffn_w3", "hy_decay", "hy_bias",
              "s5_a_re", "s5_a_im", "s5_log_dt", "s5_b_re", "s5_b_im", "s5_c_re", "s5_c_im", "s5_d", "s5_glu_w",
              "s5_glu_b", "pool_w", "pool_scale", "att_q_norm", "att_k_norm", "mlp_w1", "mlp_w2"]


def kernel(**inputs):
    inputs = {k: np.asarray(v, dtype=np.float32) for k, v in inputs.items()}
    mod = run_pm(inputs)
    xl = np.ascontiguousarray(inputs["x"][0])
    xc = np.ascontiguousarray(inputs["ctx"][0])
    for l in range(DEPTH):
        lw = {k: np.ascontiguousarray(inputs[k][l]) for k in PARAM_KEYS}
        pl, pc = run_pa(xl, xc, mod[l], lw)
        yhy_l, yhy_c = run_ph(pl, pc, lw)
        ys5_l, ys5_c = run_ps(pl, pc, lw)
        att_l, att_c = run_pt(pl, pc)
        xl, xc = run_pc1(xl, xc, yhy_l, yhy_c, ys5_l, ys5_c, pl[:, 1024:1280], pc[:, 1024:1280],
                         att_l, att_c, mod[l], lw)
        xl, xc = run_pc2(xl, xc, mod[l], lw)
    return np.ascontiguousarray(xl[None].astype(np.float32))
```

```python
import math
import numpy as np
from contextlib import ExitStack
import concourse.bass as bass
import concourse.mybir as mybir
from concourse.bass_utils import run_bass_kernel_spmd

F32 = mybir.dt.float32
BF16 = mybir.dt.bfloat16
AF = mybir.ActivationFunctionType
ALU = mybir.AluOpType
AX = mybir.AxisListType

NCORES = 8
D = 1024
L = 16384
LC = 256
DEPTH = 4
TPC = L // NCORES
NT_MAIN = TPC // 128
NT = NT_MAIN + LC // 128
INC = 1792
EPS = 1e-6
NDSEM = 8


class Dep:
    __slots__ = ("w", "r")

    def __init__(self):
        self.w = None
        self.r = {}


class Buf(Dep):
    __slots__ = ("t",)

    def __init__(self, t):
        Dep.__init__(self)
        self.t = t

    def __getitem__(self, k):
        return self.t[k]


class Prog:
    def __init__(self):
        self.nc = bass.Bass("TRN2", target_bir_lowering=False)
        self.es = ExitStack()
        nc = self.nc
        self.E = {"pe": nc.tensor, "dve": nc.vector, "act": nc.scalar, "pool": nc.gpsimd, "sp": nc.sync}
        self.sem = {k: self.es.enter_context(nc.semaphore("s_" + k)) for k in self.E}
        self.cnt = {k: 0 for k in self.E}
        self.waited = {}
        self.dq = {}
        for q in ("sp", "pool", "act"):
            self.dq[q] = [[self.es.enter_context(nc.semaphore("d_%s%d" % (q, i))), 0] for i in range(NDSEM)]
        self.dqi = {q: 0 for q in self.dq}
        self.nalloc = 0

    def inp(self, name, shape, dt=F32):
        return self.nc.dram_tensor(name, list(shape), dt, kind="ExternalInput").ap()

    def outp(self, name, shape, dt=F32):
        return self.nc.dram_tensor(name, list(shape), dt, kind="ExternalOutput").ap()

    def sb(self, shape, dt=F32, name=None):
        self.nalloc += 1
        return self.es.enter_context(self.nc.sbuf_tensor(name or ("t%d" % self.nalloc), list(shape), dt))

    def bsb(self, shape, dt=F32):
        return Buf(self.sb(shape, dt))

    def bps(self, shape, dt=F32):
        return Buf(self.ps(shape, dt))

    def ps(self, shape, dt=F32, name=None):
        self.nalloc += 1
        return self.es.enter_context(self.nc.psum_tensor(name or ("p%d" % self.nalloc), list(shape), dt))

    def _semh(self, key):
        if key[0] == "e":
            return self.sem[key[1]]
        return self.dq[key[1]][key[2]][0]

    def wait(self, eng, tok):
        if tok is None:
            return
        key, val = tok
        if eng == "pe" and key == ("e", "pe"):
            return
        k = (eng, key)
        if self.waited.get(k, 0) >= val:
            return
        self.E[eng].wait_ge(self._semh(key), val)
        self.waited[k] = val

    def _pre(self, eng, reads, writes):
        for d in reads:
            self.wait(eng, d.w)
        for d in writes:
            self.wait(eng, d.w)
            for t in d.r.values():
                self.wait(eng, t)

    def _post(self, tok, reads, writes):
        for d in reads:
            d.r[tok[0]] = tok
        for d in writes:
            d.w = tok
            d.r = {}

    def op(self, eng, reads, writes, fn):
        self._pre(eng, reads, writes)
        ins = fn(self.E[eng])
        self.cnt[eng] += 1
        ins.then_inc(self.sem[eng], 1)
        self._post((("e", eng), self.cnt[eng]), reads, writes)

    def dma(self, q, reads, writes, out, in_, **kw):
        self._pre(q, reads, writes)
        idx = self.dqi[q] % NDSEM
        self.dqi[q] += 1
        slot = self.dq[q][idx]
        key = ("d", q, idx)
        if slot[1] > 0:
            self.wait(q, (key, slot[1]))
        slot[1] += 16
        self.E[q].dma_start(out=out, in_=in_, **kw).then_inc(slot[0], 16)
        self._post((key, slot[1]), reads, writes)

    def finish(self):
        for q in self.dq:
            for i, slot in enumerate(self.dq[q]):
                if slot[1] > 0:
                    self.wait("sp", (("d", q, i), slot[1]))
        for e in self.E:
            if e != "sp" and self.cnt[e] > 0:
                self.wait("sp", (("e", e), self.cnt[e]))
        self.es.close()
        return self.nc

    def mm(self, out, lhsT, rhs, start, stop, reads, writes):
        self.op("pe", reads, writes, lambda e: e.matmul(out, lhsT, rhs, start=start, stop=stop))

    def tr(self, out, in_, ident, reads, writes):
        self.op("pe", reads, writes, lambda e: e.transpose(out, in_, ident))

    def act(self, out, in_, func, reads, writes, bias=None, scale=None, accum_out=None, eng="act"):
        kw = {}
        if bias is not None:
            kw["bias"] = bias
        if scale is not None:
            kw["scale"] = scale
        if accum_out is not None:
            kw["accum_out"] = accum_out
        self.op("act", reads, writes, lambda e: e.activation(out, in_, func, **kw))

    def tt(self, eng, out, in0, in1, op, reads, writes):
        self.op(eng, reads, writes, lambda e: e.tensor_tensor(out, in0, in1, op))

    def ts(self, eng, out, in0, s1, s2, op0, op1, reads, writes):
        if op1 is None:
            self.op(eng, reads, writes, lambda e: e.tensor_scalar(out, in0, s1, None, op0))
        else:
            self.op(eng, reads, writes, lambda e: e.tensor_scalar(out, in0, s1, s2, op0, op1))

    def stt(self, out, in0, scalar, in1, op0, op1, reads, writes):
        self.op("dve", reads, writes, lambda e: e.scalar_tensor_tensor(out, in0, scalar, in1, op0, op1))

    def cp(self, eng, out, in_, reads, writes):
        if eng == "act":
            self.op("act", reads, writes, lambda e: e.copy(out, in_))
        else:
            self.op(eng, reads, writes, lambda e: e.tensor_copy(out, in_))

    def recip(self, out, in_, reads, writes):
        self.op("dve", reads, writes, lambda e: e.reciprocal(out, in_))

    def memset(self, eng, ap, val, writes):
        self.op(eng, [], writes, lambda e: e.memset(ap, val))


def run(nc, in_maps):
    res = run_bass_kernel_spmd(nc, in_maps, core_ids=list(range(NCORES)))
    return res.results


_cache = {}


def build_pm():
    p = Prog()
    cc = p.inp("cc", [128, 8, 2])
    mw = p.inp("mw", [1024, 3072])
    mb = p.inp("mb", [1, 3072])
    out = p.outp("out", [2, 3072])
    cs = p.sb([128, 8, 2]); d_cs = Dep()
    sc = p.sb([128, 8, 2]); d_sc = Dep()
    wsb = p.sb([128, 8, 3072]); d_w = [Dep() for _ in range(8)]
    bsb = p.sb([2, 3072]); d_b = Dep()
    res = p.sb([2, 3072]); d_res = Dep()
    pss = [p.ps([128, 512]) for _ in range(2)]; d_ps = [Dep(), Dep()]
    p.dma("sp", [], [d_cs], cs[:], cc)
    p.dma("sp", [], [d_b], bsb[:], mb.partition_broadcast(2))
    for kc in range(8):
        p.dma("sp" if kc % 2 == 0 else "pool", [], [d_w[kc]], wsb[:, kc, :], mw[kc * 128:(kc + 1) * 128, :])
    p.act(sc[:], cs[:], AF.Silu, [d_cs], [d_sc])
    for n in range(6):
        b = n % 2
        for kc in range(8):
            p.mm(pss[b][0:2, :], sc[:, kc, :], wsb[:, kc, n * 512:(n + 1) * 512], kc == 0, kc == 7,
                 [d_sc, d_w[kc]], [d_ps[b]])
        p.tt("dve", res[:, n * 512:(n + 1) * 512], pss[b][0:2, :], bsb[:, n * 512:(n + 1) * 512], ALU.add,
             [d_ps[b], d_b], [d_res])
    p.dma("sp", [d_res], [], out, res[:])
    return p.finish()


def run_pm(inputs):
    if "pm" not in _cache:
        _cache["pm"] = build_pm()
    c2 = np.stack([inputs["c"][0], inputs["c_ctx"]], axis=-1)
    cc = np.ascontiguousarray(c2.reshape(8, 128, 2).transpose(1, 0, 2))
    maps = []
    for i in range(NCORES):
        l, h = i // 2, i % 2
        maps.append({"cc": cc,
                     "mw": np.ascontiguousarray(inputs["mod_w"][l][:, h * 3072:(h + 1) * 3072]),
                     "mb": np.ascontiguousarray(inputs["mod_b"][l][None, h * 3072:(h + 1) * 3072])})
    r = run(_cache["pm"], maps)
    mod = np.zeros((DEPTH, 2, 6144), np.float32)
    for i in range(NCORES):
        l, h = i // 2, i % 2
        mod[l, :, h * 3072:(h + 1) * 3072] = r[i]["out"]
    return mod


def build_pa(NT=NT, NT_MAIN=NT_MAIN, NB=2):
    p = Prog()
    x = p.inp("x", [NT * 128, D])
    modr = p.inp("modr", [4, D])
    gpre = p.inp("gpre", [1, D])
    win = p.inp("win", [D, INC])
    qkg = p.inp("qkg", [1, 384])
    rc = p.inp("rc", [NT * 128, 192])
    rs = p.inp("rs", [NT * 128, 192])
    ident = p.inp("ident", [128, 128])
    out = p.outp("proj", [NT * 128, INC])

    idb = p.bsb([128, 128], BF16)
    p.dma("pool", [], [idb], idb[:], ident)
    wsb = p.bsb([128, 8, INC], BF16)
    for kc in range(8):
        p.dma("pool", [], [wsb], wsb[:, kc, :], win[kc * 128:(kc + 1) * 128, :])
    gb = p.bsb([128, D])
    p.dma("sp", [], [gb], gb[:], gpre.partition_broadcast(128))
    mods = []
    for i in range(4):
        m = p.bsb([128, D])
        p.dma("sp", [], [m], m[:], modr[i:i + 1, :].partition_broadcast(128))
        mods.append(m)
    G = []
    for s in range(2):
        g = p.bsb([128, D])
        p.stt(g[:], mods[2 * s + 1][:], 1.0, gb[:], ALU.add, ALU.mult, [mods[2 * s + 1], gb], [g])
        G.append(g)
    SH = [mods[0], mods[2]]
    qkgb = p.bsb([128, 384])
    p.dma("sp", [], [qkgb], qkgb[:], qkg.partition_broadcast(128))

    xs = [p.bsb([128, D]) for _ in range(NB)]
    junk = p.bsb([128, D], BF16)
    ss = [p.bsb([128, 1]) for _ in range(NB)]
    rt = [p.bsb([128, 1]) for _ in range(NB)]
    rstd = [p.bsb([128, 1]) for _ in range(NB)]
    t1 = p.bsb([128, D])
    h = [p.bsb([128, D], BF16) for _ in range(NB)]
    psT = p.bps([128, 8, 128], BF16)
    hT = [p.bsb([128, 8, 128], BF16) for _ in range(NB)]
    pso = [p.bps([128, 512]) for _ in range(4)]
    osb = [p.bsb([128, INC]) for _ in range(NB)]
    sqb = p.bsb([128, 384])
    ssq = p.bsb([128, 6]); rt6 = p.bsb([128, 6]); rs6 = p.bsb([128, 6])
    qn = p.bsb([128, 384])
    tA = p.bsb([128, 192]); tB = p.bsb([128, 192])
    rcs = [p.bsb([128, 192]) for _ in range(NB)]
    rss = [p.bsb([128, 192]) for _ in range(NB)]
    chunks = [(0, 512), (512, 512), (1024, 256), (1280, 512)]

    for i in range(NT):
        b = i % NB
        s = 0 if i < NT_MAIN else 1
        rows = slice(i * 128, (i + 1) * 128)
        p.dma("sp", [], [xs[b]], xs[b][:], x[rows, :])
        p.dma("sp", [], [rcs[b]], rcs[b][:], rc[rows, :])
        p.dma("sp", [], [rss[b]], rss[b][:], rs[rows, :])
        p.act(junk[:], xs[b][:], AF.Square, [xs[b]], [junk, ss[b]], accum_out=ss[b][:])
        p.act(rt[b][:], ss[b][:], AF.Sqrt, [ss[b]], [rt[b]], bias=EPS, scale=1.0 / D)
        p.recip(rstd[b][:], rt[b][:], [rt[b]], [rstd[b]])
        p.stt(t1[:], xs[b][:], rstd[b][:, 0:1], G[s][:], ALU.mult, ALU.mult, [xs[b], rstd[b], G[s]], [t1])
        p.tt("dve", h[b][:], t1[:], SH[s][:], ALU.add, [t1, SH[s]], [h[b]])
        for kc in range(8):
            p.tr(psT[:, kc, :], h[b][:, kc * 128:(kc + 1) * 128], idb[:], [h[b], idb], [psT])
        p.cp("act", hT[b][:], psT[:], [psT], [hT[b]])
        for ci, (c0, w) in enumerate(chunks):
            for kc in range(8):
                p.mm(pso[ci][:, 0:w], hT[b][:, kc, :], wsb[:, kc, c0:c0 + w], kc == 0, kc == 7,
                     [hT[b], wsb], [pso[ci]])
        o = osb[b]
        p.cp("act", o[:, 0:512], pso[0][:, :], [pso[0]], [o])
        p.cp("dve", o[:, 512:1024], pso[1][:, :], [pso[1]], [o])
        p.cp("act", o[:, 1024:1280], pso[2][:, 0:256], [pso[2]], [o])
        p.cp("act", o[:, 1664:1792], pso[3][:, 384:512], [pso[3]], [o])
        p.act(sqb[:], pso[3][:, 0:384], AF.Square, [pso[3]], [sqb])
        p.op("dve", [sqb], [ssq], lambda e: e.tensor_reduce(
            ssq[:], sqb[:].rearrange("p (h d) -> p h d", d=64), AX.X, ALU.add))
        p.act(rt6[:], ssq[:], AF.Sqrt, [ssq], [rt6], bias=EPS, scale=1.0 / 64)
        p.recip(rs6[:], rt6[:], [rt6], [rs6])
        for hh in range(6):
            cs = slice(hh * 64, (hh + 1) * 64)
            p.stt(qn[:, cs], pso[3][:, cs], rs6[:, hh:hh + 1], qkgb[:, cs], ALU.mult, ALU.mult,
                  [pso[3], rs6, qkgb], [qn])
        qv = qn[:].rearrange("p (j two) -> p j two", two=2)
        ov = o[:, 1280:1664].rearrange("p (j two) -> p j two", two=2)
        x0, x1 = qv[:, :, 0], qv[:, :, 1]
        p.tt("dve", tA[:], x0, rcs[b][:], ALU.mult, [qn, rcs[b]], [tA])
        p.tt("dve", tB[:], x1, rss[b][:], ALU.mult, [qn, rss[b]], [tB])
        p.tt("dve", ov[:, :, 0], tA[:], tB[:], ALU.subtract, [tA, tB], [o])
        p.tt("dve", tA[:], x0, rss[b][:], ALU.mult, [qn, rss[b]], [tA])
        p.tt("dve", tB[:], x1, rcs[b][:], ALU.mult, [qn, rcs[b]], [tB])
        p.tt("dve", ov[:, :, 1], tA[:], tB[:], ALU.add, [tA, tB], [o])
        p.dma("sp", [o], [], out[rows, :], o[:])
    return p.finish()


def rope_tables():
    inv = (10000.0 ** (-np.arange(0, 32, 2, dtype=np.float32) / 32)).astype(np.float32)
    t = np.arange(L)
    rows = (t // 64).astype(np.float32)
    cols = (t % 64).astype(np.float32)
    ang = np.concatenate([rows[:, None] * inv[None, :], cols[:, None] * inv[None, :]], axis=-1).astype(np.float32)
    cos = np.cos(ang).astype(np.float32)
    sin = np.sin(ang).astype(np.float32)
    cos = np.concatenate([cos, np.ones((LC, 32), np.float32)], 0)
    sin = np.concatenate([sin, np.zeros((LC, 32), np.float32)], 0)
    return np.tile(cos, (1, 6)), np.tile(sin, (1, 6))


def tok_shard(main, ctx, i):
    return np.ascontiguousarray(np.concatenate([main[i * TPC:(i + 1) * TPC], ctx], axis=0))


def run_pa(xl, xc, mod_l, lw):
    if "pa" not in _cache:
        _cache["pa"] = build_pa()
        _cache["rope"] = rope_tables()
    cos, sin = _cache["rope"]
    modr = np.ascontiguousarray(np.stack([mod_l[0, 0:D], mod_l[0, D:2 * D], mod_l[1, 0:D], mod_l[1, D:2 * D]]))
    qkg = np.concatenate([np.tile(lw["att_q_norm"], 4), np.tile(lw["att_k_norm"], 2)])[None, :]
    maps = []
    for i in range(NCORES):
        maps.append({"x": tok_shard(xl, xc, i), "modr": modr, "gpre": lw["norm_pre_mix"][None, :],
                     "win": lw["w_in"], "qkg": np.ascontiguousarray(qkg),
                     "rc": tok_shard(cos[:L], cos[L:], i), "rs": tok_shard(sin[:L], sin[L:], i),
                     "ident": np.eye(128, dtype=np.float32)})
    r = run(_cache["pa"], maps)
    pl = np.concatenate([r[i]["proj"][:TPC] for i in range(NCORES)], 0)
    pc = r[0]["proj"][TPC:]
    return pl, pc


def bcast_load(p, q, src_row, n=128, width=D):
    b = p.bsb([n, width])
    p.dma(q, [], [b], b[:], src_row.partition_broadcast(n))
    return b


def rms_residual(p, xin, o_ps, GG, xout, ss, rt, rstd, junk, t1, width=D):
    nb = len(o_ps)
    parts = [p.bsb([128, 1]) for _ in range(0)]
    for j, ps in enumerate(o_ps):
        p.act(junk[:, j * 512:(j + 1) * 512], ps[:, :], AF.Square, [ps], [junk, ss[j]], accum_out=ss[j][:])
    if nb == 2:
        p.tt("dve", ss[0][:], ss[0][:], ss[1][:], ALU.add, [ss[0], ss[1]], [ss[0]])
    p.act(rt[:], ss[0][:], AF.Sqrt, [ss[0]], [rt], bias=EPS, scale=1.0 / width)
    p.recip(rstd[:], rt[:], [rt], [rstd])
    for j, ps in enumerate(o_ps):
        cs = slice(j * 512, (j + 1) * 512)
        p.stt(t1[:, cs], ps[:, :], rstd[:, 0:1], GG[:, cs], ALU.mult, ALU.mult, [ps, rstd, GG], [t1])
    p.tt("dve", xout[:], t1[:], xin[:], ALU.add, [t1, xin], [xout])


def build_pc1(NT=NT, NT_MAIN=NT_MAIN):
    p = Prog()
    NTOK = NT * 128
    NM = NT_MAIN * 128
    NCX = NTOK - NM
    WP = NM + 16 + NCX + 16
    x = p.inp("x", [NTOK, D])
    yhy = p.inp("yhy", [256, NTOK])
    ys5 = p.inp("ys5", [256, NTOK])
    pp = p.inp("pp", [256, WP])
    att = p.inp("att", [256, NTOK])
    wout = p.inp("wout", [D, D])
    gluw = p.inp("gluw", [256, 256])
    glub = p.inp("glub", [128, 2])
    poolw = p.inp("poolw", [2, 128, 128])
    pools = p.inp("pools", [128, 2])
    icnt = p.inp("icnt", [128, 2, WP])
    gate = p.inp("gate", [2, D])
    gpost = p.inp("gpost", [1, D])
    xo = p.outp("xo", [NTOK, D])

    wsb = p.bsb([128, 8, D], BF16)
    for kc in range(8):
        p.dma("pool", [], [wsb], wsb[:, kc, :], wout[kc * 128:(kc + 1) * 128, :])
    gw = p.bsb([128, 2, 256], BF16)
    for kc in range(2):
        p.dma("pool", [], [gw], gw[:, kc, :], gluw[kc * 128:(kc + 1) * 128, :])
    pw = p.bsb([128, 2, 128], BF16)
    for kc in range(2):
        p.dma("pool", [], [pw], pw[:, kc, :], poolw[kc])
    gbias = p.bsb([128, 2]); p.dma("sp", [], [gbias], gbias[:], glub)
    psc = p.bsb([128, 2]); p.dma("sp", [], [psc], psc[:], pools)
    gpb = bcast_load(p, "sp", gpost)
    GG = []
    for s in range(2):
        gt = bcast_load(p, "sp", gate[s:s + 1, :])
        gg = p.bsb([128, D])
        p.tt("dve", gg[:], gt[:], gpb[:], ALU.mult, [gt, gpb], [gg])
        GG.append(gg)

    mT = p.bsb([128, 8, NTOK], BF16)
    SA = p.bsb([128, 2, WP]); SB = p.bsb([128, 2, WP]); SC = p.bsb([128, 2, WP])
    SD = p.bsb([128, 2, WP]); SE = p.bsb([128, 2, WP])
    HB = p.bsb([128, 2, WP], BF16)
    for (src, k0, st) in ((yhy, 0, SD), (att, 6, SE)):
        for kc in range(2):
            p.dma("sp", [], [st], st[:, kc, 0:NTOK], src[kc * 128:(kc + 1) * 128, :])
        p.cp("dve", mT[:, k0:k0 + 2, :], st[:, :, 0:NTOK], [st], [mT])
    yv = SA; y2 = SB; gf = SC; gbf = HB
    for kc in range(2):
        p.dma("sp", [], [yv], yv[:, kc, 0:NTOK], ys5[kc * 128:(kc + 1) * 128, :])
    N_ = slice(0, NTOK)
    p.tt("dve", y2[:, :, N_], yv[:, :, N_], yv[:, :, N_], ALU.mult, [yv], [y2])
    p.ts("dve", y2[:, :, N_], y2[:, :, N_], 0.044715, 1.0, ALU.mult, ALU.add, [y2], [y2])
    p.tt("dve", y2[:, :, N_], y2[:, :, N_], yv[:, :, N_], ALU.mult, [y2, yv], [y2])
    p.act(y2[:, :, N_], y2[:, :, N_], AF.Sigmoid, [y2], [y2], scale=2.0 * math.sqrt(2.0 / math.pi))
    p.tt("dve", gf[:, :, N_], y2[:, :, N_], yv[:, :, N_], ALU.mult, [y2, yv], [gf])
    p.cp("dve", gbf[:, :, N_], gf[:, :, N_], [gf], [gbf])
    psg = [p.bps([128, 512]) for _ in range(2)]
    sg = p.bsb([128, 512])
    ci = 0
    for t0 in range(0, NTOK, 512):
        w = min(512, NTOK - t0)
        for mc in range(2):
            ps = psg[ci % 2]; ci += 1
            for kc in range(2):
                p.mm(ps[:, 0:w], gw[:, kc, mc * 128:(mc + 1) * 128], gbf[:, kc, t0:t0 + w], kc == 0, kc == 1,
                     [gw, gbf], [ps])
            p.act(sg[:, 0:w], ps[:, 0:w], AF.Sigmoid, [ps, gbias], [sg], bias=gbias[:, mc:mc + 1])
            p.tt("dve", mT[:, 2 + mc, t0:t0 + w], sg[:, 0:w], gf[:, mc, t0:t0 + w], ALU.mult, [sg, gf], [mT])
    pv = SA; ic = SB; pm = SC; pmb = HB
    for kc in range(2):
        p.dma("sp", [], [pv], pv[:, kc, :], pp[kc * 128:(kc + 1) * 128, :])
    p.dma("sp", [], [ic], ic[:], icnt)
    p.tt("dve", SD[:, :, 1:WP], pv[:, :, 0:WP - 1], pv[:, :, 1:WP], ALU.add, [pv], [SD])
    p.tt("dve", pm[0:64, 0, 1:WP], SD[0:64, 0, 1:WP], ic[0:64, 0, 1:WP], ALU.mult, [SD, ic], [pm])
    p.tt("dve", SE[:, :, 2:WP - 1], SD[:, :, 1:WP - 2], SD[:, :, 3:WP], ALU.add, [SD], [SE])
    p.tt("dve", pm[64:128, 0, 2:WP - 1], SE[64:128, 0, 2:WP - 1], ic[64:128, 0, 2:WP - 1], ALU.mult, [SE, ic], [pm])
    p.tt("dve", SD[:, :, 4:WP - 3], SE[:, :, 2:WP - 5], SE[:, :, 6:WP - 1], ALU.add, [SE], [SD])
    p.tt("dve", pm[0:64, 1, 4:WP - 3], SD[0:64, 1, 4:WP - 3], ic[0:64, 1, 4:WP - 3], ALU.mult, [SD, ic], [pm])
    p.tt("dve", SE[:, :, 8:WP - 7], SD[:, :, 4:WP - 11], SD[:, :, 12:WP - 3], ALU.add, [SD], [SE])
    p.tt("dve", pm[64:128, 1, 8:WP - 7], SE[64:128, 1, 8:WP - 7], ic[64:128, 1, 8:WP - 7], ALU.mult, [SE, ic], [pm])
    V_ = slice(8, WP - 8)
    p.tt("dve", pmb[:, :, V_], pm[:, :, V_], pv[:, :, V_], ALU.subtract, [pm, pv], [pmb])
    segs = [(8, 0, NM), (NM + 16 + 8, NM, NCX)]
    for (so, do, cnt) in segs:
        for t0 in range(0, cnt, 512):
            w = min(512, cnt - t0)
            for kc in range(2):
                ps = psg[ci % 2]; ci += 1
                p.mm(ps[:, 0:w], pw[:, kc, :], pmb[:, kc, so + t0:so + t0 + w], True, True, [pw, pmb], [ps])
                p.ts("dve", mT[:, 4 + kc, do + t0:do + t0 + w], ps[:, 0:w], psc[:, kc:kc + 1], None, ALU.mult, None,
                     [ps, psc], [mT])
    pso = [p.bps([128, 512]) for _ in range(2)]
    xs = [p.bsb([128, D]) for _ in range(2)]
    xn = [p.bsb([128, D]) for _ in range(2)]
    junk = p.bsb([128, D], BF16)
    ss = [p.bsb([128, 1]) for _ in range(2)]
    rt = p.bsb([128, 1]); rstd = p.bsb([128, 1]); t1 = p.bsb([128, D])
    for i in range(NT):
        b = i % 2
        s = 0 if i < NT_MAIN else 1
        rows = slice(i * 128, (i + 1) * 128)
        p.dma("sp", [], [xs[b]], xs[b][:], x[rows, :])
        for j in range(2):
            for kc in range(8):
                p.mm(pso[j][:, :], mT[:, kc, i * 128:(i + 1) * 128], wsb[:, kc, j * 512:(j + 1) * 512], kc == 0, kc == 7,
                     [mT, wsb], [pso[j]])
        rms_residual(p, xs[b], pso, GG[s], xn[b], ss, rt, rstd, junk, t1)
        p.dma("sp", [xn[b]], [], xo[rows, :], xn[b][:])
    return p.finish()


def pool_inv_counts(Lseq):
    t = np.arange(Lseq)
    out = []
    for w in (2, 4, 8, 16):
        lo = np.clip(t - w // 2, 0, Lseq); hi = np.clip(t + w // 2, 0, Lseq)
        out.append((1.0 / (hi - lo)).astype(np.float32))
    return np.stack(out)


def pad_seg(a, lo, hi):
    C, Ls = a.shape
    out = np.zeros((C, hi - lo), a.dtype)
    s0, s1 = max(lo, 0), min(hi, Ls)
    out[:, s0 - lo:s1 - lo] = a[:, s0:s1]
    return out


def run_pc1(xl, xc, yhy_l, yhy_c, ys5_l, ys5_c, p_l, p_c, att_l, att_c, mod_l, lw):
    if "pc1" not in _cache:
        _cache["pc1"] = build_pc1()
        _cache["icm"] = pool_inv_counts(L); _cache["icc"] = pool_inv_counts(LC)
    icm, icc = _cache["icm"], _cache["icc"]
    poolw = np.zeros((2, 128, 128), np.float32)
    for g in range(4):
        poolw[g // 2, (g % 2) * 64:(g % 2) * 64 + 64, (g % 2) * 64:(g % 2) * 64 + 64] = lw["pool_w"][g]
    gate = np.ascontiguousarray(np.stack([mod_l[0, 2 * D:3 * D], mod_l[1, 2 * D:3 * D]]))
    maps = []
    for i in range(NCORES):
        lo, hi = i * TPC, (i + 1) * TPC
        cat = lambda m, c: np.ascontiguousarray(np.concatenate([m[lo:hi], c], 0).T)
        pp = np.concatenate([pad_seg(p_l.T, lo - 8, hi + 8), pad_seg(p_c.T, -8, LC + 8)], 1)
        ic = np.concatenate([pad_seg(icm, lo - 8, hi + 8), pad_seg(icc, -8, LC + 8)], 1)
        icnt = np.repeat(ic.reshape(2, 2, 1, -1), 64, axis=2).reshape(2, 128, -1).transpose(1, 0, 2)
        maps.append({"x": tok_shard(xl, xc, i), "yhy": cat(yhy_l, yhy_c), "ys5": cat(ys5_l, ys5_c),
                     "pp": np.ascontiguousarray(pp), "att": cat(att_l, att_c), "wout": lw["w_out"],
                     "gluw": lw["s5_glu_w"], "glub": np.ascontiguousarray(lw["s5_glu_b"].reshape(2, 128).T),
                     "poolw": poolw, "pools": np.ascontiguousarray(lw["pool_scale"].reshape(2, 128).T),
                     "icnt": np.ascontiguousarray(icnt), "gate": gate, "gpost": lw["norm_post_mix"][None, :]})
    r = run(_cache["pc1"], maps)
    return (np.concatenate([r[i]["xo"][:TPC] for i in range(NCORES)], 0), r[0]["xo"][TPC:])


def build_pc2(NT=NT, NT_MAIN=NT_MAIN, GT=2):
    p = Prog()
    NTOK = NT * 128
    x = p.inp("x", [NTOK, D])
    w1 = p.inp("w1", [D, 4 * D])
    w2 = p.inp("w2", [4 * D, D])
    modr = p.inp("modr", [6, D])
    gpre = p.inp("gpre", [1, D])
    gpost = p.inp("gpost", [1, D])
    ident = p.inp("ident", [128, 128])
    xo = p.outp("xo", [NTOK, D])
    idb = p.bsb([128, 128], BF16)
    p.dma("pool", [], [idb], idb[:], ident)
    w1sb = p.bsb([128, 8, 4 * D], BF16)
    for kc in range(8):
        for hh in range(2):
            p.dma("pool", [], [w1sb], w1sb[:, kc, hh * 2048:(hh + 1) * 2048],
                  w1[kc * 128:(kc + 1) * 128, hh * 2048:(hh + 1) * 2048])
    w2sb = p.bsb([128, 32, D], BF16)
    for fc in range(32):
        p.dma("pool", [], [w2sb], w2sb[:, fc, :], w2[fc * 128:(fc + 1) * 128, :])
    tmpc = p.bsb([128, D])
    Gc = p.bsb([128, D]); SHc = p.bsb([128, D]); GGc = p.bsb([128, D])
    W = GT * 128
    xs = p.bsb([128, GT, D])
    hT = p.bsb([128, 8, W], BF16)
    uT = p.bsb([128, 32, W], BF16)
    junk = p.bsb([128, D], BF16)
    ss = [p.bsb([128, 1]) for _ in range(2)]
    rt = p.bsb([128, 1]); rstd = p.bsb([128, 1]); t1 = p.bsb([128, D])
    hb = p.bsb([128, D], BF16)
    psT = p.bps([128, 8, 128], BF16)
    psu = [p.bps([128, 512]) for _ in range(2)]
    pso = [p.bps([128, 512]) for _ in range(2)]
    rl = p.bsb([128, W])
    xn = [p.bsb([128, D]) for _ in range(2)]
    cur_stream = -1
    tiles = list(range(NT))
    groups = []
    i = 0
    while i < NT:
        lim = NT_MAIN if i < NT_MAIN else NT
        g = tiles[i:min(i + GT, lim)]
        groups.append(g)
        i += len(g)
    for g in groups:
        s = 0 if g[0] < NT_MAIN else 1
        if s != cur_stream:
            cur_stream = s
            p.dma("sp", [], [tmpc], tmpc[:], gpre.partition_broadcast(128))
            p.dma("sp", [], [Gc], Gc[:], modr[3 * s + 1:3 * s + 2, :].partition_broadcast(128))
            p.stt(Gc[:], Gc[:], 1.0, tmpc[:], ALU.add, ALU.mult, [Gc, tmpc], [Gc])
            p.dma("sp", [], [SHc], SHc[:], modr[3 * s:3 * s + 1, :].partition_broadcast(128))
            p.dma("sp", [], [tmpc], tmpc[:], gpost.partition_broadcast(128))
            p.dma("sp", [], [GGc], GGc[:], modr[3 * s + 2:3 * s + 3, :].partition_broadcast(128))
            p.tt("dve", GGc[:], GGc[:], tmpc[:], ALU.mult, [GGc, tmpc], [GGc])
        w = len(g) * 128
        for gi, ti in enumerate(g):
            rows = slice(ti * 128, (ti + 1) * 128)
            p.dma("sp", [], [xs], xs[:, gi, :], x[rows, :])
            p.act(junk[:], xs[:, gi, :], AF.Square, [xs], [junk, ss[0]], accum_out=ss[0][:])
            p.act(rt[:], ss[0][:], AF.Sqrt, [ss[0]], [rt], bias=EPS, scale=1.0 / D)
            p.recip(rstd[:], rt[:], [rt], [rstd])
            p.stt(t1[:], xs[:, gi, :], rstd[:, 0:1], Gc[:], ALU.mult, ALU.mult, [xs, rstd, Gc], [t1])
            p.tt("dve", hb[:], t1[:], SHc[:], ALU.add, [t1, SHc], [hb])
            for kc in range(8):
                p.tr(psT[:, kc, :], hb[:, kc * 128:(kc + 1) * 128], idb[:], [hb, idb], [psT])
            p.cp("act", hT[:, :, gi * 128:(gi + 1) * 128], psT[:], [psT], [hT])
        for fc in range(32):
            ps = psu[fc % 2]
            for kc in range(8):
                p.mm(ps[:, 0:w], w1sb[:, kc, fc * 128:(fc + 1) * 128], hT[:, kc, 0:w], kc == 0, kc == 7,
                     [w1sb, hT], [ps])
            p.act(rl[:, 0:w], ps[:, 0:w], AF.Relu, [ps], [rl])
            p.tt("dve", uT[:, fc, 0:w], rl[:, 0:w], rl[:, 0:w], ALU.mult, [rl], [uT])
        for gi, ti in enumerate(g):
            rows = slice(ti * 128, (ti + 1) * 128)
            b = ti % 2
            for j in range(2):
                for fc in range(32):
                    p.mm(pso[j][:, :], uT[:, fc, gi * 128:(gi + 1) * 128], w2sb[:, fc, j * 512:(j + 1) * 512],
                         fc == 0, fc == 31, [uT, w2sb], [pso[j]])
            rms_residual(p, _View(xs, lambda t, gi=gi: t[:, gi, :]), pso, GGc, xn[b], ss, rt, rstd, junk, t1)
            p.dma("sp", [xn[b]], [], xo[rows, :], xn[b][:])
    return p.finish()


class _View:
    def __init__(self, buf, fn):
        self._b = buf
        self._fn = fn

    def __getitem__(self, k):
        return self._fn(self._b.t)

    @property
    def w(self):
        return self._b.w

    @w.setter
    def w(self, v):
        self._b.w = v

    @property
    def r(self):
        return self._b.r

    @r.setter
    def r(self, v):
        self._b.r = v


def run_pc2(xl, xc, mod_l, lw):
    if "pc2" not in _cache:
        _cache["pc2"] = build_pc2()
    modr = np.ascontiguousarray(np.stack([mod_l[s, k * D:(k + 1) * D] for s in range(2) for k in (3, 4, 5)]))
    maps = []
    for i in range(NCORES):
        maps.append({"x": tok_shard(xl, xc, i), "w1": lw["mlp_w1"], "w2": lw["mlp_w2"], "modr": modr,
                     "gpre": lw["norm_pre_mlp"][None, :], "gpost": lw["norm_post_mlp"][None, :],
                     "ident": np.eye(128, dtype=np.float32)})
    r = run(_cache["pc2"], maps)
    return (np.concatenate([r[i]["xo"][:TPC] for i in range(NCORES)], 0), r[0]["xo"][TPC:])


def build_pt(NQM=TPC, NQC=LC, NKM=L, NKC=LC):
    p = Prog()
    NQ = NQM + NQC
    NK = NKM + NKC
    NKT = NK // 128
    qT = p.inp("qT", [64, 4, NQ])
    kT = p.inp("kT", [64, 2, NK])
    vt = p.inp("vt", [128, NKT, 2, 64])
    oT = p.outp("oT", [256, NQ])
    qsb = p.bsb([64, 4, NQ], BF16)
    for h in range(4):
        for c0 in range(0, NQ, 2048):
            w = min(2048, NQ - c0)
            p.dma("pool", [], [qsb], qsb[:, h, c0:c0 + w], qT[:, h, c0:c0 + w])
    ksb = p.bsb([64, 2, NK], BF16)
    for kv in range(2):
        for c0 in range(0, NK, 2048):
            w = min(2048, NK - c0)
            p.dma("pool", [], [ksb], ksb[:, kv, c0:c0 + w], kT[:, kv, c0:c0 + w])
    v1 = p.bsb([128, NKT, 2, 65], BF16)
    p.memset("dve", v1[:], 1.0, [v1])
    vst = [p.bsb([128, 13, 2, 64]) for _ in range(2)]
    for gi, k0 in enumerate(range(0, NKT, 13)):
        n = min(13, NKT - k0)
        st = vst[gi % 2]
        p.dma("sp", [], [st], st[:, 0:n], vt[:, k0:k0 + n])
        p.cp("dve", v1[:, k0:k0 + n, :, 0:64], st[:, 0:n], [st], [v1])
    ones = p.bsb([128, 64]); p.memset("dve", ones[:], 1.0, [ones])
    pss = [p.bps([128, 512]) for _ in range(3)]
    pso = [p.bps([128, 512]) for _ in range(2)]
    psb = p.bps([128, 512])
    pT = [p.bsb([128, 512], BF16) for _ in range(3)]
    rs = p.bsb([128, 512]); oc = p.bsb([128, 512])
    on = [p.bsb([64, 512]) for _ in range(2)]
    jobs = []
    for h in range(4):
        for q0 in range(0, NQM, 512):
            jobs.append((h, q0, min(512, NQM - q0), list(range(NKT))))
        jobs.append((h, NQM, NQC, list(range(NKM // 128, NKT))))
    iters = []
    for ji, (h, q0, w, kts) in enumerate(jobs):
        for n, kt in enumerate(kts):
            iters.append((ji, h, q0, w, kt, n == 0, n == len(kts) - 1))
    LA = 2
    NI = len(iters)
    for g in range(NI + LA):
        if g < NI:
            ji, h, q0, w, kt, first, last = iters[g]
            ps = pss[g % 3]
            p.mm(ps[:, 0:w], ksb[:, h // 2, kt * 128:(kt + 1) * 128], qsb[:, h, q0:q0 + w], True, True,
                 [ksb, qsb], [ps])
        e = g - LA
        if e >= 0:
            ji, h, q0, w, kt, first, last = iters[e]
            ps = pss[e % 3]; pt = pT[e % 3]; po = pso[ji % 2]
            p.act(pt[:, 0:w], ps[:, 0:w], AF.Exp, [ps], [pt], scale=0.125)
            p.mm(po[0:65, 0:w], v1[:, kt, h // 2, :], pt[:, 0:w], first, last, [v1, pt], [po])
            if last:
                p.recip(rs[64:65, 0:w], po[64:65, 0:w], [po], [rs])
                p.cp("act", oc[0:64, 0:w], po[0:64, 0:w], [po], [oc])
                p.mm(psb[0:64, 0:w], ones[64:65, 0:64], rs[64:65, 0:w], True, True, [ones, rs], [psb])
                o = on[ji % 2]
                p.tt("dve", o[:, 0:w], oc[0:64, 0:w], psb[0:64, 0:w], ALU.mult, [oc, psb], [o])
                p.dma("sp", [o], [], oT[h * 64:(h + 1) * 64, q0:q0 + w], o[:, 0:w])
    return p.finish()


def run_pt(pl, pc):
    if "pt" not in _cache:
        _cache["pt"] = build_pt()
    kall = np.concatenate([pl[:, 1536:1664], pc[:, 1536:1664]], 0)
    vall = np.concatenate([pl[:, 1664:1792], pc[:, 1664:1792]], 0)
    NK = kall.shape[0]
    kT = np.ascontiguousarray(kall.reshape(NK, 2, 64).transpose(2, 1, 0))
    vt = np.ascontiguousarray(vall.reshape(NK // 128, 128, 2, 64).transpose(1, 0, 2, 3))
    maps = []
    for i in range(NCORES):
        q = np.concatenate([pl[i * TPC:(i + 1) * TPC, 1280:1536], pc[:, 1280:1536]], 0)
        maps.append({"qT": np.ascontiguousarray(q.reshape(-1, 4, 64).transpose(2, 1, 0)), "kT": kT, "vt": vt})
    r = run(_cache["pt"], maps)
    att_l = np.concatenate([r[i]["oT"][:, :TPC].T for i in range(NCORES)], 0)
    att_c = r[0]["oT"][:, TPC:].T
    return att_l, att_c


MAGIC = 12582912.0
TWO_PI = 2.0 * math.pi
PI_LO = 3.1415925


def sin_rr(p, out, in_, phase, tmp, reads, writes):
    p.ts("dve", tmp, in_, 1.0 / TWO_PI, phase / TWO_PI, ALU.mult, ALU.add, reads, writes)
    p.ts("dve", tmp, tmp, MAGIC, None, ALU.add, None, writes, writes)
    p.ts("dve", tmp, tmp, -MAGIC, None, ALU.add, None, writes, writes)
    p.stt(tmp, tmp, -TWO_PI, in_, ALU.mult, ALU.add, reads + writes, writes)
    p.ts("dve", tmp, tmp, phase, PI_LO, ALU.add, ALU.min, writes, writes)
    p.ts("dve", tmp, tmp, -PI_LO, None, ALU.max, None, writes, writes)
    return tmp


def build_ps(NSC=LC, NSM=L, T=512):
    p = Prog()
    NS = NSC + NSM
    u_f = p.inp("uf", [32, NS])
    u_r = p.inp("ur", [32, NS])
    are = p.inp("are", [2, 128, 1]); aim = p.inp("aim", [2, 128, 1]); ldt = p.inp("ldt", [2, 128, 1])
    bre = p.inp("bre", [2, 128, 16]); bim = p.inp("bim", [2, 128, 16])
    cre = p.inp("cre", [2, 128, 32]); cim = p.inp("cim", [2, 128, 32])
    dd = p.inp("dd", [32, 32])
    jt = p.inp("jt", [128, T + 1])
    ident = p.inp("ident", [128, 128])
    yo = p.outp("y", [32, NS])

    ub = [p.bsb([32, NS], BF16), p.bsb([32, NS], BF16)]
    for d, src in enumerate((u_f, u_r)):
        for c0 in range(0, NS, 2048):
            w = min(2048, NS - c0)
            p.dma("pool", [], [ub[d]], ub[d][:, c0:c0 + w], src[:, c0:c0 + w])
    yf = p.bsb([32, NS])
    idf = p.bsb([128, 128]); p.dma("sp", [], [idf], idf[:], ident)
    jts = p.bsb([128, T + 1]); p.dma("sp", [], [jts], jts[:], jt)
    ddb = p.bsb([32, 32], BF16); p.dma("pool", [], [ddb], ddb[:], dd)
    pst = p.bps([128, 512])
    psA = [p.bps([128, 512]) for _ in range(2)]
    psB = [p.bps([128, 512]) for _ in range(2)]
    psY = [p.bps([128, 512]) for _ in range(2)]

    def small(n=1):
        return p.bsb([128, n])

    chunks = [(0, NSC)] + [(NSC + k * T, T) for k in range(NSM // T)]
    m1 = p.bsb([128, T]); m2 = p.bsb([128, T]); m3 = p.bsb([128, T]); m4 = p.bsb([128, T])
    btr = p.bsb([128, T]); bti = p.bsb([128, T])
    gr = [p.bsb([128, T]) for _ in range(2)]; gi = [p.bsb([128, T]) for _ in range(2)]
    hr = [p.bsb([128, T], BF16) for _ in range(2)]; hi = [p.bsb([128, T], BF16) for _ in range(2)]
    cosT = p.bsb([128, T + 1]); sinT = p.bsb([128, T + 1]); xt = p.bsb([128, T + 1]); tmpT = p.bsb([128, T + 1])
    rB = p.bsb([128, T])
    n1 = p.bsb([128, T]); n2 = p.bsb([128, T]); n3 = p.bsb([128, T]); n4 = p.bsb([128, T])
    for d in range(2):
        a_re = small(); a_im = small(); l_dt = small()
        p.dma("sp", [], [a_re], a_re[:], are[d]); p.dma("sp", [], [a_im], a_im[:], aim[d])
        p.dma("sp", [], [l_dt], l_dt[:], ldt[d])
        b_re = small(16); b_im = small(16)
        p.dma("sp", [], [b_re], b_re[:], bre[d]); p.dma("sp", [], [b_im], b_im[:], bim[d])
        c_re = p.bsb([128, 32], BF16); c_imf = small(32); c_imn = p.bsb([128, 32], BF16)
        p.dma("pool", [], [c_re], c_re[:], cre[d]); p.dma("sp", [], [c_imf], c_imf[:], cim[d])
        p.ts("dve", c_imn[:], c_imf[:], -1.0, None, ALU.mult, None, [c_imf], [c_imn])
        dt = small(); mag = small(); ang = small(); t0_ = small(); t1_ = small()
        p.act(dt[:], l_dt[:], AF.Exp, [l_dt], [dt])
        p.tt("dve", t0_[:], a_re[:], dt[:], ALU.mult, [a_re, dt], [t0_])
        p.act(mag[:], t0_[:], AF.Exp, [t0_], [mag])
        p.tt("dve", ang[:], a_im[:], dt[:], ALU.mult, [a_im, dt], [ang])
        sn = small(); cs = small(); w0 = small(); w1 = small()
        sin_rr(p, w0[:], ang[:], 0.0, w0[:], [ang], [w0])
        p.act(sn[:], w0[:], AF.Sin, [w0], [sn])
        sin_rr(p, w1[:], ang[:], math.pi / 2, w1[:], [ang], [w1])
        p.act(cs[:], w1[:], AF.Sin, [w1], [cs])
        lre = small(); lim = small()
        p.tt("dve", lre[:], mag[:], cs[:], ALU.mult, [mag, cs], [lre])
        p.tt("dve", lim[:], mag[:], sn[:], ALU.mult, [mag, sn], [lim])
        den = small(); rden = small(); nr = small()
        p.tt("dve", den[:], a_re[:], a_re[:], ALU.mult, [a_re], [den])
        p.stt(den[:], a_im[:], a_im[:, 0:1], den[:], ALU.mult, ALU.add, [a_im, den], [den])
        p.recip(rden[:], den[:], [den], [rden])
        p.ts("dve", nr[:], lre[:], -1.0, None, ALU.add, None, [lre], [nr])
        cr = small(); ci = small(); nci = small()
        p.tt("dve", t0_[:], lim[:], a_im[:], ALU.mult, [lim, a_im], [t0_])
        p.stt(cr[:], nr[:], a_re[:, 0:1], t0_[:], ALU.mult, ALU.add, [nr, a_re, t0_], [cr])
        p.tt("dve", cr[:], cr[:], rden[:], ALU.mult, [cr, rden], [cr])
        p.tt("dve", t1_[:], nr[:], a_im[:], ALU.mult, [nr, a_im], [t1_])
        p.stt(ci[:], lim[:], a_re[:, 0:1], t1_[:], ALU.mult, ALU.subtract, [lim, a_re, t1_], [ci])
        p.tt("dve", ci[:], ci[:], rden[:], ALU.mult, [ci, rden], [ci])
        p.ts("dve", nci[:], ci[:], -1.0, None, ALU.mult, None, [ci], [nci])
        bbr = small(16); bbi = small(16)
        p.ts("dve", bbr[:], b_re[:], cr[:, 0:1], None, ALU.mult, None, [b_re, cr], [bbr])
        p.stt(bbr[:], b_im[:], nci[:, 0:1], bbr[:], ALU.mult, ALU.add, [b_im, nci, bbr], [bbr])
        p.ts("dve", bbi[:], b_im[:], cr[:, 0:1], None, ALU.mult, None, [b_im, cr], [bbi])
        p.stt(bbi[:], b_re[:], ci[:, 0:1], bbi[:], ALU.mult, ALU.add, [b_re, ci, bbi], [bbi])
        LB = []
        for bb in (bbr, bbi):
            bx = small(32)
            p.memset("dve", bx[:], 0.0, [bx])
            p.cp("dve", bx[0:64, 0:16], bb[0:64, :], [bb], [bx])
            p.cp("dve", bx[64:128, 16:32], bb[64:128, :], [bb], [bx])
            p.tr(pst[0:32, 0:128], bx[:], idf[:], [bx, idf], [pst])
            lb = p.bsb([32, 128], BF16)
            p.cp("dve", lb[:], pst[0:32, 0:128], [pst], [lb])
            LB.append(lb)
        p.ts("dve", xt[:], jts[:], ang[:, 0:1], None, ALU.mult, None, [jts, ang], [xt])
        sin_rr(p, tmpT[:], xt[:], 0.0, tmpT[:], [xt], [tmpT])
        p.act(sinT[:], tmpT[:], AF.Sin, [tmpT], [sinT])
        sin_rr(p, tmpT[:], xt[:], math.pi / 2, tmpT[:], [xt], [tmpT])
        p.act(cosT[:], tmpT[:], AF.Sin, [tmpT], [cosT])
        p.ts("dve", rB[:], jts[:, 0:T], 0.0, mag[:, 0:1], ALU.mult, ALU.add, [jts, mag], [rB])
        init = [(small(), small()) for _ in range(2)]
        p.memset("dve", init[0][0][:], 0.0, [init[0][0]]); p.memset("dve", init[0][1][:], 0.0, [init[0][1]])
        tq = small(); tq2 = small()
        for ck, (c0, Tc) in enumerate(chunks):
            b = ck % 2
            A, B_, Y = psA[b], psB[b], psY[b]
            rhs = ub[d][:, c0:c0 + Tc]
            p.mm(A[:, 0:Tc], LB[0][:], rhs, True, True, [LB[0], ub[d]], [A])
            p.mm(B_[:, 0:Tc], LB[1][:], rhs, True, True, [LB[1], ub[d]], [B_])
            C_, S_ = cosT[:, 0:Tc], sinT[:, 0:Tc]
            p.tt("dve", m1[:, 0:Tc], A[:, 0:Tc], C_, ALU.mult, [A, cosT], [m1])
            p.tt("dve", m2[:, 0:Tc], B_[:, 0:Tc], S_, ALU.mult, [B_, sinT], [m2])
            p.tt("dve", btr[:, 0:Tc], m1[:, 0:Tc], m2[:, 0:Tc], ALU.add, [m1, m2], [btr])
            p.tt("dve", m3[:, 0:Tc], B_[:, 0:Tc], C_, ALU.mult, [B_, cosT], [m3])
            p.tt("dve", m4[:, 0:Tc], A[:, 0:Tc], S_, ALU.mult, [A, sinT], [m4])
            p.tt("dve", bti[:, 0:Tc], m3[:, 0:Tc], m4[:, 0:Tc], ALU.subtract, [m3, m4], [bti])
            ir, ii = init[b]
            p.op("dve", [rB, btr, ir], [gr[b]], lambda e, b=b, Tc=Tc, ir=ir: e.tensor_tensor_scan(
                gr[b][:, 0:Tc], rB[:, 0:Tc], btr[:, 0:Tc], ir[:, 0:1], ALU.mult, ALU.add))
            p.op("dve", [rB, bti, ii], [gi[b]], lambda e, b=b, Tc=Tc, ii=ii: e.tensor_tensor_scan(
                gi[b][:, 0:Tc], rB[:, 0:Tc], bti[:, 0:Tc], ii[:, 0:1], ALU.mult, ALU.add))
            nir, nii = init[1 - b]
            cT, sT = cosT[:, Tc:Tc + 1], sinT[:, Tc:Tc + 1]
            ge_r, ge_i = gr[b][:, Tc - 1:Tc], gi[b][:, Tc - 1:Tc]
            p.ts("dve", tq[:], ge_i, sT, None, ALU.mult, None, [gi[b], sinT], [tq])
            p.stt(nir[:], ge_r, cT, tq[:], ALU.mult, ALU.subtract, [gr[b], cosT, tq], [nir])
            p.ts("dve", tq2[:], ge_i, cT, None, ALU.mult, None, [gi[b], cosT], [tq2])
            p.stt(nii[:], ge_r, sT, tq2[:], ALU.mult, ALU.add, [gr[b], sinT, tq2], [nii])
            p.tt("pool", n1[:, 0:Tc], gr[b][:, 0:Tc], C_, ALU.mult, [gr[b], cosT], [n1])
            p.tt("pool", n2[:, 0:Tc], gi[b][:, 0:Tc], S_, ALU.mult, [gi[b], sinT], [n2])
            p.tt("pool", hr[b][:, 0:Tc], n1[:, 0:Tc], n2[:, 0:Tc], ALU.subtract, [n1, n2], [hr[b]])
            p.tt("pool", n3[:, 0:Tc], gr[b][:, 0:Tc], S_, ALU.mult, [gr[b], sinT], [n3])
            p.tt("pool", n4[:, 0:Tc], gi[b][:, 0:Tc], C_, ALU.mult, [gi[b], cosT], [n4])
            p.tt("pool", hi[b][:, 0:Tc], n3[:, 0:Tc], n4[:, 0:Tc], ALU.add, [n3, n4], [hi[b]])
            p.mm(Y[0:32, 0:Tc], c_re[:], hr[b][:, 0:Tc], True, False, [c_re, hr[b]], [Y])
            p.mm(Y[0:32, 0:Tc], c_imn[:], hi[b][:, 0:Tc], False, d == 1, [c_imn, hi[b]], [Y])
            if d == 0:
                p.mm(Y[0:32, 0:Tc], ddb[:], rhs, False, True, [ddb, ub[d]], [Y])
                p.cp("act", yf[:, c0:c0 + Tc], Y[0:32, 0:Tc], [Y], [yf])
            else:
                if ck == 0:
                    lo = 0
                else:
                    lo = NSC + NSM - (ck) * T
                yv = yf[:, lo:lo + Tc]
                p.tt("dve", yv, yv, Y[0:32, 0:Tc][:, ::-1], ALU.add, [yf, Y], [yf])
    for c0 in range(0, NS, 4096):
        w = min(4096, NS - c0)
        p.dma("sp", [yf], [], yo[:, c0:c0 + w], yf[:, c0:c0 + w])
    return p.finish()


def _cbias(p, val):
    key = ("cb", val)
    if not hasattr(p, "_consts"):
        p._consts = {}
    if key not in p._consts:
        b = p.bsb([128, 1])
        p.memset("dve", b[:], val, [b])
        p._consts[key] = b
    return p._consts[key]


def run_ps(pl, pc, lw):
    if "ps" not in _cache:
        _cache["ps"] = build_ps()
    T = 512
    s_l = pl[:, 768:1024]; s_c = pc[:, 768:1024]
    seq_f = np.concatenate([s_c, s_l], 0)
    seq_r = np.concatenate([s_c[::-1], s_l[::-1]], 0)
    jt = np.ascontiguousarray(np.tile(np.arange(T + 1, dtype=np.float32)[None, :], (128, 1)))
    maps = []
    for i in range(NCORES):
        g0 = 2 * i
        ch = slice(32 * i, 32 * i + 32)

        def gp(a):
            return np.ascontiguousarray(a[:, g0:g0 + 2].reshape(2, 128, *a.shape[3:]))
        cre = np.zeros((2, 128, 32), np.float32); cim = np.zeros((2, 128, 32), np.float32)
        for d in range(2):
            for gl in range(2):
                cre[d, gl * 64:(gl + 1) * 64, gl * 16:(gl + 1) * 16] = lw["s5_c_re"][d, g0 + gl].T
                cim[d, gl * 64:(gl + 1) * 64, gl * 16:(gl + 1) * 16] = lw["s5_c_im"][d, g0 + gl].T
        ldt = np.repeat(lw["s5_log_dt"][:, g0:g0 + 2, None], 64, axis=2).reshape(2, 128, 1)
        dd = np.zeros((32, 32), np.float32); dd[np.arange(32), np.arange(32)] = lw["s5_d"][ch]
        maps.append({"uf": np.ascontiguousarray(seq_f[:, ch].T), "ur": np.ascontiguousarray(seq_r[:, ch].T),
                     "are": gp(lw["s5_a_re"])[..., None], "aim": gp(lw["s5_a_im"])[..., None],
                     "ldt": np.ascontiguousarray(ldt), "bre": gp(lw["s5_b_re"]), "bim": gp(lw["s5_b_im"]),
                     "cre": cre, "cim": cim, "dd": dd, "jt": jt, "ident": np.eye(128, dtype=np.float32)})
    r = run(_cache["ps"], maps)
    y = np.concatenate([r[i]["y"] for i in range(NCORES)], 0).T
    return np.ascontiguousarray(y[LC:]), np.ascontiguousarray(y[:LC])


def build_ph(LM=L, LCX=LC):
    p = Prog()
    nc = p.nc
    NJ = LM // 128
    NJC = LCX // 128
    a3m = p.inp("a3m", [3, 128, NJ, 96]); a3c = p.inp("a3c", [3, 128, NJC, 96])
    cw = p.inp("cw", [1, 3 * 96]); cb = p.inp("cb", [1, 96])
    fw1 = p.inp("fw1", [33, 64]); fb1 = p.inp("fb1", [64, 1]); fw2 = p.inp("fw2", [64, 64]); fb2 = p.inp("fb2", [64, 1])
    fw3 = p.inp("fw3", [64, 128]); dec = p.inp("dec", [128, 1]); fbias = p.inp("fbias", [1, 64])
    ftm = p.inp("ftm", [33, LM]); ftc = p.inp("ftc", [33, LCX])
    antiid = p.inp("antiid", [128, 128])
    ym = p.outp("ym", [128, NJ, 32]); yc = p.outp("yc", [128, NJC, 32])
    kdm_t = nc.dram_tensor("kdm", [64, 2 * LM], BF16, kind="Internal")
    kdc_t = nc.dram_tensor("kdc", [64, 2 * LCX], BF16, kind="Internal")
    d_kdm = Dep(); d_kdc = Dep()

    w1s = p.bsb([33, 64]); p.dma("sp", [], [w1s], w1s[:], fw1)
    w2s = p.bsb([64, 64]); p.dma("sp", [], [w2s], w2s[:], fw2)
    w3s = p.bsb([64, 128]); p.dma("sp", [], [w3s], w3s[:], fw3)
    b1s = p.bsb([64, 1]); p.dma("sp", [], [b1s], b1s[:], fb1)
    b2s = p.bsb([64, 1]); p.dma("sp", [], [b2s], b2s[:], fb2)
    dcs = p.bsb([128, 1]); p.dma("sp", [], [dcs], dcs[:], dec)
    nd = p.bsb([128, 1])
    p.ts("dve", nd[:], dcs[:], -1.0, None, ALU.mult, None, [dcs], [nd])
    p.tt("dve", nd[:], nd[:], dcs[:], ALU.min, [nd, dcs], [nd])
    cws = bcast_load(p, "sp", cw, 128, 3 * 96)
    cbs = bcast_load(p, "sp", cb, 128, 96)
    fbs = bcast_load(p, "sp", fbias, 128, 64)
    Jb = p.bsb([128, 128], BF16)
    p.dma("pool", [], [Jb], Jb[:], antiid)

    fts = p.bsb([33, 512]); tg = p.bsb([128, 512])
    xa = p.bsb([64, 512]); xb = p.bsb([64, 512]); h1 = p.bsb([64, 512]); h2 = p.bsb([64, 512])
    E = p.bsb([128, 512]); hf = p.bsb([128, 512], BF16); hrv = p.bsb([128, 512], BF16)
    ps1 = p.bps([128, 512]); ps2 = p.bps([128, 512]); ps3 = p.bps([128, 512])

    def gen(ft, Lg, kd_t, d_kd):
        kd = kd_t.ap()
        for c0 in range(0, Lg, 512):
            w = min(512, Lg - c0)
            p.dma("sp", [], [fts], fts[:, 0:w], ft[:, c0:c0 + w])
            p.dma("sp", [], [tg], tg[:, 0:w], ft[0:1, c0:c0 + w].partition_broadcast(128))
            p.mm(ps1[0:64, 0:w], w1s[:], fts[:, 0:w], True, True, [w1s, fts], [ps1])
            p.ts("dve", xa[:, 0:w], ps1[0:64, 0:w], b1s[:, 0:1], None, ALU.add, None, [ps1, b1s], [xa])
            sin_rr(p, None, xa[:, 0:w], 0.0, xb[:, 0:w], [xa], [xb])
            p.act(h1[:, 0:w], xb[:, 0:w], AF.Sin, [xb], [h1])
            p.mm(ps2[0:64, 0:w], w2s[:], h1[:, 0:w], True, True, [w2s, h1], [ps2])
            p.ts("dve", xa[:, 0:w], ps2[0:64, 0:w], b2s[:, 0:1], None, ALU.add, None, [ps2, b2s], [xa])
            sin_rr(p, None, xa[:, 0:w], 0.0, xb[:, 0:w], [xa], [xb])
            p.act(h2[:, 0:w], xb[:, 0:w], AF.Sin, [xb], [h2])
            p.mm(ps3[:, 0:w], w3s[:], h2[:, 0:w], True, True, [w3s, h2], [ps3])
            p.act(E[:, 0:w], tg[:, 0:w], AF.Exp, [tg, nd], [E], scale=nd[:, 0:1])
            p.tt("dve", hf[:, 0:w], ps3[:, 0:w], E[:, 0:w], ALU.mult, [ps3, E], [hf])
            p.cp("dve", hrv[:, 0:w], hf[:, 0:w][:, ::-1], [hf], [hrv])
            for o in range(2):
                p.dma("sp", [hf], [d_kd], kd[o * 32:(o + 1) * 32, Lg + c0:Lg + c0 + w], hf[o * 64:o * 64 + 32, 0:w])
                p.dma("sp", [hrv], [d_kd], kd[o * 32:(o + 1) * 32, Lg - c0 - w:Lg - c0],
                      hrv[o * 64 + 32:o * 64 + 64, 0:w])

    gen(ftc, LCX, kdc_t, d_kdc)
    gen(ftm, LM, kdm_t, d_kdm)

    U = p.bsb([128, NJ, 96])
    HS = max(1, NJ // 4)
    stage = p.bsb([128, HS, 96])
    z1 = p.bsb([128, NJ, 32]); zb = p.bsb([128, NJ, 32], BF16); yout = p.bsb([128, NJ, 32])
    zbr = p.bsb([128, NJ, 32], BF16)
    strips = [p.bsb([128, 128 * 128], BF16) for _ in range(2)]
    psY = [p.bps([128, 512]) for _ in range(2)]
    tq = p.bsb([128, NJ])
    state = {"si": 0}

    def stream(a3, NJs, Lg, kd_t, d_kd, yo):
        for k in range(3):
            for j0 in range(0, NJs, HS):
                n = min(HS, NJs - j0)
                p.dma("sp", [], [stage], stage[:, 0:n, :], a3[k, :, j0:j0 + n, :])
                wk = cws[:, k * 96:(k + 1) * 96].unsqueeze(1).to_broadcast([128, n, 96])
                if k == 0:
                    p.tt("dve", U[:, j0:j0 + n, :], stage[:, 0:n, :], wk, ALU.mult, [stage, cws], [U])
                    p.tt("dve", U[:, j0:j0 + n, :], U[:, j0:j0 + n, :],
                         cbs[:, :].unsqueeze(1).to_broadcast([128, n, 96]), ALU.add, [U, cbs], [U])
                else:
                    p.tt("dve", stage[:, 0:n, :], stage[:, 0:n, :], wk, ALU.mult, [stage, cws], [stage])
                    p.tt("dve", U[:, j0:j0 + n, :], U[:, j0:j0 + n, :], stage[:, 0:n, :], ALU.add, [U, stage], [U])
        for o in range(2):
            zsrc = (lambda c: U[:, 0:NJs, c]) if o == 0 else (lambda c: z1[:, 0:NJs, c])
            zdep = U if o == 0 else z1
            gate = (lambda c: U[:, 0:NJs, 32 + c]) if o == 0 else (lambda c: U[:, 0:NJs, 64 + c])
            dst = z1 if o == 0 else yout
            if o == 0:
                p.cp("dve", zb[:, 0:NJs, :], U[:, 0:NJs, 0:32], [U], [zb])
            else:
                p.cp("dve", zb[:, 0:NJs, :], z1[:, 0:NJs, :], [z1], [zb])
            zbf = zb[:].rearrange("p j c -> p (j c)")
            zrf = zbr[:].rearrange("p j c -> p (j c)")
            for q0 in range(0, NJs * 32, 512):
                qw = min(512, NJs * 32 - q0)
                Yf = psY[(q0 // 512) % 2]
                p.mm(Yf[:, 0:qw], Jb[:], zbf[:, q0:q0 + qw], True, True, [Jb, zb], [Yf])
                p.cp("act", zrf[:, q0:q0 + qw], Yf[:, 0:qw], [Yf], [zbr])
            for c in range(32):
                row = o * 32 + c
                Y = psY[c % 2]
                halves = [list(range(0, NJs)), list(range(-(NJs - 1), 0))]
                nmm = sum(len(h_) for h_ in halves)
                cnt = 0
                for hv in halves:
                    if not hv:
                        continue
                    dmin = hv[0]
                    ndd = len(hv)
                    sb_ = strips[state["si"] % 2]; state["si"] += 1
                    src = bass.AP(tensor=kd_t, offset=row * 2 * Lg + Lg + 128 * dmin - 127, ap=[[1, 128], [1, 128 * ndd]])
                    p.dma("sp", [d_kd], [sb_], sb_[:, 0:128 * ndd], src)
                    for d in hv:
                        j0 = max(0, -d); j1 = min(NJs, NJs - d)
                        p.mm(Y[:, j0 + d:j1 + d], sb_[:, 128 * (d - dmin):128 * (d - dmin) + 128], zbr[:, j0:j1, c],
                             cnt == 0, cnt == nmm - 1, [sb_, zbr], [Y])
                        cnt += 1
                p.stt(tq[:, 0:NJs], zsrc(c), fbs[:, row:row + 1], Y[:, 0:NJs], ALU.mult, ALU.add, [zdep, fbs, Y], [tq])
                p.tt("dve", dst[:, 0:NJs, c], tq[:, 0:NJs], gate(c), ALU.mult, [tq, U], [dst])
        p.dma("sp", [yout], [], yo, yout[:, 0:NJs, :])

    stream(a3c, NJC, LCX, kdc_t, d_kdc, yc)
    stream(a3m, NJ, LM, kdm_t, d_kdm, ym)
    return p.finish()


def hyena_feats(Lg):
    t = (np.arange(Lg, dtype=np.float32) / np.float32(Lg)).astype(np.float32)
    fr = np.arange(1, 17, dtype=np.float32)
    ang = (np.float32(2.0 * math.pi) * t[:, None] * fr[None, :]).astype(np.float32)
    return np.ascontiguousarray(np.concatenate([t[:, None], np.cos(ang), np.sin(ang)], -1).T.astype(np.float32))


def run_ph(pl, pc, lw):
    if "ph" not in _cache:
        _cache["ph"] = build_ph()
        _cache["ftm"] = hyena_feats(L); _cache["ftc"] = hyena_feats(LC)
    maps = []
    for i in range(NCORES):
        cols = np.concatenate([np.arange(32) + 32 * i + 256 * part for part in range(3)])

        def a3(a, Lg):
            am = a[:, cols]
            pad = np.concatenate([np.zeros((1, 96), np.float32), am, np.zeros((1, 96), np.float32)], 0)
            return np.ascontiguousarray(np.stack([pad[k:k + Lg].reshape(Lg // 128, 128, 96).transpose(1, 0, 2) for k in range(3)]))
        w3cols = np.concatenate([np.arange(32) + 32 * i + 256 * (o * 2 + dr) for o in range(2) for dr in range(2)])
        maps.append({"a3m": a3(pl, L), "a3c": a3(pc, LC),
                     "cw": np.ascontiguousarray(lw["hy_conv_w"][:, cols].reshape(1, 288)), "cb": lw["hy_conv_b"][None, cols],
                     "fw1": lw["hy_ffn_w1"], "fb1": lw["hy_ffn_b1"][:, None], "fw2": lw["hy_ffn_w2"], "fb2": lw["hy_ffn_b2"][:, None],
                     "fw3": np.ascontiguousarray(lw["hy_ffn_w3"][:, w3cols]),
                     "dec": np.ascontiguousarray(lw["hy_decay"][:, :, 32 * i:32 * i + 32].reshape(128, 1)),
                     "fbias": np.ascontiguousarray(lw["hy_bias"][:, 32 * i:32 * i + 32].reshape(1, 64)),
                     "ftm": _cache["ftm"], "ftc": _cache["ftc"],
                     "antiid": np.ascontiguousarray(np.eye(128, dtype=np.float32)[::-1])})
    r = run(_cache["ph"], maps)
    yl = np.concatenate([r[i]["ym"].transpose(1, 0, 2).reshape(L, 32) for i in range(NCORES)], 1)
    ycx = np.concatenate([r[i]["yc"].transpose(1, 0, 2).reshape(LC, 32) for i in range(NCORES)], 1)
    return yl, ycx


PARAM_KEYS = ["norm_pre_mix", "norm_post_mix", "norm_pre_mlp", "norm_post_mlp", "w_in", "w_out", "hy_conv_w",
              "hy_conv_b", "hy_ffn_w1", "hy_ffn_b1", "hy_ffn_w2", "hy_ffn_b2", "hy_ffn_w3", "hy_decay", "hy_bias",
              "s5_a_re", "s5_a_im", "s5_log_dt", "s5_b_re", "s5_b_im", "s5_c_re", "s5_c_im", "s5_d", "s5_glu_w",
              "s5_glu_b", "pool_w", "pool_scale", "att_q_norm", "att_k_norm", "mlp_w1", "mlp_w2"]


def kernel(**inputs):
    inputs = {k: np.asarray(v, dtype=np.float32) for k, v in inputs.items()}
    mod = run_pm(inputs)
    xl = np.ascontiguousarray(inputs["x"][0])
    xc = np.ascontiguousarray(inputs["ctx"][0])
    for l in range(DEPTH):
        lw = {k: np.ascontiguousarray(inputs[k][l]) for k in PARAM_KEYS}
        pl, pc = run_pa(xl, xc, mod[l], lw)
        yhy_l, yhy_c = run_ph(pl, pc, lw)
        ys5_l, ys5_c = run_ps(pl, pc, lw)
        att_l, att_c = run_pt(pl, pc)
        xl, xc = run_pc1(xl, xc, yhy_l, yhy_c, ys5_l, ys5_c, pl[:, 1024:1280], pc[:, 1024:1280],
                         att_l, att_c, mod[l], lw)
        xl, xc = run_pc2(xl, xc, mod[l], lw)
    return np.ascontiguousarray(xl[None].astype(np.float32))
```

```python
import math
import numpy as np
from contextlib import ExitStack
import concourse.bass as bass
import concourse.mybir as mybir
from concourse.bass_utils import run_bass_kernel_spmd

F32 = mybir.dt.float32
BF16 = mybir.dt.bfloat16
AF = mybir.ActivationFunctionType
ALU = mybir.AluOpType
AX = mybir.AxisListType

NCORES = 8
D = 1024
L = 16384
LC = 256
DEPTH = 4
TPC = L // NCORES
NT_MAIN = TPC // 128
NT = NT_MAIN + LC // 128
INC = 1792
EPS = 1e-6
NDSEM = 8


class Dep:
    __slots__ = ("w", "r")

    def __init__(self):
        self.w = None
        self.r = {}


class Buf(Dep):
    __slots__ = ("t",)

    def __init__(self, t):
        Dep.__init__(self)
        self.t = t

    def __getitem__(self, k):
        return self.t[k]


class Prog:
    def __init__(self):
        self.nc = bass.Bass("TRN2", target_bir_lowering=False)
        self.es = ExitStack()
        nc = self.nc
        self.E = {"pe": nc.tensor, "dve": nc.vector, "act": nc.scalar, "pool": nc.gpsimd, "sp": nc.sync}
        self.sem = {k: self.es.enter_context(nc.semaphore("s_" + k)) for k in self.E}
        self.cnt = {k: 0 for k in self.E}
        self.waited = {}
        self.dq = {}
        for q in ("sp", "pool", "act"):
            self.dq[q] = [[self.es.enter_context(nc.semaphore("d_%s%d" % (q, i))), 0] for i in range(NDSEM)]
        self.dqi = {q: 0 for q in self.dq}
        self.nalloc = 0

    def inp(self, name, shape, dt=F32):
        return self.nc.dram_tensor(name, list(shape), dt, kind="ExternalInput").ap()

    def outp(self, name, shape, dt=F32):
        return self.nc.dram_tensor(name, list(shape), dt, kind="ExternalOutput").ap()

    def sb(self, shape, dt=F32, name=None):
        self.nalloc += 1
        return self.es.enter_context(self.nc.sbuf_tensor(name or ("t%d" % self.nalloc), list(shape), dt))

    def bsb(self, shape, dt=F32):
        return Buf(self.sb(shape, dt))

    def bps(self, shape, dt=F32):
        return Buf(self.ps(shape, dt))

    def ps(self, shape, dt=F32, name=None):
        self.nalloc += 1
        return self.es.enter_context(self.nc.psum_tensor(name or ("p%d" % self.nalloc), list(shape), dt))

    def _semh(self, key):
        if key[0] == "e":
            return self.sem[key[1]]
        return self.dq[key[1]][key[2]][0]

    def wait(self, eng, tok):
        if tok is None:
            return
        key, val = tok
        if eng == "pe" and key == ("e", "pe"):
            return
        k = (eng, key)
        if self.waited.get(k, 0) >= val:
            return
        self.E[eng].wait_ge(self._semh(key), val)
        self.waited[k] = val

    def _pre(self, eng, reads, writes):
        for d in reads:
            self.wait(eng, d.w)
        for d in writes:
            self.wait(eng, d.w)
            for t in d.r.values():
                self.wait(eng, t)

    def _post(self, tok, reads, writes):
        for d in reads:
            d.r[tok[0]] = tok
        for d in writes:
            d.w = tok
            d.r = {}

    def op(self, eng, reads, writes, fn):
        self._pre(eng, reads, writes)
        ins = fn(self.E[eng])
        self.cnt[eng] += 1
        ins.then_inc(self.sem[eng], 1)
        self._post((("e", eng), self.cnt[eng]), reads, writes)

    def dma(self, q, reads, writes, out, in_, **kw):
        self._pre(q, reads, writes)
        idx = self.dqi[q] % NDSEM
        self.dqi[q] += 1
        slot = self.dq[q][idx]
        key = ("d", q, idx)
        if slot[1] > 0:
            self.wait(q, (key, slot[1]))
        slot[1] += 16
        self.E[q].dma_start(out=out, in_=in_, **kw).then_inc(slot[0], 16)
        self._post((key, slot[1]), reads, writes)

    def finish(self):
        for q in self.dq:
            for i, slot in enumerate(self.dq[q]):
                if slot[1] > 0:
                    self.wait("sp", (("d", q, i), slot[1]))
        for e in self.E:
            if e != "sp" and self.cnt[e] > 0:
                self.wait("sp", (("e", e), self.cnt[e]))
        self.es.close()
        return self.nc

    def mm(self, out, lhsT, rhs, start, stop, reads, writes):
        self.op("pe", reads, writes, lambda e: e.matmul(out, lhsT, rhs, start=start, stop=stop))

    def tr(self, out, in_, ident, reads, writes):
        self.op("pe", reads, writes, lambda e: e.transpose(out, in_, ident))

    def act(self, out, in_, func, reads, writes, bias=None, scale=None, accum_out=None, eng="act"):
        kw = {}
        if bias is not None:
            kw["bias"] = bias
        if scale is not None:
            kw["scale"] = scale
        if accum_out is not None:
            kw["accum_out"] = accum_out
        self.op("act", reads, writes, lambda e: e.activation(out, in_, func, **kw))

    def tt(self, eng, out, in0, in1, op, reads, writes):
        self.op(eng, reads, writes, lambda e: e.tensor_tensor(out, in0, in1, op))

    def ts(self, eng, out, in0, s1, s2, op0, op1, reads, writes):
        if op1 is None:
            self.op(eng, reads, writes, lambda e: e.tensor_scalar(out, in0, s1, None, op0))
        else:
            self.op(eng, reads, writes, lambda e: e.tensor_scalar(out, in0, s1, s2, op0, op1))

    def stt(self, out, in0, scalar, in1, op0, op1, reads, writes):
        self.op("dve", reads, writes, lambda e: e.scalar_tensor_tensor(out, in0, scalar, in1, op0, op1))

    def cp(self, eng, out, in_, reads, writes):
        if eng == "act":
            self.op("act", reads, writes, lambda e: e.copy(out, in_))
        else:
            self.op(eng, reads, writes, lambda e: e.tensor_copy(out, in_))

    def recip(self, out, in_, reads, writes):
        self.op("dve", reads, writes, lambda e: e.reciprocal(out, in_))

    def memset(self, eng, ap, val, writes):
        self.op(eng, [], writes, lambda e: e.memset(ap, val))


def run(nc, in_maps):
    res = run_bass_kernel_spmd(nc, in_maps, core_ids=list(range(NCORES)))
    return res.results


_cache = {}


def build_pm():
    p = Prog()
    cc = p.inp("cc", [128, 8, 2])
    mw = p.inp("mw", [1024, 3072])
    mb = p.inp("mb", [1, 3072])
    out = p.outp("out", [2, 3072])
    cs = p.sb([128, 8, 2]); d_cs = Dep()
    sc = p.sb([128, 8, 2]); d_sc = Dep()
    wsb = p.sb([128, 8, 3072]); d_w = [Dep() for _ in range(8)]
    bsb = p.sb([2, 3072]); d_b = Dep()
    res = p.sb([2, 3072]); d_res = Dep()
    pss = [p.ps([128, 512]) for _ in range(2)]; d_ps = [Dep(), Dep()]
    p.dma("sp", [], [d_cs], cs[:], cc)
    p.dma("sp", [], [d_b], bsb[:], mb.partition_broadcast(2))
    for kc in range(8):
        p.dma("sp" if kc % 2 == 0 else "pool", [], [d_w[kc]], wsb[:, kc, :], mw[kc * 128:(kc + 1) * 128, :])
    p.act(sc[:], cs[:], AF.Silu, [d_cs], [d_sc])
    for n in range(6):
        b = n % 2
        for kc in range(8):
            p.mm(pss[b][0:2, :], sc[:, kc, :], wsb[:, kc, n * 512:(n + 1) * 512], kc == 0, kc == 7,
                 [d_sc, d_w[kc]], [d_ps[b]])
        p.tt("dve", res[:, n * 512:(n + 1) * 512], pss[b][0:2, :], bsb[:, n * 512:(n + 1) * 512], ALU.add,
             [d_ps[b], d_b], [d_res])
    p.dma("sp", [d_res], [], out, res[:])
    return p.finish()


def run_pm(inputs):
    if "pm" not in _cache:
        _cache["pm"] = build_pm()
    c2 = np.stack([inputs["c"][0], inputs["c_ctx"]], axis=-1)
    cc = np.ascontiguousarray(c2.reshape(8, 128, 2).transpose(1, 0, 2))
    maps = []
    for i in range(NCORES):
        l, h = i // 2, i % 2
        maps.append({"cc": cc,
                     "mw": np.ascontiguousarray(inputs["mod_w"][l][:, h * 3072:(h + 1) * 3072]),
                     "mb": np.ascontiguousarray(inputs["mod_b"][l][None, h * 3072:(h + 1) * 3072])})
    r = run(_cache["pm"], maps)
    mod = np.zeros((DEPTH, 2, 6144), np.float32)
    for i in range(NCORES):
        l, h = i // 2, i % 2
        mod[l, :, h * 3072:(h + 1) * 3072] = r[i]["out"]
    return mod


def build_pa(NT=NT, NT_MAIN=NT_MAIN, NB=2):
    p = Prog()
    x = p.inp("x", [NT * 128, D])
    modr = p.inp("modr", [4, D])
    gpre = p.inp("gpre", [1, D])
    win = p.inp("win", [D, INC])
    qkg = p.inp("qkg", [1, 384])
    rc = p.inp("rc", [NT * 128, 192])
    rs = p.inp("rs", [NT * 128, 192])
    ident = p.inp("ident", [128, 128])
    out = p.outp("proj", [NT * 128, INC])

    idb = p.bsb([128, 128], BF16)
    p.dma("pool", [], [idb], idb[:], ident)
    wsb = p.bsb([128, 8, INC], BF16)
    for kc in range(8):
        p.dma("pool", [], [wsb], wsb[:, kc, :], win[kc * 128:(kc + 1) * 128, :])
    gb = p.bsb([128, D])
    p.dma("sp", [], [gb], gb[:], gpre.partition_broadcast(128))
    mods = []
    for i in range(4):
        m = p.bsb([128, D])
        p.dma("sp", [], [m], m[:], modr[i:i + 1, :].partition_broadcast(128))
        mods.append(m)
    G = []
    for s in range(2):
        g = p.bsb([128, D])
        p.stt(g[:], mods[2 * s + 1][:], 1.0, gb[:], ALU.add, ALU.mult, [mods[2 * s + 1], gb], [g])
        G.append(g)
    SH = [mods[0], mods[2]]
    qkgb = p.bsb([128, 384])
    p.dma("sp", [], [qkgb], qkgb[:], qkg.partition_broadcast(128))

    xs = [p.bsb([128, D]) for _ in range(NB)]
    junk = p.bsb([128, D], BF16)
    ss = [p.bsb([128, 1]) for _ in range(NB)]
    rt = [p.bsb([128, 1]) for _ in range(NB)]
    rstd = [p.bsb([128, 1]) for _ in range(NB)]
    t1 = p.bsb([128, D])
    h = [p.bsb([128, D], BF16) for _ in range(NB)]
    psT = p.bps([128, 8, 128], BF16)
    hT = [p.bsb([128, 8, 128], BF16) for _ in range(NB)]
    pso = [p.bps([128, 512]) for _ in range(4)]
    osb = [p.bsb([128, INC]) for _ in range(NB)]
    sqb = p.bsb([128, 384])
    ssq = p.bsb([128, 6]); rt6 = p.bsb([128, 6]); rs6 = p.bsb([128, 6])
    qn = p.bsb([128, 384])
    tA = p.bsb([128, 192]); tB = p.bsb([128, 192])
    rcs = [p.bsb([128, 192]) for _ in range(NB)]
    rss = [p.bsb([128, 192]) for _ in range(NB)]
    chunks = [(0, 512), (512, 512), (1024, 256), (1280, 512)]

    for i in range(NT):
        b = i % NB
        s = 0 if i < NT_MAIN else 1
        rows = slice(i * 128, (i + 1) * 128)
        p.dma("sp", [], [xs[b]], xs[b][:], x[rows, :])
        p.dma("sp", [], [rcs[b]], rcs[b][:], rc[rows, :])
        p.dma("sp", [], [rss[b]], rss[b][:], rs[rows, :])
        p.act(junk[:], xs[b][:], AF.Square, [xs[b]], [junk, ss[b]], accum_out=ss[b][:])
        p.act(rt[b][:], ss[b][:], AF.Sqrt, [ss[b]], [rt[b]], bias=EPS, scale=1.0 / D)
        p.recip(rstd[b][:], rt[b][:], [rt[b]], [rstd[b]])
        p.stt(t1[:], xs[b][:], rstd[b][:, 0:1], G[s][:], ALU.mult, ALU.mult, [xs[b], rstd[b], G[s]], [t1])
        p.tt("dve", h[b][:], t1[:], SH[s][:], ALU.add, [t1, SH[s]], [h[b]])
        for kc in range(8):
            p.tr(psT[:, kc, :], h[b][:, kc * 128:(kc + 1) * 128], idb[:], [h[b], idb], [psT])
        p.cp("act", hT[b][:], psT[:], [psT], [hT[b]])
        for ci, (c0, w) in enumerate(chunks):
            for kc in range(8):
                p.mm(pso[ci][:, 0:w], hT[b][:, kc, :], wsb[:, kc, c0:c0 + w], kc == 0, kc == 7,
                     [hT[b], wsb], [pso[ci]])
        o = osb[b]
        p.cp("act", o[:, 0:512], pso[0][:, :], [pso[0]], [o])
        p.cp("dve", o[:, 512:1024], pso[1][:, :], [pso[1]], [o])
        p.cp("act", o[:, 1024:1280], pso[2][:, 0:256], [pso[2]], [o])
        p.cp("act", o[:, 1664:1792], pso[3][:, 384:512], [pso[3]], [o])
        p.act(sqb[:], pso[3][:, 0:384], AF.Square, [pso[3]], [sqb])
        p.op("dve", [sqb], [ssq], lambda e: e.tensor_reduce(
            ssq[:], sqb[:].rearrange("p (h d) -> p h d", d=64), AX.X, ALU.add))
        p.act(rt6[:], ssq[:], AF.Sqrt, [ssq], [rt6], bias=EPS, scale=1.0 / 64)
        p.recip(rs6[:], rt6[:], [rt6], [rs6])
        for hh in range(6):
            cs = slice(hh * 64, (hh + 1) * 64)
            p.stt(qn[:, cs], pso[3][:, cs], rs6[:, hh:hh + 1], qkgb[:, cs], ALU.mult, ALU.mult,
                  [pso[3], rs6, qkgb], [qn])
        qv = qn[:].rearrange("p (j two) -> p j two", two=2)
        ov = o[:, 1280:1664].rearrange("p (j two) -> p j two", two=2)
        x0, x1 = qv[:, :, 0], qv[:, :, 1]
        p.tt("dve", tA[:], x0, rcs[b][:], ALU.mult, [qn, rcs[b]], [tA])
        p.tt("dve", tB[:], x1, rss[b][:], ALU.mult, [qn, rss[b]], [tB])
        p.tt("dve", ov[:, :, 0], tA[:], tB[:], ALU.subtract, [tA, tB], [o])
        p.tt("dve", tA[:], x0, rss[b][:], ALU.mult, [qn, rss[b]], [tA])
        p.tt("dve", tB[:], x1, rcs[b][:], ALU.mult, [qn, rcs[b]], [tB])
        p.tt("dve", ov[:, :, 1], tA[:], tB[:], ALU.add, [tA, tB], [o])
        p.dma("sp", [o], [], out[rows, :], o[:])
    return p.finish()


def rope_tables():
    inv = (10000.0 ** (-np.arange(0, 32, 2, dtype=np.float32) / 32)).astype(np.float32)
    t = np.arange(L)
    rows = (t // 64).astype(np.float32)
    cols = (t % 64).astype(np.float32)
    ang = np.concatenate([rows[:, None] * inv[None, :], cols[:, None] * inv[None, :]], axis=-1).astype(np.float32)
    cos = np.cos(ang).astype(np.float32)
    sin = np.sin(ang).astype(np.float32)
    cos = np.concatenate([cos, np.ones((LC, 32), np.float32)], 0)
    sin = np.concatenate([sin, np.zeros((LC, 32), np.float32)], 0)
    return np.tile(cos, (1, 6)), np.tile(sin, (1, 6))


def tok_shard(main, ctx, i):
    return np.ascontiguousarray(np.concatenate([main[i * TPC:(i + 1) * TPC], ctx], axis=0))


def run_pa(xl, xc, mod_l, lw):
    if "pa" not in _cache:
        _cache["pa"] = build_pa()
        _cache["rope"] = rope_tables()
    cos, sin = _cache["rope"]
    modr = np.ascontiguousarray(np.stack([mod_l[0, 0:D], mod_l[0, D:2 * D], mod_l[1, 0:D], mod_l[1, D:2 * D]]))
    qkg = np.concatenate([np.tile(lw["att_q_norm"], 4), np.tile(lw["att_k_norm"], 2)])[None, :]
    maps = []
    for i in range(NCORES):
        maps.append({"x": tok_shard(xl, xc, i), "modr": modr, "gpre": lw["norm_pre_mix"][None, :],
                     "win": lw["w_in"], "qkg": np.ascontiguousarray(qkg),
                     "rc": tok_shard(cos[:L], cos[L:], i), "rs": tok_shard(sin[:L], sin[L:], i),
                     "ident": np.eye(128, dtype=np.float32)})
    r = run(_cache["pa"], maps)
    pl = np.concatenate([r[i]["proj"][:TPC] for i in range(NCORES)], 0)
    pc = r[0]["proj"][TPC:]
    return pl, pc


def bcast_load(p, q, src_row, n=128, width=D):
    b = p.bsb([n, width])
    p.dma(q, [], [b], b[:], src_row.partition_broadcast(n))
    return b


def rms_residual(p, xin, o_ps, GG, xout, ss, rt, rstd, junk, t1, width=D):
    nb = len(o_ps)
    parts = [p.bsb([128, 1]) for _ in range(0)]
    for j, ps in enumerate(o_ps):
        p.act(junk[:, j * 512:(j + 1) * 512], ps[:, :], AF.Square, [ps], [junk, ss[j]], accum_out=ss[j][:])
    if nb == 2:
        p.tt("dve", ss[0][:], ss[0][:], ss[1][:], ALU.add, [ss[0], ss[1]], [ss[0]])
    p.act(rt[:], ss[0][:], AF.Sqrt, [ss[0]], [rt], bias=EPS, scale=1.0 / width)
    p.recip(rstd[:], rt[:], [rt], [rstd])
    for j, ps in enumerate(o_ps):
        cs = slice(j * 512, (j + 1) * 512)
        p.stt(t1[:, cs], ps[:, :], rstd[:, 0:1], GG[:, cs], ALU.mult, ALU.mult, [ps, rstd, GG], [t1])
    p.tt("dve", xout[:], t1[:], xin[:], ALU.add, [t1, xin], [xout])


def build_pc1(NT=NT, NT_MAIN=NT_MAIN):
    p = Prog()
    NTOK = NT * 128
    NM = NT_MAIN * 128
    NCX = NTOK - NM
    WP = NM + 16 + NCX + 16
    x = p.inp("x", [NTOK, D])
    yhy = p.inp("yhy", [256, NTOK])
    ys5 = p.inp("ys5", [256, NTOK])
    pp = p.inp("pp", [256, WP])
    att = p.inp("att", [256, NTOK])
    wout = p.inp("wout", [D, D])
    gluw = p.inp("gluw", [256, 256])
    glub = p.inp("glub", [128, 2])
    poolw = p.inp("poolw", [2, 128, 128])
    pools = p.inp("pools", [128, 2])
    icnt = p.inp("icnt", [128, 2, WP])
    gate = p.inp("gate", [2, D])
    gpost = p.inp("gpost", [1, D])
    xo = p.outp("xo", [NTOK, D])

    wsb = p.bsb([128, 8, D], BF16)
    for kc in range(8):
        p.dma("pool", [], [wsb], wsb[:, kc, :], wout[kc * 128:(kc + 1) * 128, :])
    gw = p.bsb([128, 2, 256], BF16)
    for kc in range(2):
        p.dma("pool", [], [gw], gw[:, kc, :], gluw[kc * 128:(kc + 1) * 128, :])
    pw = p.bsb([128, 2, 128], BF16)
    for kc in range(2):
        p.dma("pool", [], [pw], pw[:, kc, :], poolw[kc])
    gbias = p.bsb([128, 2]); p.dma("sp", [], [gbias], gbias[:], glub)
    psc = p.bsb([128, 2]); p.dma("sp", [], [psc], psc[:], pools)
    gpb = bcast_load(p, "sp", gpost)
    GG = []
    for s in range(2):
        gt = bcast_load(p, "sp", gate[s:s + 1, :])
        gg = p.bsb([128, D])
        p.tt("dve", gg[:], gt[:], gpb[:], ALU.mult, [gt, gpb], [gg])
        GG.append(gg)

    mT = p.bsb([128, 8, NTOK], BF16)
    SA = p.bsb([128, 2, WP]); SB = p.bsb([128, 2, WP]); SC = p.bsb([128, 2, WP])
    SD = p.bsb([128, 2, WP]); SE = p.bsb([128, 2, WP])
    HB = p.bsb([128, 2, WP], BF16)
    for (src, k0, st) in ((yhy, 0, SD), (att, 6, SE)):
        for kc in range(2):
            p.dma("sp", [], [st], st[:, kc, 0:NTOK], src[kc * 128:(kc + 1) * 128, :])
        p.cp("dve", mT[:, k0:k0 + 2, :], st[:, :, 0:NTOK], [st], [mT])
    yv = SA; y2 = SB; gf = SC; gbf = HB
    for kc in range(2):
        p.dma("sp", [], [yv], yv[:, kc, 0:NTOK], ys5[kc * 128:(kc + 1) * 128, :])
    N_ = slice(0, NTOK)
    p.tt("dve", y2[:, :, N_], yv[:, :, N_], yv[:, :, N_], ALU.mult, [yv], [y2])
    p.ts("dve", y2[:, :, N_], y2[:, :, N_], 0.044715, 1.0, ALU.mult, ALU.add, [y2], [y2])
    p.tt("dve", y2[:, :, N_], y2[:, :, N_], yv[:, :, N_], ALU.mult, [y2, yv], [y2])
    p.act(y2[:, :, N_], y2[:, :, N_], AF.Sigmoid, [y2], [y2], scale=2.0 * math.sqrt(2.0 / math.pi))
    p.tt("dve", gf[:, :, N_], y2[:, :, N_], yv[:, :, N_], ALU.mult, [y2, yv], [gf])
    p.cp("dve", gbf[:, :, N_], gf[:, :, N_], [gf], [gbf])
    psg = [p.bps([128, 512]) for _ in range(2)]
    sg = p.bsb([128, 512])
    ci = 0
    for t0 in range(0, NTOK, 512):
        w = min(512, NTOK - t0)
        for mc in range(2):
            ps = psg[ci % 2]; ci += 1
            for kc in range(2):
                p.mm(ps[:, 0:w], gw[:, kc, mc * 128:(mc + 1) * 128], gbf[:, kc, t0:t0 + w], kc == 0, kc == 1,
                     [gw, gbf], [ps])
            p.act(sg[:, 0:w], ps[:, 0:w], AF.Sigmoid, [ps, gbias], [sg], bias=gbias[:, mc:mc + 1])
            p.tt("dve", mT[:, 2 + mc, t0:t0 + w], sg[:, 0:w], gf[:, mc, t0:t0 + w], ALU.mult, [sg, gf], [mT])
    pv = SA; ic = SB; pm = SC; pmb = HB
    for kc in range(2):
        p.dma("sp", [], [pv], pv[:, kc, :], pp[kc * 128:(kc + 1) * 128, :])
    p.dma("sp", [], [ic], ic[:], icnt)
    p.tt("dve", SD[:, :, 1:WP], pv[:, :, 0:WP - 1], pv[:, :, 1:WP], ALU.add, [pv], [SD])
    p.tt("dve", pm[0:64, 0, 1:WP], SD[0:64, 0, 1:WP], ic[0:64, 0, 1:WP], ALU.mult, [SD, ic], [pm])
    p.tt("dve", SE[:, :, 2:WP - 1], SD[:, :, 1:WP - 2], SD[:, :, 3:WP], ALU.add, [SD], [SE])
    p.tt("dve", pm[64:128, 0, 2:WP - 1], SE[64:128, 0, 2:WP - 1], ic[64:128, 0, 2:WP - 1], ALU.mult, [SE, ic], [pm])
    p.tt("dve", SD[:, :, 4:WP - 3], SE[:, :, 2:WP - 5], SE[:, :, 6:WP - 1], ALU.add, [SE], [SD])
    p.tt("dve", pm[0:64, 1, 4:WP - 3], SD[0:64, 1, 4:WP - 3], ic[0:64, 1, 4:WP - 3], ALU.mult, [SD, ic], [pm])
    p.tt("dve", SE[:, :, 8:WP - 7], SD[:, :, 4:WP - 11], SD[:, :, 12:WP - 3], ALU.add, [SD], [SE])
    p.tt("dve", pm[64:128, 1, 8:WP - 7], SE[64:128, 1, 8:WP - 7], ic[64:128, 1, 8:WP - 7], ALU.mult, [SE, ic], [pm])
    V_ = slice(8, WP - 8)
    p.tt("dve", pmb[:, :, V_], pm[:, :, V_], pv[:, :, V_], ALU.subtract, [pm, pv], [pmb])
    segs = [(8, 0, NM), (NM + 16 + 8, NM, NCX)]
    for (so, do, cnt) in segs:
        for t0 in range(0, cnt, 512):
            w = min(512, cnt - t0)
            for kc in range(2):
                ps = psg[ci % 2]; ci += 1
                p.mm(ps[:, 0:w], pw[:, kc, :], pmb[:, kc, so + t0:so + t0 + w], True, True, [pw, pmb], [ps])
                p.ts("dve", mT[:, 4 + kc, do + t0:do + t0 + w], ps[:, 0:w], psc[:, kc:kc + 1], None, ALU.mult, None,
                     [ps, psc], [mT])
    pso = [p.bps([128, 512]) for _ in range(2)]
    xs = [p.bsb([128, D]) for _ in range(2)]
    xn = [p.bsb([128, D]) for _ in range(2)]
    junk = p.bsb([128, D], BF16)
    ss = [p.bsb([128, 1]) for _ in range(2)]
    rt = p.bsb([128, 1]); rstd = p.bsb([128, 1]); t1 = p.bsb([128, D])
    for i in range(NT):
        b = i % 2
        s = 0 if i < NT_MAIN else 1
        rows = slice(i * 128, (i + 1) * 128)
        p.dma("sp", [], [xs[b]], xs[b][:], x[rows, :])
        for j in range(2):
            for kc in range(8):
                p.mm(pso[j][:, :], mT[:, kc, i * 128:(i + 1) * 128], wsb[:, kc, j * 512:(j + 1) * 512], kc == 0, kc == 7,
                     [mT, wsb], [pso[j]])
        rms_residual(p, xs[b], pso, GG[s], xn[b], ss, rt, rstd, junk, t1)
        p.dma("sp", [xn[b]], [], xo[rows, :], xn[b][:])
    return p.finish()


def pool_inv_counts(Lseq):
    t = np.arange(Lseq)
    out = []
    for w in (2, 4, 8, 16):
        lo = np.clip(t - w // 2, 0, Lseq); hi = np.clip(t + w // 2, 0, Lseq)
        out.append((1.0 / (hi - lo)).astype(np.float32))
    return np.stack(out)


def pad_seg(a, lo, hi):
    C, Ls = a.shape
    out = np.zeros((C, hi - lo), a.dtype)
    s0, s1 = max(lo, 0), min(hi, Ls)
    out[:, s0 - lo:s1 - lo] = a[:, s0:s1]
    return out


def run_pc1(xl, xc, yhy_l, yhy_c, ys5_l, ys5_c, p_l, p_c, att_l, att_c, mod_l, lw):
    if "pc1" not in _cache:
        _cache["pc1"] = build_pc1()
        _cache["icm"] = pool_inv_counts(L); _cache["icc"] = pool_inv_counts(LC)
    icm, icc = _cache["icm"], _cache["icc"]
    poolw = np.zeros((2, 128, 128), np.float32)
    for g in range(4):
        poolw[g // 2, (g % 2) * 64:(g % 2) * 64 + 64, (g % 2) * 64:(g % 2) * 64 + 64] = lw["pool_w"][g]
    gate = np.ascontiguousarray(np.stack([mod_l[0, 2 * D:3 * D], mod_l[1, 2 * D:3 * D]]))
    maps = []
    for i in range(NCORES):
        lo, hi = i * TPC, (i + 1) * TPC
        cat = lambda m, c: np.ascontiguousarray(np.concatenate([m[lo:hi], c], 0).T)
        pp = np.concatenate([pad_seg(p_l.T, lo - 8, hi + 8), pad_seg(p_c.T, -8, LC + 8)], 1)
        ic = np.concatenate([pad_seg(icm, lo - 8, hi + 8), pad_seg(icc, -8, LC + 8)], 1)
        icnt = np.repeat(ic.reshape(2, 2, 1, -1), 64, axis=2).reshape(2, 128, -1).transpose(1, 0, 2)
        maps.append({"x": tok_shard(xl, xc, i), "yhy": cat(yhy_l, yhy_c), "ys5": cat(ys5_l, ys5_c),
                     "pp": np.ascontiguousarray(pp), "att": cat(att_l, att_c), "wout": lw["w_out"],
                     "gluw": lw["s5_glu_w"], "glub": np.ascontiguousarray(lw["s5_glu_b"].reshape(2, 128).T),
                     "poolw": poolw, "pools": np.ascontiguousarray(lw["pool_scale"].reshape(2, 128).T),
                     "icnt": np.ascontiguousarray(icnt), "gate": gate, "gpost": lw["norm_post_mix"][None, :]})
    r = run(_cache["pc1"], maps)
    return (np.concatenate([r[i]["xo"][:TPC] for i in range(NCORES)], 0), r[0]["xo"][TPC:])


def build_pc2(NT=NT, NT_MAIN=NT_MAIN, GT=2):
    p = Prog()
    NTOK = NT * 128
    x = p.inp("x", [NTOK, D])
    w1 = p.inp("w1", [D, 4 * D])
    w2 = p.inp("w2", [4 * D, D])
    modr = p.inp("modr", [6, D])
    gpre = p.inp("gpre", [1, D])
    gpost = p.inp("gpost", [1, D])
    ident = p.inp("ident", [128, 128])
    xo = p.outp("xo", [NTOK, D])
    idb = p.bsb([128, 128], BF16)
    p.dma("pool", [], [idb], idb[:], ident)
    w1sb = p.bsb([128, 8, 4 * D], BF16)
    for kc in range(8):
        for hh in range(2):
            p.dma("pool", [], [w1sb], w1sb[:, kc, hh * 2048:(hh + 1) * 2048],
                  w1[kc * 128:(kc + 1) * 128, hh * 2048:(hh + 1) * 2048])
    w2sb = p.bsb([128, 32, D], BF16)
    for fc in range(32):
        p.dma("pool", [], [w2sb], w2sb[:, fc, :], w2[fc * 128:(fc + 1) * 128, :])
    tmpc = p.bsb([128, D])
    Gc = p.bsb([128, D]); SHc = p.bsb([128, D]); GGc = p.bsb([128, D])
    W = GT * 128
    xs = p.bsb([128, GT, D])
    hT = p.bsb([128, 8, W], BF16)
    uT = p.bsb([128, 32, W], BF16)
    junk = p.bsb([128, D], BF16)
    ss = [p.bsb([128, 1]) for _ in range(2)]
    rt = p.bsb([128, 1]); rstd = p.bsb([128, 1]); t1 = p.bsb([128, D])
    hb = p.bsb([128, D], BF16)
    psT = p.bps([128, 8, 128], BF16)
    psu = [p.bps([128, 512]) for _ in range(2)]
    pso = [p.bps([128, 512]) for _ in range(2)]
    rl = p.bsb([128, W])
    xn = [p.bsb([128, D]) for _ in range(2)]
    cur_stream = -1
    tiles = list(range(NT))
    groups = []
    i = 0
    while i < NT:
        lim = NT_MAIN if i < NT_MAIN else NT
        g = tiles[i:min(i + GT, lim)]
        groups.append(g)
        i += len(g)
    for g in groups:
        s = 0 if g[0] < NT_MAIN else 1
        if s != cur_stream:
            cur_stream = s
            p.dma("sp", [], [tmpc], tmpc[:], gpre.partition_broadcast(128))
            p.dma("sp", [], [Gc], Gc[:], modr[3 * s + 1:3 * s + 2, :].partition_broadcast(128))
            p.stt(Gc[:], Gc[:], 1.0, tmpc[:], ALU.add, ALU.mult, [Gc, tmpc], [Gc])
            p.dma("sp", [], [SHc], SHc[:], modr[3 * s:3 * s + 1, :].partition_broadcast(128))
            p.dma("sp", [], [tmpc], tmpc[:], gpost.partition_broadcast(128))
            p.dma("sp", [], [GGc], GGc[:], modr[3 * s + 2:3 * s + 3, :].partition_broadcast(128))
            p.tt("dve", GGc[:], GGc[:], tmpc[:], ALU.mult, [GGc, tmpc], [GGc])
        w = len(g) * 128
        for gi, ti in enumerate(g):
            rows = slice(ti * 128, (ti + 1) * 128)
            p.dma("sp", [], [xs], xs[:, gi, :], x[rows, :])
            p.act(junk[:], xs[:, gi, :], AF.Square, [xs], [junk, ss[0]], accum_out=ss[0][:])
            p.act(rt[:], ss[0][:], AF.Sqrt, [ss[0]], [rt], bias=EPS, scale=1.0 / D)
            p.recip(rstd[:], rt[:], [rt], [rstd])
            p.stt(t1[:], xs[:, gi, :], rstd[:, 0:1], Gc[:], ALU.mult, ALU.mult, [xs, rstd, Gc], [t1])
            p.tt("dve", hb[:], t1[:], SHc[:], ALU.add, [t1, SHc], [hb])
            for kc in range(8):
                p.tr(psT[:, kc, :], hb[:, kc * 128:(kc + 1) * 128], idb[:], [hb, idb], [psT])
            p.cp("act", hT[:, :, gi * 128:(gi + 1) * 128], psT[:], [psT], [hT])
        for fc in range(32):
            ps = psu[fc % 2]
            for kc in range(8):
                p.mm(ps[:, 0:w], w1sb[:, kc, fc * 128:(fc + 1) * 128], hT[:, kc, 0:w], kc == 0, kc == 7,
                     [w1sb, hT], [ps])
            p.act(rl[:, 0:w], ps[:, 0:w], AF.Relu, [ps], [rl])
            p.tt("dve", uT[:, fc, 0:w], rl[:, 0:w], rl[:, 0:w], ALU.mult, [rl], [uT])
        for gi, ti in enumerate(g):
            rows = slice(ti * 128, (ti + 1) * 128)
            b = ti % 2
            for j in range(2):
                for fc in range(32):
                    p.mm(pso[j][:, :], uT[:, fc, gi * 128:(gi + 1) * 128], w2sb[:, fc, j * 512:(j + 1) * 512],
                         fc == 0, fc == 31, [uT, w2sb], [pso[j]])
            rms_residual(p, _View(xs, lambda t, gi=gi: t[:, gi, :]), pso, GGc, xn[b], ss, rt, rstd, junk, t1)
            p.dma("sp", [xn[b]], [], xo[rows, :], xn[b][:])
    return p.finish()


class _View:
    def __init__(self, buf, fn):
        self._b = buf
        self._fn = fn

    def __getitem__(self, k):
        return self._fn(self._b.t)

    @property
    def w(self):
        return self._b.w

    @w.setter
    def w(self, v):
        self._b.w = v

    @property
    def r(self):
        return self._b.r

    @r.setter
    def r(self, v):
        self._b.r = v


def run_pc2(xl, xc, mod_l, lw):
    if "pc2" not in _cache:
        _cache["pc2"] = build_pc2()
    modr = np.ascontiguousarray(np.stack([mod_l[s, k * D:(k + 1) * D] for s in range(2) for k in (3, 4, 5)]))
    maps = []
    for i in range(NCORES):
        maps.append({"x": tok_shard(xl, xc, i), "w1": lw["mlp_w1"], "w2": lw["mlp_w2"], "modr": modr,
                     "gpre": lw["norm_pre_mlp"][None, :], "gpost": lw["norm_post_mlp"][None, :],
                     "ident": np.eye(128, dtype=np.float32)})
    r = run(_cache["pc2"], maps)
    return (np.concatenate([r[i]["xo"][:TPC] for i in range(NCORES)], 0), r[0]["xo"][TPC:])


def build_pt(NQM=TPC, NQC=LC, NKM=L, NKC=LC):
    p = Prog()
    NQ = NQM + NQC
    NK = NKM + NKC
    NKT = NK // 128
    qT = p.inp("qT", [64, 4, NQ])
    kT = p.inp("kT", [64, 2, NK])
    vt = p.inp("vt", [128, NKT, 2, 64])
    oT = p.outp("oT", [256, NQ])
    qsb = p.bsb([128, 4, NQ], BF16)
    p.memset("dve", qsb[64:128, :, :], 0.0, [qsb])
    for h in range(4):
        for c0 in range(0, NQ, 2048):
            w = min(2048, NQ - c0)
            p.dma("pool", [], [qsb], qsb[0:64, h, c0:c0 + w], qT[:, h, c0:c0 + w])
    ksb = p.bsb([128, 2, NK], BF16)
    p.memset("dve", ksb[64:128, :, :], 0.0, [ksb])
    for kv in range(2):
        for c0 in range(0, NK, 2048):
            w = min(2048, NK - c0)
            p.dma("pool", [], [ksb], ksb[0:64, kv, c0:c0 + w], kT[:, kv, c0:c0 + w])
    v1 = p.bsb([128, NKT, 2, 65], BF16)
    p.memset("dve", v1[:], 1.0, [v1])
    vst = [p.bsb([128, 13, 2, 64]) for _ in range(2)]
    for gi, k0 in enumerate(range(0, NKT, 13)):
        n = min(13, NKT - k0)
        st = vst[gi % 2]
        p.dma("sp", [], [st], st[:, 0:n], vt[:, k0:k0 + n])
        p.cp("dve", v1[:, k0:k0 + n, :, 0:64], st[:, 0:n], [st], [v1])
    ones = p.bsb([128, 64]); p.memset("dve", ones[:], 1.0, [ones])
    NPS = 4
    pss = [p.bps([128, 512]) for _ in range(NPS)]
    pso = [p.bps([128, 512]) for _ in range(2)]
    psb = p.bps([128, 512])
    pT = [p.bsb([128, 512], BF16) for _ in range(NPS)]
    rs = p.bsb([128, 512]); oc = p.bsb([128, 512])
    on = [p.bsb([64, 512]) for _ in range(2)]
    jobs = []
    for h in range(4):
        for q0 in range(0, NQM, 512):
            jobs.append((h, q0, min(512, NQM - q0), list(range(NKT))))
        jobs.append((h, NQM, NQC, list(range(NKM // 128, NKT))))
    iters = []
    for ji, (h, q0, w, kts) in enumerate(jobs):
        for n, kt in enumerate(kts):
            iters.append((ji, h, q0, w, kt, n == 0, n == len(kts) - 1))
    LA = NPS - 1
    NI = len(iters)
    for g in range(NI + LA):
        if g < NI:
            ji, h, q0, w, kt, first, last = iters[g]
            ps = pss[g % NPS]
            p.mm(ps[:, 0:w], ksb[:, h // 2, kt * 128:(kt + 1) * 128], qsb[:, h, q0:q0 + w], True, True,
                 [ksb, qsb], [ps])
        e = g - LA
        if e >= 0:
            ji, h, q0, w, kt, first, last = iters[e]
            ps = pss[e % NPS]; pt = pT[e % NPS]; po = pso[ji % 2]
            p.act(pt[:, 0:w], ps[:, 0:w], AF.Exp, [ps], [pt], scale=0.125)
            p.mm(po[0:65, 0:w], v1[:, kt, h // 2, :], pt[:, 0:w], first, last, [v1, pt], [po])
            if last:
                p.recip(rs[64:65, 0:w], po[64:65, 0:w], [po], [rs])
                p.cp("act", oc[0:64, 0:w], po[0:64, 0:w], [po], [oc])
                p.mm(psb[0:64, 0:w], ones[64:65, 0:64], rs[64:65, 0:w], True, True, [ones, rs], [psb])
                o = on[ji % 2]
                p.tt("dve", o[:, 0:w], oc[0:64, 0:w], psb[0:64, 0:w], ALU.mult, [oc, psb], [o])
                p.dma("sp", [o], [], oT[h * 64:(h + 1) * 64, q0:q0 + w], o[:, 0:w])
    return p.finish()


def run_pt(pl, pc):
    if "pt" not in _cache:
        _cache["pt"] = build_pt()
    kall = np.concatenate([pl[:, 1536:1664], pc[:, 1536:1664]], 0)
    vall = np.concatenate([pl[:, 1664:1792], pc[:, 1664:1792]], 0)
    NK = kall.shape[0]
    kT = np.ascontiguousarray(kall.reshape(NK, 2, 64).transpose(2, 1, 0))
    vt = np.ascontiguousarray(vall.reshape(NK // 128, 128, 2, 64).transpose(1, 0, 2, 3))
    maps = []
    for i in range(NCORES):
        q = np.concatenate([pl[i * TPC:(i + 1) * TPC, 1280:1536], pc[:, 1280:1536]], 0)
        maps.append({"qT": np.ascontiguousarray(q.reshape(-1, 4, 64).transpose(2, 1, 0)), "kT": kT, "vt": vt})
    r = run(_cache["pt"], maps)
    att_l = np.concatenate([r[i]["oT"][:, :TPC].T for i in range(NCORES)], 0)
    att_c = r[0]["oT"][:, TPC:].T
    return att_l, att_c


MAGIC = 12582912.0
TWO_PI = 2.0 * math.pi
PI_LO = 3.1415925


def sin_rr(p, out, in_, phase, tmp, reads, writes):
    p.ts("dve", tmp, in_, 1.0 / TWO_PI, phase / TWO_PI, ALU.mult, ALU.add, reads, writes)
    p.ts("dve", tmp, tmp, MAGIC, None, ALU.add, None, writes, writes)
    p.ts("dve", tmp, tmp, -MAGIC, None, ALU.add, None, writes, writes)
    p.stt(tmp, tmp, -TWO_PI, in_, ALU.mult, ALU.add, reads + writes, writes)
    p.ts("dve", tmp, tmp, phase, PI_LO, ALU.add, ALU.min, writes, writes)
    p.ts("dve", tmp, tmp, -PI_LO, None, ALU.max, None, writes, writes)
    return tmp


def build_ps(NSC=LC, NSM=L, T=512):
    p = Prog()
    NS = NSC + NSM
    u_f = p.inp("uf", [32, NS])
    u_r = p.inp("ur", [32, NS])
    are = p.inp("are", [2, 128, 1]); aim = p.inp("aim", [2, 128, 1]); ldt = p.inp("ldt", [2, 128, 1])
    bre = p.inp("bre", [2, 128, 16]); bim = p.inp("bim", [2, 128, 16])
    cre = p.inp("cre", [2, 128, 32]); cim = p.inp("cim", [2, 128, 32])
    dd = p.inp("dd", [32, 32])
    jt = p.inp("jt", [128, T + 1])
    ident = p.inp("ident", [128, 128])
    yo = p.outp("y", [32, NS])

    ub = [p.bsb([32, NS], BF16), p.bsb([32, NS], BF16)]
    for d, src in enumerate((u_f, u_r)):
        for c0 in range(0, NS, 2048):
            w = min(2048, NS - c0)
            p.dma("pool", [], [ub[d]], ub[d][:, c0:c0 + w], src[:, c0:c0 + w])
    yf = p.bsb([32, NS])
    idf = p.bsb([128, 128]); p.dma("sp", [], [idf], idf[:], ident)
    jts = p.bsb([128, T + 1]); p.dma("sp", [], [jts], jts[:], jt)
    ddb = p.bsb([32, 32], BF16); p.dma("pool", [], [ddb], ddb[:], dd)
    pst = p.bps([128, 512])
    psA = [p.bps([128, 512]) for _ in range(2)]
    psB = [p.bps([128, 512]) for _ in range(2)]
    psY = [p.bps([128, 512]) for _ in range(2)]

    def small(n=1):
        return p.bsb([128, n])

    chunks = [(0, NSC)] + [(NSC + k * T, T) for k in range(NSM // T)]
    m1 = p.bsb([128, T]); m2 = p.bsb([128, T]); m3 = p.bsb([128, T]); m4 = p.bsb([128, T])
    btr = p.bsb([128, T]); bti = p.bsb([128, T])
    gr = [p.bsb([128, T]) for _ in range(2)]; gi = [p.bsb([128, T]) for _ in range(2)]
    hr = [p.bsb([128, T], BF16) for _ in range(2)]; hi = [p.bsb([128, T], BF16) for _ in range(2)]
    cosT = p.bsb([128, T + 1]); sinT = p.bsb([128, T + 1]); xt = p.bsb([128, T + 1]); tmpT = p.bsb([128, T + 1])
    rB = p.bsb([128, T])
    n1 = p.bsb([128, T]); n2 = p.bsb([128, T]); n3 = p.bsb([128, T]); n4 = p.bsb([128, T])
    for d in range(2):
        a_re = small(); a_im = small(); l_dt = small()
        p.dma("sp", [], [a_re], a_re[:], are[d]); p.dma("sp", [], [a_im], a_im[:], aim[d])
        p.dma("sp", [], [l_dt], l_dt[:], ldt[d])
        b_re = small(16); b_im = small(16)
        p.dma("sp", [], [b_re], b_re[:], bre[d]); p.dma("sp", [], [b_im], b_im[:], bim[d])
        c_re = p.bsb([128, 32], BF16); c_imf = small(32); c_imn = p.bsb([128, 32], BF16)
        p.dma("pool", [], [c_re], c_re[:], cre[d]); p.dma("sp", [], [c_imf], c_imf[:], cim[d])
        p.ts("dve", c_imn[:], c_imf[:], -1.0, None, ALU.mult, None, [c_imf], [c_imn])
        dt = small(); mag = small(); ang = small(); t0_ = small(); t1_ = small()
        p.act(dt[:], l_dt[:], AF.Exp, [l_dt], [dt])
        p.tt("dve", t0_[:], a_re[:], dt[:], ALU.mult, [a_re, dt], [t0_])
        p.act(mag[:], t0_[:], AF.Exp, [t0_], [mag])
        p.tt("dve", ang[:], a_im[:], dt[:], ALU.mult, [a_im, dt], [ang])
        sn = small(); cs = small(); w0 = small(); w1 = small()
        sin_rr(p, w0[:], ang[:], 0.0, w0[:], [ang], [w0])
        p.act(sn[:], w0[:], AF.Sin, [w0], [sn])
        sin_rr(p, w1[:], ang[:], math.pi / 2, w1[:], [ang], [w1])
        p.act(cs[:], w1[:], AF.Sin, [w1], [cs])
        lre = small(); lim = small()
        p.tt("dve", lre[:], mag[:], cs[:], ALU.mult, [mag, cs], [lre])
        p.tt("dve", lim[:], mag[:], sn[:], ALU.mult, [mag, sn], [lim])
        den = small(); rden = small(); nr = small()
        p.tt("dve", den[:], a_re[:], a_re[:], ALU.mult, [a_re], [den])
        p.stt(den[:], a_im[:], a_im[:, 0:1], den[:], ALU.mult, ALU.add, [a_im, den], [den])
        p.recip(rden[:], den[:], [den], [rden])
        p.ts("dve", nr[:], lre[:], -1.0, None, ALU.add, None, [lre], [nr])
        cr = small(); ci = small(); nci = small()
        p.tt("dve", t0_[:], lim[:], a_im[:], ALU.mult, [lim, a_im], [t0_])
        p.stt(cr[:], nr[:], a_re[:, 0:1], t0_[:], ALU.mult, ALU.add, [nr, a_re, t0_], [cr])
        p.tt("dve", cr[:], cr[:], rden[:], ALU.mult, [cr, rden], [cr])
        p.tt("dve", t1_[:], nr[:], a_im[:], ALU.mult, [nr, a_im], [t1_])
        p.stt(ci[:], lim[:], a_re[:, 0:1], t1_[:], ALU.mult, ALU.subtract, [lim, a_re, t1_], [ci])
        p.tt("dve", ci[:], ci[:], rden[:], ALU.mult, [ci, rden], [ci])
        p.ts("dve", nci[:], ci[:], -1.0, None, ALU.mult, None, [ci], [nci])
        bbr = small(16); bbi = small(16)
        p.ts("dve", bbr[:], b_re[:], cr[:, 0:1], None, ALU.mult, None, [b_re, cr], [bbr])
        p.stt(bbr[:], b_im[:], nci[:, 0:1], bbr[:], ALU.mult, ALU.add, [b_im, nci, bbr], [bbr])
        p.ts("dve", bbi[:], b_im[:], cr[:, 0:1], None, ALU.mult, None, [b_im, cr], [bbi])
        p.stt(bbi[:], b_re[:], ci[:, 0:1], bbi[:], ALU.mult, ALU.add, [b_re, ci, bbi], [bbi])
        LB = []
        for bb in (bbr, bbi):
            bx = small(32)
            p.memset("dve", bx[:], 0.0, [bx])
            p.cp("dve", bx[0:64, 0:16], bb[0:64, :], [bb], [bx])
            p.cp("dve", bx[64:128, 16:32], bb[64:128, :], [bb], [bx])
            p.tr(pst[0:32, 0:128], bx[:], idf[:], [bx, idf], [pst])
            lb = p.bsb([32, 128], BF16)
            p.cp("dve", lb[:], pst[0:32, 0:128], [pst], [lb])
            LB.append(lb)
        p.ts("dve", xt[:], jts[:], ang[:, 0:1], None, ALU.mult, None, [jts, ang], [xt])
        sin_rr(p, tmpT[:], xt[:], 0.0, tmpT[:], [xt], [tmpT])
        p.act(sinT[:], tmpT[:], AF.Sin, [tmpT], [sinT])
        sin_rr(p, tmpT[:], xt[:], math.pi / 2, tmpT[:], [xt], [tmpT])
        p.act(cosT[:], tmpT[:], AF.Sin, [tmpT], [cosT])
        p.ts("dve", rB[:], jts[:, 0:T], 0.0, mag[:, 0:1], ALU.mult, ALU.add, [jts, mag], [rB])
        init = [(small(), small()) for _ in range(2)]
        p.memset("dve", init[0][0][:], 0.0, [init[0][0]]); p.memset("dve", init[0][1][:], 0.0, [init[0][1]])
        tq = small(); tq2 = small()
        for ck, (c0, Tc) in enumerate(chunks):
            b = ck % 2
            A, B_, Y = psA[b], psB[b], psY[b]
            rhs = ub[d][:, c0:c0 + Tc]
            p.mm(A[:, 0:Tc], LB[0][:], rhs, True, True, [LB[0], ub[d]], [A])
            p.mm(B_[:, 0:Tc], LB[1][:], rhs, True, True, [LB[1], ub[d]], [B_])
            C_, S_ = cosT[:, 0:Tc], sinT[:, 0:Tc]
            p.tt("dve", m1[:, 0:Tc], A[:, 0:Tc], C_, ALU.mult, [A, cosT], [m1])
            p.tt("dve", m2[:, 0:Tc], B_[:, 0:Tc], S_, ALU.mult, [B_, sinT], [m2])
            p.tt("dve", btr[:, 0:Tc], m1[:, 0:Tc], m2[:, 0:Tc], ALU.add, [m1, m2], [btr])
            p.tt("dve", m3[:, 0:Tc], B_[:, 0:Tc], C_, ALU.mult, [B_, cosT], [m3])
            p.tt("dve", m4[:, 0:Tc], A[:, 0:Tc], S_, ALU.mult, [A, sinT], [m4])
            p.tt("dve", bti[:, 0:Tc], m3[:, 0:Tc], m4[:, 0:Tc], ALU.subtract, [m3, m4], [bti])
            ir, ii = init[b]
            p.op("dve", [rB, btr, ir], [gr[b]], lambda e, b=b, Tc=Tc, ir=ir: e.tensor_tensor_scan(
                gr[b][:, 0:Tc], rB[:, 0:Tc], btr[:, 0:Tc], ir[:, 0:1], ALU.mult, ALU.add))
            p.op("dve", [rB, bti, ii], [gi[b]], lambda e, b=b, Tc=Tc, ii=ii: e.tensor_tensor_scan(
                gi[b][:, 0:Tc], rB[:, 0:Tc], bti[:, 0:Tc], ii[:, 0:1], ALU.mult, ALU.add))
            nir, nii = init[1 - b]
            cT, sT = cosT[:, Tc:Tc + 1], sinT[:, Tc:Tc + 1]
            ge_r, ge_i = gr[b][:, Tc - 1:Tc], gi[b][:, Tc - 1:Tc]
            p.ts("dve", tq[:], ge_i, sT, None, ALU.mult, None, [gi[b], sinT], [tq])
            p.stt(nir[:], ge_r, cT, tq[:], ALU.mult, ALU.subtract, [gr[b], cosT, tq], [nir])
            p.ts("dve", tq2[:], ge_i, cT, None, ALU.mult, None, [gi[b], cosT], [tq2])
            p.stt(nii[:], ge_r, sT, tq2[:], ALU.mult, ALU.add, [gr[b], sinT, tq2], [nii])
            p.tt("pool", n1[:, 0:Tc], gr[b][:, 0:Tc], C_, ALU.mult, [gr[b], cosT], [n1])
            p.tt("pool", n2[:, 0:Tc], gi[b][:, 0:Tc], S_, ALU.mult, [gi[b], sinT], [n2])
            p.tt("pool", hr[b][:, 0:Tc], n1[:, 0:Tc], n2[:, 0:Tc], ALU.subtract, [n1, n2], [hr[b]])
            p.tt("pool", n3[:, 0:Tc], gr[b][:, 0:Tc], S_, ALU.mult, [gr[b], sinT], [n3])
            p.tt("pool", n4[:, 0:Tc], gi[b][:, 0:Tc], C_, ALU.mult, [gi[b], cosT], [n4])
            p.tt("pool", hi[b][:, 0:Tc], n3[:, 0:Tc], n4[:, 0:Tc], ALU.add, [n3, n4], [hi[b]])
            p.mm(Y[0:32, 0:Tc], c_re[:], hr[b][:, 0:Tc], True, False, [c_re, hr[b]], [Y])
            p.mm(Y[0:32, 0:Tc], c_imn[:], hi[b][:, 0:Tc], False, d == 1, [c_imn, hi[b]], [Y])
            if d == 0:
                p.mm(Y[0:32, 0:Tc], ddb[:], rhs, False, True, [ddb, ub[d]], [Y])
                p.cp("act", yf[:, c0:c0 + Tc], Y[0:32, 0:Tc], [Y], [yf])
            else:
                if ck == 0:
                    lo = 0
                else:
                    lo = NSC + NSM - (ck) * T
                yv = yf[:, lo:lo + Tc]
                p.tt("dve", yv, yv, Y[0:32, 0:Tc][:, ::-1], ALU.add, [yf, Y], [yf])
    for c0 in range(0, NS, 4096):
        w = min(4096, NS - c0)
        p.dma("sp", [yf], [], yo[:, c0:c0 + w], yf[:, c0:c0 + w])
    return p.finish()


def _cbias(p, val):
    key = ("cb", val)
    if not hasattr(p, "_consts"):
        p._consts = {}
    if key not in p._consts:
        b = p.bsb([128, 1])
        p.memset("dve", b[:], val, [b])
        p._consts[key] = b
    return p._consts[key]


def run_ps(pl, pc, lw):
    if "ps" not in _cache:
        _cache["ps"] = build_ps()
    T = 512
    s_l = pl[:, 768:1024]; s_c = pc[:, 768:1024]
    seq_f = np.concatenate([s_c, s_l], 0)
    seq_r = np.concatenate([s_c[::-1], s_l[::-1]], 0)
    jt = np.ascontiguousarray(np.tile(np.arange(T + 1, dtype=np.float32)[None, :], (128, 1)))
    maps = []
    for i in range(NCORES):
        g0 = 2 * i
        ch = slice(32 * i, 32 * i + 32)

        def gp(a):
            return np.ascontiguousarray(a[:, g0:g0 + 2].reshape(2, 128, *a.shape[3:]))
        cre = np.zeros((2, 128, 32), np.float32); cim = np.zeros((2, 128, 32), np.float32)
        for d in range(2):
            for gl in range(2):
                cre[d, gl * 64:(gl + 1) * 64, gl * 16:(gl + 1) * 16] = lw["s5_c_re"][d, g0 + gl].T
                cim[d, gl * 64:(gl + 1) * 64, gl * 16:(gl + 1) * 16] = lw["s5_c_im"][d, g0 + gl].T
        ldt = np.repeat(lw["s5_log_dt"][:, g0:g0 + 2, None], 64, axis=2).reshape(2, 128, 1)
        dd = np.zeros((32, 32), np.float32); dd[np.arange(32), np.arange(32)] = lw["s5_d"][ch]
        maps.append({"uf": np.ascontiguousarray(seq_f[:, ch].T), "ur": np.ascontiguousarray(seq_r[:, ch].T),
                     "are": gp(lw["s5_a_re"])[..., None], "aim": gp(lw["s5_a_im"])[..., None],
                     "ldt": np.ascontiguousarray(ldt), "bre": gp(lw["s5_b_re"]), "bim": gp(lw["s5_b_im"]),
                     "cre": cre, "cim": cim, "dd": dd, "jt": jt, "ident": np.eye(128, dtype=np.float32)})
    r = run(_cache["ps"], maps)
    y = np.concatenate([r[i]["y"] for i in range(NCORES)], 0).T
    return np.ascontiguousarray(y[LC:]), np.ascontiguousarray(y[:LC])


def build_ph(LM=L, LCX=LC):
    p = Prog()
    nc = p.nc
    NJ = LM // 128
    NJC = LCX // 128
    a3m = p.inp("a3m", [3, 128, NJ, 96]); a3c = p.inp("a3c", [3, 128, NJC, 96])
    cw = p.inp("cw", [1, 3 * 96]); cb = p.inp("cb", [1, 96])
    fw1 = p.inp("fw1", [33, 64]); fb1 = p.inp("fb1", [64, 1]); fw2 = p.inp("fw2", [64, 64]); fb2 = p.inp("fb2", [64, 1])
    fw3 = p.inp("fw3", [64, 128]); dec = p.inp("dec", [128, 1]); fbias = p.inp("fbias", [1, 64])
    ftm = p.inp("ftm", [33, LM]); ftc = p.inp("ftc", [33, LCX])
    antiid = p.inp("antiid", [128, 128])
    ym = p.outp("ym", [128, NJ, 32]); yc = p.outp("yc", [128, NJC, 32])
    kdm_t = nc.dram_tensor("kdm", [64, 2 * LM], BF16, kind="Internal")
    kdc_t = nc.dram_tensor("kdc", [64, 2 * LCX], BF16, kind="Internal")
    d_kdm = Dep(); d_kdc = Dep()

    w1s = p.bsb([33, 64]); p.dma("sp", [], [w1s], w1s[:], fw1)
    w2s = p.bsb([64, 64]); p.dma("sp", [], [w2s], w2s[:], fw2)
    w3s = p.bsb([64, 128]); p.dma("sp", [], [w3s], w3s[:], fw3)
    b1s = p.bsb([64, 1]); p.dma("sp", [], [b1s], b1s[:], fb1)
    b2s = p.bsb([64, 1]); p.dma("sp", [], [b2s], b2s[:], fb2)
    dcs = p.bsb([128, 1]); p.dma("sp", [], [dcs], dcs[:], dec)
    nd = p.bsb([128, 1])
    p.ts("dve", nd[:], dcs[:], -1.0, None, ALU.mult, None, [dcs], [nd])
    p.tt("dve", nd[:], nd[:], dcs[:], ALU.min, [nd, dcs], [nd])
    cws = bcast_load(p, "sp", cw, 128, 3 * 96)
    cbs = bcast_load(p, "sp", cb, 128, 96)
    fbs = bcast_load(p, "sp", fbias, 128, 64)
    Jb = p.bsb([128, 128], BF16)
    p.dma("pool", [], [Jb], Jb[:], antiid)

    fts = p.bsb([33, 512]); tg = p.bsb([128, 512])
    xa = p.bsb([64, 512]); xb = p.bsb([64, 512]); h1 = p.bsb([64, 512]); h2 = p.bsb([64, 512])
    E = p.bsb([128, 512]); hf = p.bsb([128, 512], BF16); hrv = p.bsb([128, 512], BF16)
    ps1 = p.bps([128, 512]); ps2 = p.bps([128, 512]); ps3 = p.bps([128, 512])

    def gen(ft, Lg, kd_t, d_kd):
        kd = kd_t.ap()
        for c0 in range(0, Lg, 512):
            w = min(512, Lg - c0)
            p.dma("sp", [], [fts], fts[:, 0:w], ft[:, c0:c0 + w])
            p.dma("sp", [], [tg], tg[:, 0:w], ft[0:1, c0:c0 + w].partition_broadcast(128))
            p.mm(ps1[0:64, 0:w], w1s[:], fts[:, 0:w], True, True, [w1s, fts], [ps1])
            p.ts("dve", xa[:, 0:w], ps1[0:64, 0:w], b1s[:, 0:1], None, ALU.add, None, [ps1, b1s], [xa])
            sin_rr(p, None, xa[:, 0:w], 0.0, xb[:, 0:w], [xa], [xb])
            p.act(h1[:, 0:w], xb[:, 0:w], AF.Sin, [xb], [h1])
            p.mm(ps2[0:64, 0:w], w2s[:], h1[:, 0:w], True, True, [w2s, h1], [ps2])
            p.ts("dve", xa[:, 0:w], ps2[0:64, 0:w], b2s[:, 0:1], None, ALU.add, None, [ps2, b2s], [xa])
            sin_rr(p, None, xa[:, 0:w], 0.0, xb[:, 0:w], [xa], [xb])
            p.act(h2[:, 0:w], xb[:, 0:w], AF.Sin, [xb], [h2])
            p.mm(ps3[:, 0:w], w3s[:], h2[:, 0:w], True, True, [w3s, h2], [ps3])
            p.act(E[:, 0:w], tg[:, 0:w], AF.Exp, [tg, nd], [E], scale=nd[:, 0:1])
            p.tt("dve", hf[:, 0:w], ps3[:, 0:w], E[:, 0:w], ALU.mult, [ps3, E], [hf])
            p.cp("dve", hrv[:, 0:w], hf[:, 0:w][:, ::-1], [hf], [hrv])
            for o in range(2):
                p.dma("sp", [hf], [d_kd], kd[o * 32:(o + 1) * 32, Lg + c0:Lg + c0 + w], hf[o * 64:o * 64 + 32, 0:w])
                p.dma("sp", [hrv], [d_kd], kd[o * 32:(o + 1) * 32, Lg - c0 - w:Lg - c0],
                      hrv[o * 64 + 32:o * 64 + 64, 0:w])

    gen(ftc, LCX, kdc_t, d_kdc)
    gen(ftm, LM, kdm_t, d_kdm)

    U = p.bsb([128, NJ, 96])
    HS = max(1, NJ // 4)
    stage = p.bsb([128, HS, 96])
    z1 = p.bsb([128, NJ, 32]); zb = p.bsb([128, NJ, 32], BF16); yout = p.bsb([128, NJ, 32])
    zbr = p.bsb([128, NJ, 32], BF16)
    strips = [p.bsb([128, 128 * 128], BF16) for _ in range(2)]
    psY = [p.bps([128, 512]) for _ in range(2)]
    tq = p.bsb([128, NJ])
    state = {"si": 0}

    def stream(a3, NJs, Lg, kd_t, d_kd, yo):
        for k in range(3):
            for j0 in range(0, NJs, HS):
                n = min(HS, NJs - j0)
                p.dma("sp", [], [stage], stage[:, 0:n, :], a3[k, :, j0:j0 + n, :])
                wk = cws[:, k * 96:(k + 1) * 96].unsqueeze(1).to_broadcast([128, n, 96])
                if k == 0:
                    p.tt("dve", U[:, j0:j0 + n, :], stage[:, 0:n, :], wk, ALU.mult, [stage, cws], [U])
                    p.tt("dve", U[:, j0:j0 + n, :], U[:, j0:j0 + n, :],
                         cbs[:, :].unsqueeze(1).to_broadcast([128, n, 96]), ALU.add, [U, cbs], [U])
                else:
                    p.tt("dve", stage[:, 0:n, :], stage[:, 0:n, :], wk, ALU.mult, [stage, cws], [stage])
                    p.tt("dve", U[:, j0:j0 + n, :], U[:, j0:j0 + n, :], stage[:, 0:n, :], ALU.add, [U, stage], [U])
        for o in range(2):
            zsrc = (lambda c: U[:, 0:NJs, c]) if o == 0 else (lambda c: z1[:, 0:NJs, c])
            zdep = U if o == 0 else z1
            gate = (lambda c: U[:, 0:NJs, 32 + c]) if o == 0 else (lambda c: U[:, 0:NJs, 64 + c])
            dst = z1 if o == 0 else yout
            if o == 0:
                p.cp("dve", zb[:, 0:NJs, :], U[:, 0:NJs, 0:32], [U], [zb])
            else:
                p.cp("dve", zb[:, 0:NJs, :], z1[:, 0:NJs, :], [z1], [zb])
            zbf = zb[:].rearrange("p j c -> p (j c)")
            zrf = zbr[:].rearrange("p j c -> p (j c)")
            for q0 in range(0, NJs * 32, 512):
                qw = min(512, NJs * 32 - q0)
                Yf = psY[(q0 // 512) % 2]
                p.mm(Yf[:, 0:qw], Jb[:], zbf[:, q0:q0 + qw], True, True, [Jb, zb], [Yf])
                p.cp("act", zrf[:, q0:q0 + qw], Yf[:, 0:qw], [Yf], [zbr])
            for c in range(32):
                row = o * 32 + c
                Y = psY[c % 2]
                halves = [list(range(0, NJs)), list(range(-(NJs - 1), 0))]
                nmm = sum(len(h_) for h_ in halves)
                cnt = 0
                for hv in halves:
                    if not hv:
                        continue
                    dmin = hv[0]
                    ndd = len(hv)
                    sb_ = strips[state["si"] % 2]; state["si"] += 1
                    src = bass.AP(tensor=kd_t, offset=row * 2 * Lg + Lg + 128 * dmin - 127, ap=[[1, 128], [1, 128 * ndd]])
                    p.dma("sp", [d_kd], [sb_], sb_[:, 0:128 * ndd], src)
                    for d in hv:
                        j0 = max(0, -d); j1 = min(NJs, NJs - d)
                        p.mm(Y[:, j0 + d:j1 + d], sb_[:, 128 * (d - dmin):128 * (d - dmin) + 128], zbr[:, j0:j1, c],
                             cnt == 0, cnt == nmm - 1, [sb_, zbr], [Y])
                        cnt += 1
                p.stt(tq[:, 0:NJs], zsrc(c), fbs[:, row:row + 1], Y[:, 0:NJs], ALU.mult, ALU.add, [zdep, fbs, Y], [tq])
                p.tt("dve", dst[:, 0:NJs, c], tq[:, 0:NJs], gate(c), ALU.mult, [tq, U], [dst])
        p.dma("sp", [yout], [], yo, yout[:, 0:NJs, :])

    stream(a3c, NJC, LCX, kdc_t, d_kdc, yc)
    stream(a3m, NJ, LM, kdm_t, d_kdm, ym)
    return p.finish()


def hyena_feats(Lg):
    t = (np.arange(Lg, dtype=np.float32) / np.float32(Lg)).astype(np.float32)
    fr = np.arange(1, 17, dtype=np.float32)
    ang = (np.float32(2.0 * math.pi) * t[:, None] * fr[None, :]).astype(np.float32)
    return np.ascontiguousarray(np.concatenate([t[:, None], np.cos(ang), np.sin(ang)], -1).T.astype(np.float32))


def run_ph(pl, pc, lw):
    if "ph" not in _cache:
        _cache["ph"] = build_ph()
        _cache["ftm"] = hyena_feats(L); _cache["ftc"] = hyena_feats(LC)
    maps = []
    for i in range(NCORES):
        cols = np.concatenate([np.arange(32) + 32 * i + 256 * part for part in range(3)])

        def a3(a, Lg):
            am = a[:, cols]
            pad = np.concatenate([np.zeros((1, 96), np.float32), am, np.zeros((1, 96), np.float32)], 0)
            return np.ascontiguousarray(np.stack([pad[k:k + Lg].reshape(Lg // 128, 128, 96).transpose(1, 0, 2) for k in range(3)]))
        w3cols = np.concatenate([np.arange(32) + 32 * i + 256 * (o * 2 + dr) for o in range(2) for dr in range(2)])
        maps.append({"a3m": a3(pl, L), "a3c": a3(pc, LC),
                     "cw": np.ascontiguousarray(lw["hy_conv_w"][:, cols].reshape(1, 288)), "cb": lw["hy_conv_b"][None, cols],
                     "fw1": lw["hy_ffn_w1"], "fb1": lw["hy_ffn_b1"][:, None], "fw2": lw["hy_ffn_w2"], "fb2": lw["hy_ffn_b2"][:, None],
                     "fw3": np.ascontiguousarray(lw["hy_ffn_w3"][:, w3cols]),
                     "dec": np.ascontiguousarray(lw["hy_decay"][:, :, 32 * i:32 * i + 32].reshape(128, 1)),
                     "fbias": np.ascontiguousarray(lw["hy_bias"][:, 32 * i:32 * i + 32].reshape(1, 64)),
                     "ftm": _cache["ftm"], "ftc": _cache["ftc"],
                     "antiid": np.ascontiguousarray(np.eye(128, dtype=np.float32)[::-1])})
    r = run(_cache["ph"], maps)
    yl = np.concatenate([r[i]["ym"].transpose(1, 0, 2).reshape(L, 32) for i in range(NCORES)], 1)
    ycx = np.concatenate([r[i]["yc"].transpose(1, 0, 2).reshape(LC, 32) for i in range(NCORES)], 1)
    return yl, ycx


PARAM_KEYS = ["norm_pre_mix", "norm_post_mix", "norm_pre_mlp", "norm_post_mlp", "w_in", "w_out", "hy_conv_w",
              "hy_conv_b", "hy_ffn_w1", "hy_ffn_b1", "hy_ffn_w2", "hy_ffn_b2", "hy_ffn_w3", "hy_decay", "hy_bias",
              "s5_a_re", "s5_a_im", "s5_log_dt", "s5_b_re", "s5_b_im", "s5_c_re", "s5_c_im", "s5_d", "s5_glu_w",
              "s5_glu_b", "pool_w", "pool_scale", "att_q_norm", "att_k_norm", "mlp_w1", "mlp_w2"]


def kernel(**inputs):
    inputs = {k: np.asarray(v, dtype=np.float32) for k, v in inputs.items()}
    mod = run_pm(inputs)
    xl = np.ascontiguousarray(inputs["x"][0])
    xc = np.ascontiguousarray(inputs["ctx"][0])
    for l in range(DEPTH):
        lw = {k: np.ascontiguousarray(inputs[k][l]) for k in PARAM_KEYS}
        pl, pc = run_pa(xl, xc, mod[l], lw)
        yhy_l, yhy_c = run_ph(pl, pc, lw)
        ys5_l, ys5_c = run_ps(pl, pc, lw)
        att_l, att_c = run_pt(pl, pc)
        xl, xc = run_pc1(xl, xc, yhy_l, yhy_c, ys5_l, ys5_c, pl[:, 1024:1280], pc[:, 1024:1280],
                         att_l, att_c, mod[l], lw)
        xl, xc = run_pc2(xl, xc, mod[l], lw)
    return np.ascontiguousarray(xl[None].astype(np.float32))
```
